# Optimizing a Trainium2 kernel written in Bass

```python
import math
import jax, jax.numpy as jnp
from jax import lax
import numpy as np

D_MODEL = 2048
BATCH = 4
SEQ = 2048
DEPTH = 4
DEC_BATCH = 8
DEC_SEQ = 8
PAST_LEN = 16384
PAGE_SIZE = 128

N_MIXERS = 3
KIND_POOL = 0
KIND_SSM = 1
KIND_SB = 2
LAYER_KINDS = tuple(i % N_MIXERS for i in range(DEPTH))
N_POOL_LAYERS = LAYER_KINDS.count(KIND_POOL)
N_SSM_LAYERS = LAYER_KINDS.count(KIND_SSM)
N_SB_LAYERS = LAYER_KINDS.count(KIND_SB)

POOL_WINDOWS = (2, 4, 8, 16)
POOL_GROUPS = len(POOL_WINDOWS)
POOL_GROUP_DIM = D_MODEL // POOL_GROUPS
POOL_BUF = max(POOL_WINDOWS) - 1

S5_GROUP_SIZE = 16
S5_GROUPS = D_MODEL // S5_GROUP_SIZE
S5_STATE = 64

SB_HEAD_DIM = 128
SB_HEADS = D_MODEL // SB_HEAD_DIM
BLOCK_Q = 128
SB_BIAS_LO = -8.0
SB_BIAS_HI = -4.0

D_FF = -(-8 * D_MODEL // (3 * 256)) * 256

RMS_EPS = 1e-6

kernel_name = 'hybrid_pool_s5_stickbreaking_decoder_step'


def rmsnorm(x, g):
    x32 = x.astype(jnp.float32)
    y = x32 * lax.rsqrt(jnp.mean(x32 * x32, axis=-1, keepdims=True) + RMS_EPS)
    return (y * g.astype(jnp.float32)).astype(x.dtype)


def swiglu_ffn(h, w_gate, w_up, w_down):
    return (jax.nn.silu(h @ w_gate) * (h @ w_up)) @ w_down


def pool_mixer(h, prev, start_pos, w_pool, scale):
    B, T, _ = h.shape
    ext = jnp.concatenate([prev.astype(h.dtype), h], axis=1)
    cs = jnp.concatenate([jnp.zeros((B, 1, D_MODEL), jnp.float32),
                          jnp.cumsum(ext.astype(jnp.float32), axis=1)], axis=1)
    h32 = h.astype(jnp.float32)
    pos = start_pos + jnp.arange(T)
    P = POOL_BUF
    outs = []
    for g, w in enumerate(POOL_WINDOWS):
        sl = slice(g * POOL_GROUP_DIM, (g + 1) * POOL_GROUP_DIM)
        win = cs[:, P + 1:P + 1 + T, sl] - cs[:, P + 1 - w:P + 1 - w + T, sl]
        cnt = jnp.minimum(pos + 1, w).astype(jnp.float32)
        outs.append(win / cnt[None, :, None] - h32[:, :, sl])
    p = jnp.stack(outs, axis=2)
    y = jnp.einsum('btgc,gcd->btgd', p, w_pool.astype(jnp.float32)).reshape(B, T, D_MODEL)
    y = y * scale.astype(jnp.float32)
    return y, ext[:, -P:]


def s5_mixer(h, h0_re, h0_im, a_re, a_im, b_re, b_im, c_re, c_im, d, log_dt, w_glu_a, w_glu_b):
    f32 = jnp.float32
    B, T, _ = h.shape
    u = h.astype(f32)
    ug = u.reshape(B, T, S5_GROUPS, S5_GROUP_SIZE)
    a_re = a_re.astype(f32); a_im = a_im.astype(f32)
    dt = jnp.exp(log_dt.astype(f32))[:, None]
    mag = jnp.exp(dt * a_re)
    abar_re = mag * jnp.cos(dt * a_im)
    abar_im = mag * jnp.sin(dt * a_im)
    den = a_re * a_re + a_im * a_im
    nr = abar_re - 1.0
    f_re = (nr * a_re + abar_im * a_im) / den
    f_im = (abar_im * a_re - nr * a_im) / den
    b_re = b_re.astype(f32); b_im = b_im.astype(f32)
    bbar_re = f_re[..., None] * b_re - f_im[..., None] * b_im
    bbar_im = f_re[..., None] * b_im + f_im[..., None] * b_re
    bu_re = jnp.einsum('btgc,gnc->btgn', ug, bbar_re)
    bu_im = jnp.einsum('btgc,gnc->btgn', ug, bbar_im)
    a_r = jnp.broadcast_to(abar_re, bu_re.shape)
    a_i = jnp.broadcast_to(abar_im, bu_im.shape)

    def combine(e1, e2):
        ar1, ai1, br1, bi1 = e1
        ar2, ai2, br2, bi2 = e2
        return (ar2 * ar1 - ai2 * ai1,
                ar2 * ai1 + ai2 * ar1,
                ar2 * br1 - ai2 * bi1 + br2,
                ar2 * bi1 + ai2 * br1 + bi2)

    pr, pi, sr, si = lax.associative_scan(combine, (a_r, a_i, bu_re, bu_im), axis=1)
    h0r = h0_re.astype(f32)[:, None]
    h0i = h0_im.astype(f32)[:, None]
    sr = sr + pr * h0r - pi * h0i
    si = si + pr * h0i + pi * h0r
    y = (jnp.einsum('btgn,gcn->btgc', sr, c_re.astype(f32))
         - jnp.einsum('btgn,gcn->btgc', si, c_im.astype(f32))).reshape(B, T, D_MODEL)
    y = y + d.astype(f32) * u
    g = jax.nn.gelu(y, approximate=False)
    out = (g @ w_glu_a.astype(f32)) * jax.nn.sigmoid(g @ w_glu_b.astype(f32))
    return out, sr[:, -1], si[:, -1]


def sb_project(h, w_qkv, g_q, g_k):
    B, T, _ = h.shape
    qkv = (h @ w_qkv).reshape(B, T, 3, SB_HEADS, SB_HEAD_DIM)
    q = rmsnorm(qkv[:, :, 0], g_q)
    k = rmsnorm(qkv[:, :, 1], g_k)
    v = qkv[:, :, 2]
    return q, k, v


def stick_breaking(q, k, v, bias, q_pos, k_pos):
    f32 = jnp.float32
    z = jnp.einsum('bqhd,bkhd->bhqk', q.astype(f32), k.astype(f32)) * (1.0 / math.sqrt(SB_HEAD_DIM))
    z = z + bias.astype(f32)[None, :, None, None]
    mask = (k_pos[None, :] < q_pos[:, None])[None, None]
    log_keep = jnp.where(mask, jax.nn.log_sigmoid(-z), 0.0)
    later = lax.cumsum(log_keep, axis=3, reverse=True) - log_keep
    w = jnp.where(mask, jnp.exp(jax.nn.log_sigmoid(z) + later), 0.0)
    return jnp.einsum('bhqk,bkhd->bqhd', w, v.astype(f32))


def sb_prompt(q, k, v, bias):
    T = q.shape[1]
    outs = []
    for i in range(T // BLOCK_Q):
        lo, hi = i * BLOCK_Q, (i + 1) * BLOCK_Q
        outs.append(stick_breaking(q[:, lo:hi], k[:, :hi], v[:, :hi], bias,
                                   jnp.arange(lo, hi), jnp.arange(hi)))
    return jnp.concatenate(outs, axis=1)


def setup_inputs(seed: int = 0) -> dict:
    key = jax.random.key(seed)
    ks = jax.random.split(key, 40)
    f32 = jnp.float32
    n_pages = PAST_LEN // PAGE_SIZE
    n_used = DEC_BATCH * n_pages
    n_phys = n_used + -(-n_used // 4)

    def normal(k, shape, scale):
        return scale * jax.random.normal(k, shape, f32)

    def gain(k, shape):
        return 1.0 + 0.02 * jax.random.normal(k, shape, f32)

    n_idx = jnp.arange(S5_STATE, dtype=f32)
    inputs = {
        'x_prompt': normal(ks[0], (BATCH, SEQ, D_MODEL), 1.0),
        'x_sample': normal(ks[1], (DEC_BATCH, DEC_SEQ, D_MODEL), 1.0),
        'cache_pool': normal(ks[2], (N_POOL_LAYERS, DEC_BATCH, POOL_BUF, D_MODEL), 1.0),
        'state_ssm_re': normal(ks[3], (N_SSM_LAYERS, DEC_BATCH, S5_GROUPS, S5_STATE), 0.1),
        'state_ssm_im': normal(ks[4], (N_SSM_LAYERS, DEC_BATCH, S5_GROUPS, S5_STATE), 0.1),
        'cache_k': normal(ks[5], (N_SB_LAYERS, n_phys, PAGE_SIZE, SB_HEADS, SB_HEAD_DIM), 1.0),
        'cache_v': normal(ks[6], (N_SB_LAYERS, n_phys, PAGE_SIZE, SB_HEADS, SB_HEAD_DIM), 1.0),
        'page_table': jax.random.permutation(ks[7], n_phys)[:n_used].reshape(DEC_BATCH, n_pages).astype(jnp.int32),
        'norm_mix': gain(ks[8], (DEPTH, D_MODEL)),
        'norm_ffn': gain(ks[9], (DEPTH, D_MODEL)),
        'w_ffn_gate': normal(ks[10], (DEPTH, D_MODEL, D_FF), D_MODEL ** -0.5),
        'w_ffn_up': normal(ks[11], (DEPTH, D_MODEL, D_FF), D_MODEL ** -0.5),
        'w_ffn_down': normal(ks[12], (DEPTH, D_FF, D_MODEL), D_FF ** -0.5),
        'w_pool': normal(ks[13], (N_POOL_LAYERS, POOL_GROUPS, POOL_GROUP_DIM, POOL_GROUP_DIM), POOL_GROUP_DIM ** -0.5),
        'pool_scale': 0.5 + 0.05 * jax.random.normal(ks[14], (N_POOL_LAYERS, D_MODEL), f32),
        'ssm_a_re': -0.5 + 0.01 * jax.random.normal(ks[15], (N_SSM_LAYERS, S5_GROUPS, S5_STATE), f32),
        'ssm_a_im': math.pi * n_idx + 0.01 * jax.random.normal(ks[16], (N_SSM_LAYERS, S5_GROUPS, S5_STATE), f32),
        'ssm_b_re': normal(ks[17], (N_SSM_LAYERS, S5_GROUPS, S5_STATE, S5_GROUP_SIZE), (2 * S5_GROUP_SIZE) ** -0.5),
        'ssm_b_im': normal(ks[18], (N_SSM_LAYERS, S5_GROUPS, S5_STATE, S5_GROUP_SIZE), (2 * S5_GROUP_SIZE) ** -0.5),
        'ssm_c_re': normal(ks[19], (N_SSM_LAYERS, S5_GROUPS, S5_GROUP_SIZE, S5_STATE), S5_STATE ** -0.5),
        'ssm_c_im': normal(ks[20], (N_SSM_LAYERS, S5_GROUPS, S5_GROUP_SIZE, S5_STATE), S5_STATE ** -0.5),
        'ssm_d': normal(ks[21], (N_SSM_LAYERS, D_MODEL), 0.5),
        'ssm_log_dt': jax.random.uniform(ks[22], (N_SSM_LAYERS, S5_GROUPS), f32, math.log(1e-3), math.log(1e-1)),
        'w_glu_a': normal(ks[23], (N_SSM_LAYERS, D_MODEL, D_MODEL), D_MODEL ** -0.5),
        'w_glu_b': normal(ks[24], (N_SSM_LAYERS, D_MODEL, D_MODEL), D_MODEL ** -0.5),
        'w_qkv': normal(ks[25], (N_SB_LAYERS, D_MODEL, 3 * D_MODEL), D_MODEL ** -0.5),
        'w_o': normal(ks[26], (N_SB_LAYERS, D_MODEL, D_MODEL), D_MODEL ** -0.5),
        'sb_q_norm': gain(ks[27], (N_SB_LAYERS, SB_HEAD_DIM)),
        'sb_k_norm': gain(ks[28], (N_SB_LAYERS, SB_HEAD_DIM)),
        'sb_bias': jax.random.uniform(ks[29], (N_SB_LAYERS, SB_HEADS), f32, SB_BIAS_LO, SB_BIAS_HI),
    }
    return inputs


def reference(x_prompt, x_sample, cache_pool, state_ssm_re, state_ssm_im, cache_k, cache_v, page_table,
              norm_mix, norm_ffn, w_ffn_gate, w_ffn_up, w_ffn_down, w_pool, pool_scale,
              ssm_a_re, ssm_a_im, ssm_b_re, ssm_b_im, ssm_c_re, ssm_c_im, ssm_d, ssm_log_dt,
              w_glu_a, w_glu_b, w_qkv, w_o, sb_q_norm, sb_k_norm, sb_bias):
    xp, xs = x_prompt, x_sample
    Bp, Tp, _ = xp.shape
    Bs, Ts, _ = xs.shape
    pool_p, pool_s = [], []
    ssm_re_p, ssm_im_p, ssm_re_s, ssm_im_s = [], [], [], []
    k_p, v_p, k_s, v_s = [], [], [], []
    for i in range(DEPTH):
        kind = LAYER_KINDS[i]
        j = i // N_MIXERS
        hp = rmsnorm(xp, norm_mix[i])
        hs = rmsnorm(xs, norm_mix[i])
        if kind == KIND_POOL:
            zero_buf = jnp.zeros((Bp, POOL_BUF, D_MODEL), hp.dtype)
            mp, bp = pool_mixer(hp, zero_buf, 0, w_pool[j], pool_scale[j])
            ms, bs = pool_mixer(hs, cache_pool[j], PAST_LEN, w_pool[j], pool_scale[j])
            pool_p.append(bp)
            pool_s.append(bs)
        elif kind == KIND_SSM:
            params = (ssm_a_re[j], ssm_a_im[j], ssm_b_re[j], ssm_b_im[j], ssm_c_re[j], ssm_c_im[j],
                      ssm_d[j], ssm_log_dt[j], w_glu_a[j], w_glu_b[j])
            zero_state = jnp.zeros((Bp, S5_GROUPS, S5_STATE), jnp.float32)
            mp, sr_p, si_p = s5_mixer(hp, zero_state, zero_state, *params)
            ms, sr_s, si_s = s5_mixer(hs, state_ssm_re[j], state_ssm_im[j], *params)
            ssm_re_p.append(sr_p); ssm_im_p.append(si_p)
            ssm_re_s.append(sr_s); ssm_im_s.append(si_s)
        else:
            q_pr, k_pr, v_pr = sb_project(hp, w_qkv[j], sb_q_norm[j], sb_k_norm[j])
            o_p = sb_prompt(q_pr, k_pr, v_pr, sb_bias[j])
            q_sm, k_sm, v_sm = sb_project(hs, w_qkv[j], sb_q_norm[j], sb_k_norm[j])
            past_k = cache_k[j][page_table].reshape(Bs, -1, SB_HEADS, SB_HEAD_DIM)
            past_v = cache_v[j][page_table].reshape(Bs, -1, SB_HEADS, SB_HEAD_DIM)
            past = past_k.shape[1]
            k_all = jnp.concatenate([past_k.astype(k_sm.dtype), k_sm], axis=1)
            v_all = jnp.concatenate([past_v.astype(v_sm.dtype), v_sm], axis=1)
            o_s = stick_breaking(q_sm, k_all, v_all, sb_bias[j], past + jnp.arange(Ts), jnp.arange(past + Ts))
            mp = o_p.reshape(Bp, Tp, D_MODEL) @ w_o[j].astype(jnp.float32)
            ms = o_s.reshape(Bs, Ts, D_MODEL) @ w_o[j].astype(jnp.float32)
            k_p.append(k_pr); v_p.append(v_pr)
            k_s.append(k_sm); v_s.append(v_sm)
        xp = xp + mp.astype(xp.dtype)
        xs = xs + ms.astype(xs.dtype)
        xp = xp + swiglu_ffn(rmsnorm(xp, norm_ffn[i]), w_ffn_gate[i], w_ffn_up[i], w_ffn_down[i]).astype(xp.dtype)
        xs = xs + swiglu_ffn(rmsnorm(xs, norm_ffn[i]), w_ffn_gate[i], w_ffn_up[i], w_ffn_down[i]).astype(xs.dtype)
    y_prompt = xp
    y_sample = xs
    new_pool_prompt = jnp.stack(pool_p, axis=0)
    new_pool_sample = jnp.stack(pool_s, axis=0)
    new_ssm_re_prompt = jnp.stack(ssm_re_p, axis=0)
    new_ssm_im_prompt = jnp.stack(ssm_im_p, axis=0)
    new_ssm_re_sample = jnp.stack(ssm_re_s, axis=0)
    new_ssm_im_sample = jnp.stack(ssm_im_s, axis=0)
    new_k_prompt = jnp.stack(k_p, axis=0)
    new_v_prompt = jnp.stack(v_p, axis=0)
    new_k_sample = jnp.stack(k_s, axis=0)
    new_v_sample = jnp.stack(v_s, axis=0)
    return (y_prompt, y_sample, new_pool_prompt, new_pool_sample,
            new_ssm_re_prompt, new_ssm_im_prompt, new_ssm_re_sample, new_ssm_im_sample,
            new_k_prompt, new_v_prompt, new_k_sample, new_v_sample)
```

```python
from contextlib import ExitStack
import math
import numpy as np
import concourse.bass as bass
import concourse.mybir as mybir
from concourse.bass_utils import run_bass_kernel_spmd

F32 = mybir.dt.float32
BF16 = mybir.dt.bfloat16
I32 = mybir.dt.int32
AF = mybir.ActivationFunctionType
ALU = mybir.AluOpType

D = 2048
NCH = 16
DFF = 5632
NFC = 44
NS = 8
PBUF = 15
EPS = 1e-6
SEM_EPOCH = 30000


class Sched:
    ENGS = ("pe", "act", "dve", "pool", "sp")

    def __init__(self, nc, es, n_dma=24, n_spare=16):
        self.nc = nc
        self.eobj = {"pe": nc.tensor, "act": nc.scalar, "dve": nc.vector, "pool": nc.gpsimd, "sp": nc.sync}
        self.q = {e: [] for e in self.ENGS}
        self.spare = [es.enter_context(nc.semaphore(f"esem{i}")) for i in range(n_spare)]
        self.esem = {}
        self.epoch = {e: 0 for e in self.ENGS}
        self.cnt = {e: 0 for e in self.ENGS}
        for e in ("pe", "act", "dve", "pool"):
            self.esem[(e, 0)] = self.spare.pop()
        self.dsem = [es.enter_context(nc.semaphore(f"dsem{i}")) for i in range(2 * n_dma)]
        self.dcnt = [0] * (2 * n_dma)
        self.dpool = {"sp": list(range(0, n_dma)), "pool": list(range(n_dma, 2 * n_dma))}
        self.dnext = {"sp": 0, "pool": 0}
        self.waited = {e: {} for e in self.ENGS}
        self.lastw = {}
        self.readers = {}
        self.ninst = {e: 0 for e in self.ENGS}

    def _sem(self, key):
        if key[0] == "d":
            return self.dsem[key[1]]
        return self.esem[(key[1], key[2])]

    def _deps(self, reads, writes):
        deps = {}
        def add(d):
            if d is None:
                return
            k, v = d
            if deps.get(k, 0) < v:
                deps[k] = v
        for r in reads:
            add(self.lastw.get(r))
        for w in writes:
            add(self.lastw.get(w))
            for k, v in self.readers.get(w, {}).items():
                add((k, v))
        return deps

    def _emit_waits(self, eng, deps):
        for k, v in deps.items():
            if eng == "pe" and k[0] == "e" and k[1] == "pe":
                continue
            if self.waited[eng].get(k, 0) >= v:
                continue
            self.waited[eng][k] = v
            sem = self._sem(k)
            self.q[eng].append(lambda e, s=sem, vv=v: e.wait_ge(s, vv))
            self.ninst[eng] += 1

    def _record(self, me, reads, writes):
        k, v = me
        for r in reads:
            self.readers.setdefault(r, {})[k] = v
        for w in writes:
            self.lastw[w] = me
            self.readers[w] = {}

    def op(self, eng, fn, reads=(), writes=()):
        self._emit_waits(eng, self._deps(reads, writes))
        if self.cnt[eng] >= SEM_EPOCH:
            self.epoch[eng] += 1
            self.cnt[eng] = 0
            self.esem[(eng, self.epoch[eng])] = self.spare.pop()
        self.cnt[eng] += 1
        key = ("e", eng, self.epoch[eng])
        sem = self.esem[(eng, self.epoch[eng])]
        self.q[eng].append(lambda e, f=fn, s=sem: f(e).then_inc(s, 1))
        self.ninst[eng] += 1
        self._record((key, self.cnt[eng]), reads, writes)

    def dma(self, eng, fn, reads=(), writes=()):
        pool_ = self.dpool[eng]
        i = pool_[self.dnext[eng]]
        self.dnext[eng] = (self.dnext[eng] + 1) % len(pool_)
        deps = self._deps(reads, writes)
        if self.dcnt[i] > 0:
            k = ("d", i)
            if deps.get(k, 0) < self.dcnt[i]:
                deps[k] = self.dcnt[i]
        self._emit_waits(eng, deps)
        self.dcnt[i] += 16
        sem = self.dsem[i]
        self.q[eng].append(lambda e, f=fn, s=sem: f(e).then_inc(s, 16))
        self.ninst[eng] += 1
        self._record((("d", i), self.dcnt[i]), reads, writes)

    def barrier(self):
        for eng in self.ENGS:
            deps = {}
            for e2 in ("pe", "act", "dve", "pool"):
                if self.cnt[e2] > 0:
                    deps[("e", e2, self.epoch[e2])] = self.cnt[e2]
            for i, c in enumerate(self.dcnt):
                if c > 0:
                    deps[("d", i)] = c
            if eng == "pe":
                pass
            self._emit_waits(eng, deps)

    def flush(self, block):
        reg = {"pe": block.tensor, "act": block.scalar, "dve": block.vector, "pool": block.gpsimd, "sp": block.sync}
        for eng in self.ENGS:
            items = self.q[eng]
            self.q[eng] = []
            if not items:
                continue
            def body(e, items=items):
                for f in items:
                    f(e)
            reg[eng](body)


class Cfg:
    def __init__(self, T=2048, npages=128, nphys=1280, kinds=(0, 1, 2, 0)):
        self.T = T
        self.npages = npages
        self.nphys = nphys
        self.kinds = tuple(kinds)
        self.TT = T + NS
        self.ntile = T // 512
        self.n_pool = self.kinds.count(0)
        self.n_ssm = self.kinds.count(1)
        self.n_sb = self.kinds.count(2)
        self.skip_ffn = False


class K:
    pass


def tile_cols(cfg, i):
    n = 512 + (NS if i == cfg.ntile - 1 else 0)
    return 512 * i, n


def mm_cols(n):
    segs = [(0, 512, False)]
    if n > 512:
        segs.append((512, n, True))
    return segs


def phase_transpose_in(k):
    nc, S, cfg = k.nc, k.S, k.cfg
    with ExitStack() as es:
        xin = [es.enter_context(nc.sbuf_tensor(f"ti_in{i}", [128, D], F32)) for i in range(2)]
        xo = [es.enter_context(nc.sbuf_tensor(f"ti_out{i}", [128, NCH, 520], F32)) for i in range(2)]
        nblk = cfg.T // 128
        for ti in range(cfg.ntile):
            c0, ncols = tile_cols(cfg, ti)
            o = xo[ti % 2]
            otok = ("ti_out", ti % 2)
            blocks = [(c0 + 128 * j, 128, 128 * j, False) for j in range(4)]
            if ncols > 512:
                blocks.append((0, NS, 512, True))
            for bi, (r0, nr, oc0, smp) in enumerate(blocks):
                slot = (ti * 5 + bi) % 2
                src = k.x_sample[0:NS, :] if smp else k.x_prompt[r0:r0 + nr, :]
                S.dma("sp", lambda e, s=src, d=xin[slot], nr=nr: e.dma_start(out=d[0:nr, :], in_=s),
                      reads=(), writes=(("ti_in", slot),))
                for g in range(4):
                    bank = g
                    for cc in range(4):
                        c = g * 4 + cc
                        S.op("pe", lambda e, b=bank, cc=cc, c=c, slot=slot, nr=nr:
                             e.transpose(out=k.ps[b][:, cc * 128:cc * 128 + nr],
                                         in_=xin[slot][0:nr, c * 128:(c + 1) * 128],
                                         identity=k.ident[0:nr, 0:nr]),
                             reads=(("ti_in", slot),), writes=(("ps", bank),))
                    eng = "act" if g % 2 == 0 else "dve"
                    src_ap = k.ps[bank][:, :].rearrange("p (c t) -> p c t", c=4)[:, :, 0:nr]
                    dst_ap = o[:, g * 4:(g + 1) * 4, oc0:oc0 + nr]
                    if eng == "act":
                        S.op("act", lambda e, s=src_ap, d=dst_ap: e.copy(out=d, in_=s),
                             reads=(("ps", bank),), writes=(otok,))
                    else:
                        S.op("dve", lambda e, s=src_ap, d=dst_ap: e.tensor_copy(out=d, in_=s),
                             reads=(("ps", bank),), writes=(otok,))
            S.dma("sp", lambda e, o=o, c0=c0, n=ncols: e.dma_start(
                out=k.XT[:, :, c0:c0 + n], in_=o[:, :, 0:n]),
                reads=(otok,), writes=(("XT", ti),))
        S.flush(k.block)
    S.barrier()


def rmsnorm_tile(k, x, xtok, ncols, gcol, out_h, htok, sq, bank_ss, smp_cols, rstd, out_dtype_note=None):
    S = k.S
    segs = mm_cols(ncols)
    for c in range(NCH):
        s = sq[c % 2]
        stok = ("sq", id(sq), c % 2)
        S.op("act", lambda e, s=s, c=c: e.activation(out=s[:, 0:ncols], in_=x[:, c, 0:ncols], func=AF.Square),
             reads=(xtok,), writes=(stok,))
        for (a, b, smp) in segs:
            if smp:
                bk, c0 = smp_cols
                out_ap = k.ps[bk][:, c0:c0 + (b - a)]
                wt = ("ps", bk)
            else:
                out_ap = k.ps[bank_ss][:, 0:512]
                wt = ("ps", bank_ss)
            S.op("pe", lambda e, o=out_ap, s=s, a=a, b=b, c=c: e.matmul(
                out=o, lhsT=k.ones_f[:, :], rhs=s[:, a:b], start=(c == 0), stop=(c == NCH - 1)),
                reads=(stok,), writes=(wt,))
    rtok = ("rstd", id(rstd))
    for (a, b, smp) in segs:
        if smp:
            bk, c0 = smp_cols
            in_ap = k.ps[bk][:, c0:c0 + (b - a)]
            rt = ("ps", bk)
        else:
            in_ap = k.ps[bank_ss][:, 0:512]
            rt = ("ps", bank_ss)
        S.op("act", lambda e, i=in_ap, a=a, b=b: e.activation(
            out=rstd[:, a:b], in_=i, func=AF.Sqrt, scale=1.0 / D, bias=k.eps_col[:, 0:1]),
            reads=(rt,), writes=(rtok,))
    S.op("dve", lambda e: e.reciprocal(out=rstd[:, 0:ncols], in_=rstd[:, 0:ncols]),
         reads=(rtok,), writes=(rtok,))
    for c in range(NCH):
        S.op("dve", lambda e, c=c: e.scalar_tensor_tensor(
            out=out_h[:, c, 0:ncols], in0=x[:, c, 0:ncols], scalar=gcol[:, c:c + 1], in1=rstd[:, 0:ncols],
            op0=ALU.mult, op1=ALU.mult),
            reads=(xtok, rtok), writes=(htok,))


def phase_ffn(k, L):
    nc, S, cfg = k.nc, k.S, k.cfg
    Wg = k.w_gate[L].rearrange("(kc p) f -> p kc f", p=128)
    Wu = k.w_up[L].rearrange("(kc p) f -> p kc f", p=128)
    Wd = k.w_down[L].rearrange("(fc p) d -> p fc d", p=128)
    with ExitStack() as es:
        A = lambda name, shape, dt: es.enter_context(nc.sbuf_tensor(f"{name}_L{L}", shape, dt))
        xt = A("ffn_x", [128, NCH, 520], F32)
        ht = A("ffn_h", [128, NCH, 520], BF16)
        at = A("ffn_a", [128, NFC, 520], BF16)
        wg = [A(f"ffn_wg{i}", [128, NCH, 256], BF16) for i in range(2)]
        wu = [A(f"ffn_wu{i}", [128, NCH, 256], BF16) for i in range(2)]
        wd = [A(f"ffn_wd{i}", [128, NFC, 256], BF16) for i in range(2)]
        sq = [A(f"ffn_sq{i}", [128, 520], F32) for i in range(2)]
        rstd = A("ffn_rstd", [128, 520], F32)
        sg = [A(f"ffn_sg{i}", [128, 520], F32) for i in range(2)]
        gcol = k.g_ffn[:, L * NCH:(L + 1) * NCH]
        SBK = (4, 5)
        nw = 0
        nd = 0
        for ti in range(cfg.ntile):
            c0, ncols = tile_cols(cfg, ti)
            segs = mm_cols(ncols)
            S.dma("sp", lambda e, c0=c0, n=ncols: e.dma_start(out=xt[:, :, 0:n], in_=k.XT[:, :, c0:c0 + n]),
                  reads=(("XT", ti),), writes=("ffn_x",))
            rmsnorm_tile(k, xt, "ffn_x", ncols, gcol, ht, "ffn_h", sq, 6, (7, 0), rstd)
            for fc2 in range(NFC // 2):
                slot = nw % 2
                nw += 1
                S.dma("pool", lambda e, slot=slot, fc2=fc2: e.dma_start(
                    out=wg[slot][:, :, :], in_=Wg[:, :, fc2 * 256:(fc2 + 1) * 256]),
                    reads=(), writes=(("wg", slot),))
                S.dma("pool", lambda e, slot=slot, fc2=fc2: e.dma_start(
                    out=wu[slot][:, :, :], in_=Wu[:, :, fc2 * 256:(fc2 + 1) * 256]),
                    reads=(), writes=(("wu", slot),))
                for half in range(2):
                    fc = fc2 * 2 + half
                    par = fc % 2
                    for which, wt_, wtok in ((0, wg, "wg"), (1, wu, "wu")):
                        bank = par * 2 + which
                        for kc in range(NCH):
                            for (a, b, smp) in segs:
                                if smp:
                                    col = which * 8
                                    o = k.ps[SBK[par]][:, col:col + NS]
                                    tok = ("ps", SBK[par])
                                else:
                                    o = k.ps[bank][:, 0:512]
                                    tok = ("ps", bank)
                                S.op("pe", lambda e, o=o, w=wt_[slot], kc=kc, half=half, a=a, b=b: e.matmul(
                                    out=o, lhsT=w[:, kc, half * 128:(half + 1) * 128], rhs=ht[:, kc, a:b],
                                    start=(kc == 0), stop=(kc == NCH - 1)),
                                    reads=((wtok, slot), "ffn_h"), writes=(tok,))
                    for (a, b, smp) in segs:
                        if smp:
                            gin = k.ps[SBK[par]][:, 0:NS]
                            uin = k.ps[SBK[par]][:, 8:8 + NS]
                            gt, ut = ("ps", SBK[par]), ("ps", SBK[par])
                        else:
                            gin = k.ps[par * 2][:, 0:512]
                            uin = k.ps[par * 2 + 1][:, 0:512]
                            gt, ut = ("ps", par * 2), ("ps", par * 2 + 1)
                        sgt = ("ffn_sg", par, smp)
                        S.op("act", lambda e, gin=gin, a=a, b=b, par=par: e.activation(
                            out=sg[par][:, a:b], in_=gin, func=AF.Silu),
                            reads=(gt,), writes=(sgt,))
                        S.op("dve", lambda e, uin=uin, a=a, b=b, par=par, fc=fc: e.tensor_tensor(
                            out=at[:, fc, a:b], in0=sg[par][:, a:b], in1=uin, op=ALU.mult),
                            reads=(sgt, ut), writes=(("ffn_a", fc),))
            for dc2 in range(NCH // 2):
                slot = nd % 2
                nd += 1
                S.dma("pool", lambda e, slot=slot, dc2=dc2: e.dma_start(
                    out=wd[slot][:, :, :], in_=Wd[:, :, dc2 * 256:(dc2 + 1) * 256]),
                    reads=(), writes=(("wd", slot),))
                for half in range(2):
                    dc = dc2 * 2 + half
                    bank = dc % 2
                    for fc in range(NFC):
                        for (a, b, smp) in segs:
                            if smp:
                                o = k.ps[SBK[dc % 2]][:, 32:32 + NS]
                                tok = ("ps", SBK[dc % 2])
                            else:
                                o = k.ps[bank][:, 0:512]
                                tok = ("ps", bank)
                            S.op("pe", lambda e, o=o, slot=slot, fc=fc, half=half, a=a, b=b: e.matmul(
                                out=o, lhsT=wd[slot][:, fc, half * 128:(half + 1) * 128], rhs=at[:, fc, a:b],
                                start=(fc == 0), stop=(fc == NFC - 1)),
                                reads=(("wd", slot), ("ffn_a", fc)), writes=(tok,))
                    for (a, b, smp) in segs:
                        if smp:
                            din = k.ps[SBK[dc % 2]][:, 32:32 + NS]
                            tok = ("ps", SBK[dc % 2])
                        else:
                            din = k.ps[bank][:, 0:512]
                            tok = ("ps", bank)
                        S.op("dve", lambda e, din=din, dc=dc, a=a, b=b: e.tensor_tensor(
                            out=xt[:, dc, a:b], in0=xt[:, dc, a:b], in1=din, op=ALU.add),
                            reads=(tok, "ffn_x"), writes=("ffn_x",))
            S.dma("sp", lambda e, c0=c0, n=ncols: e.dma_start(out=k.XT[:, :, c0:c0 + n], in_=xt[:, :, 0:n]),
                  reads=("ffn_x",), writes=(("XT", ti),))
            S.flush(k.block)
    S.barrier()


def phase_pool(k, L, j):
    nc, S, cfg = k.nc, k.S, k.cfg
    HB = PBUF + 520
    with ExitStack() as es:
        A_ = lambda name, shape, dt: es.enter_context(nc.sbuf_tensor(f"{name}_L{L}", shape, dt))
        xt = A_("pl_x", [128, NCH, 520], F32)
        hb = A_("pl_h", [128, NCH, HB], F32)
        wa = A_("pl_wa", [128, NCH, HB], F32)
        wb = A_("pl_wb", [128, NCH, HB], F32)
        pb = A_("pl_p", [128, NCH, 520], BF16)
        hs = A_("pl_hs", [128, NCH, 24], F32)
        sa = A_("pl_sa", [128, NCH, 24], F32)
        sb_ = A_("pl_sb", [128, NCH, 24], F32)
        wp = A_("pl_wp", [128, NCH, 512], BF16)
        sq = [A_(f"pl_sq{i}", [128, 520], F32) for i in range(2)]
        rstd = A_("pl_rstd", [128, 520], F32)
        tmp = A_("pl_tmp", [128, NCH, PBUF], F32)
        cin = A_("pl_cin", [PBUF, D], F32)
        pout = [A_(f"pl_po{i}", [PBUF, D], F32) for i in range(2)]
        gcol = k.g_mix[:, L * NCH:(L + 1) * NCH]
        SBK = (4, 5)
        S.dma("pool", lambda e: e.dma_start(out=wp[:, :, :], in_=k.w_pool[j].rearrange("g (kc p) o -> p (g kc) o", p=128)),
              writes=("pl_wp",))
        S.op("pool", lambda e: e.memset(hb[:, :, 0:PBUF], 0.0), writes=("pl_h",))
        S.dma("sp", lambda e: e.dma_start(out=cin[:, :], in_=k.cache_pool[j]), writes=("pl_cin",))
        for g in range(4):
            for cc in range(4):
                c = g * 4 + cc
                S.op("pe", lambda e, g=g, cc=cc, c=c: e.transpose(
                    out=k.ps[g][:, cc * 128:cc * 128 + PBUF], in_=cin[0:PBUF, c * 128:(c + 1) * 128],
                    identity=k.ident[0:PBUF, 0:PBUF]), reads=("pl_cin",), writes=(("ps", g),))
            S.op("act", lambda e, g=g: e.copy(
                out=hs[:, g * 4:(g + 1) * 4, 0:PBUF],
                in_=k.ps[g][:, :].rearrange("p (c t) -> p c t", c=4)[:, :, 0:PBUF]),
                reads=(("ps", g),), writes=("pl_hs",))

        def windows(eng, h, a, b, n, toks):
            th, ta, tb = toks
            S.op(eng, lambda e: e.tensor_tensor(out=a[:, :, 1:n], in0=h[:, :, 1:n], in1=h[:, :, 0:n - 1], op=ALU.add),
                 reads=(th,), writes=(ta,))
            S.op(eng, lambda e: e.tensor_tensor(out=b[:, 4:16, 3:n], in0=a[:, 4:16, 3:n], in1=a[:, 4:16, 1:n - 2], op=ALU.add),
                 reads=(ta,), writes=(tb,))
            S.op(eng, lambda e: e.tensor_tensor(out=a[:, 8:16, 7:n], in0=b[:, 8:16, 7:n], in1=b[:, 8:16, 3:n - 4], op=ALU.add),
                 reads=(tb,), writes=(ta,))
            S.op(eng, lambda e: e.tensor_tensor(out=b[:, 12:16, 15:n], in0=a[:, 12:16, 15:n], in1=a[:, 12:16, 7:n - 8], op=ALU.add),
                 reads=(ta,), writes=(tb,))

        for ti in range(cfg.ntile):
            c0, ncols = tile_cols(cfg, ti)
            segs = mm_cols(ncols)
            last = ncols > 512
            S.dma("sp", lambda e, c0=c0, n=ncols: e.dma_start(out=xt[:, :, 0:n], in_=k.XT[:, :, c0:c0 + n]),
                  reads=(("XT", ti),), writes=("pl_x",))
            if ti > 0:
                S.op("pool", lambda e: e.tensor_copy(out=hb[:, :, 0:PBUF], in_=hb[:, :, 512:512 + PBUF]),
                     reads=("pl_h",), writes=("pl_h",))
            rmsnorm_tile(k, xt, "pl_x", ncols, gcol, hb[:, :, PBUF:HB], "pl_h", sq, 6, (7, 0), rstd)
            windows("pool", hb, wa, wb, PBUF + 512, ("pl_h", "pl_wa", "pl_wb"))
            for g in range(4):
                w = 2 << g
                src = wa if g % 2 == 0 else wb
                stok = "pl_wa" if g % 2 == 0 else "pl_wb"
                S.op("dve", lambda e, g=g, w=w, src=src: e.scalar_tensor_tensor(
                    out=pb[:, g * 4:(g + 1) * 4, 0:512], in0=src[:, g * 4:(g + 1) * 4, PBUF:PBUF + 512],
                    scalar=1.0 / w, in1=hb[:, g * 4:(g + 1) * 4, PBUF:PBUF + 512], op0=ALU.mult, op1=ALU.subtract),
                    reads=(stok, "pl_h"), writes=("pl_p",))
                if ti == 0:
                    S.op("dve", lambda e, g=g, src=src: e.tensor_tensor(
                        out=tmp[:, g * 4:(g + 1) * 4, :], in0=src[:, g * 4:(g + 1) * 4, PBUF:2 * PBUF],
                        in1=k.invc[:, g * 4:(g + 1) * 4, :], op=ALU.mult),
                        reads=(stok,), writes=("pl_tmp",))
                    S.op("dve", lambda e, g=g: e.tensor_tensor(
                        out=pb[:, g * 4:(g + 1) * 4, 0:PBUF], in0=tmp[:, g * 4:(g + 1) * 4, :],
                        in1=hb[:, g * 4:(g + 1) * 4, PBUF:2 * PBUF], op=ALU.subtract),
                        reads=("pl_tmp", "pl_h"), writes=("pl_p",))
            if last:
                S.op("dve", lambda e: e.tensor_copy(out=hs[:, :, PBUF:PBUF + NS], in_=hb[:, :, PBUF + 512:PBUF + 520]),
                     reads=("pl_h",), writes=("pl_hs",))
                windows("dve", hs, sa, sb_, PBUF + NS, ("pl_hs", "pl_sa", "pl_sb"))
                for g in range(4):
                    w = 2 << g
                    src = sa if g % 2 == 0 else sb_
                    stok = "pl_sa" if g % 2 == 0 else "pl_sb"
                    S.op("dve", lambda e, g=g, w=w, src=src: e.scalar_tensor_tensor(
                        out=pb[:, g * 4:(g + 1) * 4, 512:520], in0=src[:, g * 4:(g + 1) * 4, PBUF:PBUF + NS],
                        scalar=1.0 / w, in1=hs[:, g * 4:(g + 1) * 4, PBUF:PBUF + NS], op0=ALU.mult, op1=ALU.subtract),
                        reads=(stok, "pl_hs"), writes=("pl_p",))
            n_out = 0
            for g in range(4):
                for oc in range(4):
                    c = g * 4 + oc
                    bank = n_out % 4
                    sbank = SBK[n_out % 2]
                    n_out += 1
                    for kc in range(4):
                        for (a, b, smp) in segs:
                            o = k.ps[sbank][:, 0:NS] if smp else k.ps[bank][:, 0:512]
                            tok = ("ps", sbank) if smp else ("ps", bank)
                            S.op("pe", lambda e, o=o, g=g, kc=kc, oc=oc, a=a, b=b: e.matmul(
                                out=o, lhsT=wp[:, g * 4 + kc, oc * 128:(oc + 1) * 128], rhs=pb[:, g * 4 + kc, a:b],
                                start=(kc == 0), stop=(kc == 3)),
                                reads=("pl_wp", "pl_p"), writes=(tok,))
                    for (a, b, smp) in segs:
                        o = k.ps[sbank][:, 0:NS] if smp else k.ps[bank][:, 0:512]
                        tok = ("ps", sbank) if smp else ("ps", bank)
                        S.op("dve", lambda e, o=o, c=c, a=a, b=b: e.scalar_tensor_tensor(
                            out=xt[:, c, a:b], in0=o, scalar=k.pscale[:, j * NCH + c:j * NCH + c + 1], in1=xt[:, c, a:b],
                            op0=ALU.mult, op1=ALU.add),
                            reads=(tok, "pl_x"), writes=("pl_x",))
            S.dma("sp", lambda e, c0=c0, n=ncols: e.dma_start(out=k.XT[:, :, c0:c0 + n], in_=xt[:, :, 0:n]),
                  reads=("pl_x",), writes=(("XT", ti),))
            if last:
                for oi, (src, c_lo, dst) in enumerate(((hb, 512, k.new_pool_prompt[j]), (hs, NS, k.new_pool_sample[j]))):
                    stok = "pl_h" if oi == 0 else "pl_hs"
                    for g in range(4):
                        for cc in range(4):
                            c = g * 4 + cc
                            S.op("pe", lambda e, g=g, cc=cc, c=c, src=src, c_lo=c_lo: e.transpose(
                                out=k.ps[g][0:PBUF, cc * 128:(cc + 1) * 128], in_=src[:, c, c_lo:c_lo + PBUF],
                                identity=k.ident[:, :]), reads=(stok,), writes=(("ps", g),))
                        S.op("act", lambda e, g=g, oi=oi: e.copy(
                            out=pout[oi][0:PBUF, g * 512:(g + 1) * 512], in_=k.ps[g][0:PBUF, :]),
                            reads=(("ps", g),), writes=(("pl_po", oi),))
                    S.dma("sp", lambda e, dst=dst, oi=oi: e.dma_start(out=dst, in_=pout[oi][0:PBUF, :]),
                          reads=(("pl_po", oi),), writes=(("pool_out", j, oi),))
            S.flush(k.block)
    S.barrier()


def phase_s5(k, L, j):
    nc, S, cfg = k.nc, k.S, k.cfg
    TC = 32
    G2 = 64
    TWO_PI_HI = 6.28125
    TWO_PI_LO = 2.0 * math.pi - 6.28125
    with ExitStack() as es:
        A_ = lambda name, shape, dt: es.enter_context(nc.sbuf_tensor(f"{name}_L{L}", shape, dt))
        AR2 = A_("s5_ar2", [128, G2, 2], F32)
        AI2 = A_("s5_ai2", [128, G2, 2], F32)
        LB = [[A_(f"s5_lb{ri}{eo}", [128, NCH, 128], BF16) for eo in range(2)] for ri in range(2)]
        LCR = A_("s5_lcr", [128, G2, 64], BF16)
        LCI = A_("s5_lci", [128, G2, 64], BF16)
        Dg = A_("s5_dg", [128, NCH, 128], F32)
        S0 = A_("s5_s0", [128, G2, 2], F32)
        with ExitStack() as pes:
            P_ = lambda name, shape, dt: pes.enter_context(nc.sbuf_tensor(f"{name}_L{L}", shape, dt))
            names = ["ar", "ai", "dt", "th", "mag", "kk", "r", "sh", "sn", "cs", "abr", "abi", "den", "nr", "t1", "t2", "fr", "fi"]
            v = {n: P_("s5p_" + n, [128, G2], F32) for n in names}
            fT = [P_(f"s5p_fT{i}", [64, 128], F32) for i in range(2)]
            FW = [P_(f"s5p_fw{i}", [128, NCH, 128], F32) for i in range(2)]
            BW = [P_(f"s5p_bw{i}", [128, NCH, 128], F32) for i in range(2)]
            TM = [P_(f"s5p_tm{i}", [128, NCH, 128], F32) for i in range(2)]
            sel = P_("s5p_sel", [64, NCH, 128], F32)
            msk = P_("s5p_msk", [128, 2], F32)
            LBf = [P_(f"s5p_lbf{i}", [128, NCH, 128], F32) for i in range(2)]
            LCf = [P_(f"s5p_lcf{i}", [128, G2, 64], F32) for i in range(2)]
            dcol = P_("s5p_dcol", [128, NCH], F32)
            st_view = lambda ap2d: ap2d.rearrange("(gp par) n -> (par n) gp", par=2)
            with nc.allow_non_contiguous_dma(reason="small ssm parameter tables"):
                S.dma("sp", lambda e: e.dma_start(out=v["ar"][:, :], in_=st_view(k.ssm_a_re[j])), writes=("p_ar",))
                S.dma("sp", lambda e: e.dma_start(out=v["ai"][:, :], in_=st_view(k.ssm_a_im[j])), writes=("p_ai",))
                for par in range(2):
                    S.dma("sp", lambda e, par=par: e.dma_start(
                        out=v["dt"][par * 64:(par + 1) * 64, :],
                        in_=k.ssm_log_dt[j:j + 1, :].rearrange("o (gp par) -> o gp par", par=2)[:, :, par].broadcast_to([64, G2])),
                        writes=("p_dt",))
                S.dma("sp", lambda e: e.dma_start(out=S0[:, :, 0], in_=st_view(k.state_re[j])), writes=("s5_s0",))
                S.dma("sp", lambda e: e.dma_start(out=S0[:, :, 1], in_=st_view(k.state_im[j])), writes=("s5_s0",))
                S.dma("sp", lambda e: e.dma_start(out=dcol[:, :], in_=k.ssm_d[j].rearrange("(c p) -> p c", p=128)), writes=("p_dcol",))
                S.dma("sp", lambda e: e.dma_start(out=sel[:, :, :], in_=k.c_sel), writes=("p_sel",))
                S.dma("sp", lambda e: e.dma_start(out=BW[0][:, :, :], in_=k.b_re_w[j]), writes=("p_bw0",))
                S.dma("sp", lambda e: e.dma_start(out=BW[1][:, :, :], in_=k.b_im_w[j]), writes=("p_bw1",))
                S.dma("sp", lambda e: e.dma_start(out=LCf[0][:, :, :], in_=k.c_re_w[j]), writes=("p_lcf0",))
                S.dma("sp", lambda e: e.dma_start(out=LCf[1][:, :, :], in_=k.c_im_w[j]), writes=("p_lcf1",))
                S.dma("sp", lambda e: e.dma_start(out=msk[:, :], in_=k.c_msk), writes=("p_msk",))
                S.flush(k.block)
            def tt(o, a, b, op, rd, wr, eng="dve"):
                S.op(eng, lambda e: e.tensor_tensor(out=o, in0=a, in1=b, op=op), reads=rd, writes=wr)
            def ts(o, a, s1, op0, rd, wr, s2=None, op1=None, eng="dve"):
                if op1 is None:
                    S.op(eng, lambda e: e.tensor_scalar(out=o, in0=a, scalar1=s1, scalar2=None, op0=op0), reads=rd, writes=wr)
                else:
                    S.op(eng, lambda e: e.tensor_scalar(out=o, in0=a, scalar1=s1, scalar2=s2, op0=op0, op1=op1), reads=rd, writes=wr)
            def act(o, a, f, rd, wr, scale=1.0):
                S.op("act", lambda e: e.activation(out=o, in_=a, func=f, scale=scale), reads=rd, writes=wr)
            V = lambda n: v[n][:, :]
            act(V("dt"), V("dt"), AF.Exp, ("p_dt",), ("p_dt",))
            tt(V("t1"), V("dt"), V("ar"), ALU.mult, ("p_dt", "p_ar"), ("p_t1",))
            act(V("mag"), V("t1"), AF.Exp, ("p_t1",), ("p_mag",))
            tt(V("th"), V("dt"), V("ai"), ALU.mult, ("p_dt", "p_ai"), ("p_th",))
            ts(V("kk"), V("th"), math.pi, ALU.is_gt, ("p_th",), ("p_kk",))
            for m in (1, 2, 3):
                ts(V("t2"), V("th"), (2 * m + 1) * math.pi, ALU.is_gt, ("p_th",), ("p_t2",))
                tt(V("kk"), V("kk"), V("t2"), ALU.add, ("p_kk", "p_t2"), ("p_kk",))
            ts(V("t2"), V("kk"), -TWO_PI_HI, ALU.mult, ("p_kk",), ("p_t2",))
            tt(V("r"), V("th"), V("t2"), ALU.add, ("p_th", "p_t2"), ("p_r",))
            ts(V("t2"), V("kk"), -TWO_PI_LO, ALU.mult, ("p_kk",), ("p_t2",))
            tt(V("r"), V("r"), V("t2"), ALU.add, ("p_r", "p_t2"), ("p_r",))
            act(V("sn"), V("r"), AF.Sin, ("p_r",), ("p_sn",))
            act(V("sh"), V("r"), AF.Sin, ("p_r",), ("p_sh",), scale=0.5)
            tt(V("cs"), V("sh"), V("sh"), ALU.mult, ("p_sh",), ("p_cs",))
            ts(V("cs"), V("cs"), -2.0, ALU.mult, ("p_cs",), ("p_cs",), s2=1.0, op1=ALU.add)
            tt(V("abr"), V("mag"), V("cs"), ALU.mult, ("p_mag", "p_cs"), ("p_abr",))
            tt(V("abi"), V("mag"), V("sn"), ALU.mult, ("p_mag", "p_sn"), ("p_abi",))
            S.op("dve", lambda e: e.tensor_copy(out=AR2[:, :, 0], in_=V("abr")), reads=("p_abr",), writes=("s5_ar2",))
            S.op("dve", lambda e: e.tensor_copy(out=AR2[:, :, 1], in_=V("abr")), reads=("p_abr",), writes=("s5_ar2",))
            S.op("dve", lambda e: e.tensor_copy(out=AI2[:, :, 1], in_=V("abi")), reads=("p_abi",), writes=("s5_ai2",))
            ts(AI2[:, :, 0], V("abi"), -1.0, ALU.mult, ("p_abi",), ("s5_ai2",))
            tt(V("den"), V("ar"), V("ar"), ALU.mult, ("p_ar",), ("p_den",))
            tt(V("t1"), V("ai"), V("ai"), ALU.mult, ("p_ai",), ("p_t1",))
            tt(V("den"), V("den"), V("t1"), ALU.add, ("p_den", "p_t1"), ("p_den",))
            S.op("dve", lambda e: e.reciprocal(out=V("den"), in_=V("den")), reads=("p_den",), writes=("p_den",))
            ts(V("nr"), V("abr"), -1.0, ALU.add, ("p_abr",), ("p_nr",))
            tt(V("t1"), V("nr"), V("ar"), ALU.mult, ("p_nr", "p_ar"), ("p_t1",))
            tt(V("t2"), V("abi"), V("ai"), ALU.mult, ("p_abi", "p_ai"), ("p_t2",))
            tt(V("t1"), V("t1"), V("t2"), ALU.add, ("p_t1", "p_t2"), ("p_t1",))
            tt(V("fr"), V("t1"), V("den"), ALU.mult, ("p_t1", "p_den"), ("p_fr",))
            tt(V("t1"), V("abi"), V("ar"), ALU.mult, ("p_abi", "p_ar"), ("p_t1",))
            tt(V("t2"), V("nr"), V("ai"), ALU.mult, ("p_nr", "p_ai"), ("p_t2",))
            tt(V("t1"), V("t1"), V("t2"), ALU.subtract, ("p_t1", "p_t2"), ("p_t1",))
            tt(V("fi"), V("t1"), V("den"), ALU.mult, ("p_t1", "p_den"), ("p_fi",))
            for i, nm in enumerate(("fr", "fi")):
                S.op("pe", lambda e, nm=nm, i=i: e.transpose(out=k.ps[i][0:64, 0:128], in_=v[nm][:, :], identity=k.ident[:, :]),
                     reads=("p_" + nm,), writes=(("ps", i),))
                S.op("act", lambda e, i=i: e.copy(out=fT[i][:, :], in_=k.ps[i][0:64, 0:128]),
                     reads=(("ps", i),), writes=(("p_fT", i),))
                for q4 in range(4):
                    bank = 2 + (i * 4 + q4) % 4
                    for jj4 in range(4):
                        jj = q4 * 4 + jj4
                        S.op("pe", lambda e, i=i, jj=jj, jj4=jj4, bank=bank: e.matmul(
                            out=k.ps[bank][:, jj4 * 128:(jj4 + 1) * 128], lhsT=sel[:, jj, :], rhs=fT[i][:, :],
                            start=True, stop=True), reads=("p_sel", ("p_fT", i)), writes=(("ps", bank),))
                    S.op("act", lambda e, i=i, q4=q4, bank=bank: e.copy(
                        out=FW[i][:, q4 * 4:(q4 + 1) * 4, :], in_=k.ps[bank][:, :].rearrange("p (a b) -> p a b", a=4)),
                        reads=(("ps", bank),), writes=(("p_fw", i),))
            FWr, FWi, BWr, BWi = (t[:, :, :] for t in (FW[0], FW[1], BW[0], BW[1]))
            tt(TM[0][:, :, :], FWr, BWr, ALU.mult, (("p_fw", 0), "p_bw0"), ("p_tm0",))
            tt(TM[1][:, :, :], FWi, BWi, ALU.mult, (("p_fw", 1), "p_bw1"), ("p_tm1",))
            tt(LBf[0][:, :, :], TM[0][:, :, :], TM[1][:, :, :], ALU.subtract, ("p_tm0", "p_tm1"), ("p_lbf0",))
            tt(TM[0][:, :, :], FWr, BWi, ALU.mult, (("p_fw", 0), "p_bw1"), ("p_tm0",))
            tt(TM[1][:, :, :], FWi, BWr, ALU.mult, (("p_fw", 1), "p_bw0"), ("p_tm1",))
            tt(LBf[1][:, :, :], TM[0][:, :, :], TM[1][:, :, :], ALU.add, ("p_tm0", "p_tm1"), ("p_lbf1",))
            for ri in range(2):
                for eo in range(2):
                    ts(LB[ri][eo][:, :, :], LBf[ri][:, :, :], msk[:, eo:eo + 1], ALU.mult, (f"p_lbf{ri}", "p_msk"), ("s5_lb",))
            S.op("dve", lambda e: e.tensor_copy(out=LCR[:, :, :], in_=LCf[0][:, :, :]), reads=("p_lcf0",), writes=("s5_lcr",))
            ts(LCI[:, :, :], LCf[1][:, :, :], -1.0, ALU.mult, ("p_lcf1",), ("s5_lci",))
            for c in range(NCH):
                ts(Dg[:, c, :], k.ident[:, :], dcol[:, c:c + 1], ALU.mult, ("ident", "p_dcol"), ("s5_dg",))
            S.flush(k.block)
        S.barrier()
        if getattr(cfg, "s5_stop", 0) == 1:
            return
        xt = A_("s5_x", [128, NCH, 520], F32)
        hb = A_("s5_hb", [128, NCH, 520], BF16)
        gb = A_("s5_gb", [128, NCH, 520], BF16)
        BU = A_("s5_bu", [128, TC, G2, 2], F32)
        SS = A_("s5_ss", [128, TC + 1, G2, 2], F32)
        SSb = A_("s5_ssb", [128, TC, G2, 2], BF16)
        t1 = A_("s5_t1", [128, G2, 2], F32)
        t2 = A_("s5_t2", [128, G2, 2], F32)
        t1p = A_("s5_t1p", [128, G2, 2], F32)
        t2p = A_("s5_t2p", [128, G2, 2], F32)
        GA = getattr(cfg, "s5_ga", 40)
        SSTOK = (("s5_ss", "dve"), ("s5_ss", "pool"))
        sq = [A_(f"s5_sq{i}", [128, 520], F32) for i in range(2)]
        rstd = A_("s5_rstd", [128, 520], F32)
        wa = [A_(f"s5_wa{i}", [128, NCH, 256], BF16) for i in range(2)]
        wb = [A_(f"s5_wb{i}", [128, NCH, 256], BF16) for i in range(2)]
        sg = [A_(f"s5_sg{i}", [128, 520], F32) for i in range(2)]
        xr = [A_(f"s5_xr{i}", [128, 520], F32) for i in range(2)]
        gcol = k.g_mix[:, L * NCH:(L + 1) * NCH]
        Wa = k.w_glu_a[j].rearrange("(kc p) f -> p kc f", p=128)
        Wb = k.w_glu_b[j].rearrange("(kc p) f -> p kc f", p=128)
        SBK = (4, 5)
        S.op("pool", lambda e: e.memset(SS[:, 0, :, :], 0.0), writes=SSTOK)
        nbank = [0]
        nw = [0]

        stop = getattr(cfg, "s5_stop", 0)

        def scan_chunk(col0, nt):
            if stop == 2:
                return
            for g8 in range(G2 // 8):
                banks = (nbank[0] % 4, (nbank[0] + 1) % 4)
                nbank[0] += 2
                for hh in range(2):
                    bank = banks[hh]
                    for a in range(2):
                        for q2 in range(2):
                            gp = g8 * 8 + a * 4 + hh * 2 + q2
                            jj = gp // 4
                            for ri in range(2):
                                col = ((a * 2 + q2) * 2 + ri) * nt
                                S.op("pe", lambda e, bank=bank, col=col, ri=ri, jj=jj, q2=q2, hh=hh: e.matmul(
                                    out=k.ps[bank][:, col:col + nt],
                                    lhsT=LB[ri][q2][64 * hh:64 * hh + 64, jj, :], rhs=hb[64 * hh:64 * hh + 64, jj, col0:col0 + nt],
                                    start=True, stop=True),
                                    reads=("s5_lb", "s5_hb"), writes=(("ps", bank),))
                    for a in range(2):
                        gp0 = g8 * 8 + a * 4 + hh * 2
                        S.op("act", lambda e, bank=bank, a=a, gp0=gp0: e.copy(
                            out=BU[:, 0:nt, gp0:gp0 + 2, :].rearrange("p t g r -> p g r t"),
                            in_=k.ps[bank][:, a * 4 * nt:(a + 1) * 4 * nt].rearrange("p (g r t) -> p g r t", g=2, r=2)),
                            reads=(("ps", bank),), writes=("s5_bu",))
            if stop == 3:
                return
            for t in range(nt):
                for eng, g0, g1, tt1, tt2 in (("dve", 0, GA, t1, t2), ("pool", GA, G2, t1p, t2p)):
                    if g1 <= g0:
                        continue
                    sst = ("s5_ss", eng)
                    S.op(eng, lambda e, t=t, g0=g0, g1=g1, tt1=tt1: e.tensor_tensor(
                        out=tt1[:, g0:g1, :], in0=AR2[:, g0:g1, :], in1=SS[:, t, g0:g1, :], op=ALU.mult),
                        reads=("s5_ar2", sst), writes=(("s5_t1", eng),))
                    S.op(eng, lambda e, t=t, g0=g0, g1=g1, tt2=tt2: e.tensor_tensor(
                        out=tt2[:, g0:g1, :], in0=AI2[:, g0:g1, :], in1=SS[:, t, g0:g1, ::-1], op=ALU.mult),
                        reads=("s5_ai2", sst), writes=(("s5_t2", eng),))
                    S.op(eng, lambda e, g0=g0, g1=g1, tt1=tt1, tt2=tt2: e.tensor_tensor(
                        out=tt1[:, g0:g1, :], in0=tt1[:, g0:g1, :], in1=tt2[:, g0:g1, :], op=ALU.add),
                        reads=(("s5_t1", eng), ("s5_t2", eng)), writes=(("s5_t1", eng),))
                    S.op(eng, lambda e, t=t, g0=g0, g1=g1, tt1=tt1: e.tensor_tensor(
                        out=SS[:, t + 1, g0:g1, :], in0=tt1[:, g0:g1, :], in1=BU[:, t, g0:g1, :], op=ALU.add),
                        reads=(("s5_t1", eng), "s5_bu"), writes=(sst,))
            if stop == 4:
                return
            S.op("act", lambda e: e.copy(out=SSb[:, 0:nt, :, :], in_=SS[:, 1:nt + 1, :, :]),
                 reads=SSTOK, writes=("s5_ssb",))
            for jj in range(NCH):
                bank = nbank[0] % 4
                nbank[0] += 1
                for hh in range(2):
                    first = True
                    for q2 in range(2):
                        gp = jj * 4 + hh * 2 + q2
                        for ri, LC in enumerate((LCR, LCI)):
                            S.op("pe", lambda e, bank=bank, hh=hh, gp=gp, ri=ri, LC=LC, first=first: e.matmul(
                                out=k.ps[bank][64 * hh:64 * hh + 64, 0:nt], lhsT=LC[:, gp, :], rhs=SSb[:, 0:nt, gp, ri],
                                start=first, stop=False),
                                reads=("s5_lcr", "s5_lci", "s5_ssb"), writes=(("ps", bank),))
                            first = False
                    S.op("pe", lambda e, bank=bank, hh=hh, jj=jj: e.matmul(
                        out=k.ps[bank][64 * hh:64 * hh + 64, 0:nt], lhsT=Dg[:, jj, 64 * hh:64 * hh + 64], rhs=xt[:, jj, col0:col0 + nt],
                        start=False, stop=True),
                        reads=("s5_dg", "s5_x"), writes=(("ps", bank),))
                S.op("act", lambda e, bank=bank, jj=jj: e.activation(
                    out=gb[:, jj, col0:col0 + nt], in_=k.ps[bank][:, 0:nt], func=AF.Gelu),
                    reads=(("ps", bank),), writes=("s5_gb",))

        st_view = lambda ap2d: ap2d.rearrange("(gp par) n -> (par n) gp", par=2)
        for ti in range(cfg.ntile):
            c0, ncols = tile_cols(cfg, ti)
            segs = mm_cols(ncols)
            last = ncols > 512
            S.dma("sp", lambda e, c0=c0, n=ncols: e.dma_start(out=xt[:, :, 0:n], in_=k.XT[:, :, c0:c0 + n]),
                  reads=(("XT", ti),), writes=("s5_x",))
            rmsnorm_tile(k, xt, "s5_x", ncols, gcol, xt, "s5_x", sq, 6, (7, 0), rstd)
            for c in range(NCH):
                S.op("act", lambda e, c=c, n=ncols: e.copy(out=hb[:, c, 0:n], in_=xt[:, c, 0:n]),
                     reads=("s5_x",), writes=("s5_hb",))
            for tc in range(512 // TC):
                scan_chunk(tc * TC, TC)
                S.op("act", lambda e: e.copy(out=SS[:, 0, :, :], in_=SS[:, TC, :, :]),
                     reads=SSTOK, writes=SSTOK)
                S.flush(k.block)
            if last:
                with nc.allow_non_contiguous_dma(reason="ssm state output"):
                    S.dma("sp", lambda e: e.dma_start(out=st_view(k.new_ssm_re_prompt[j]), in_=SS[:, 0, :, 0]),
                          reads=SSTOK, writes=("ssm_out0",))
                    S.dma("sp", lambda e: e.dma_start(out=st_view(k.new_ssm_im_prompt[j]), in_=SS[:, 0, :, 1]),
                          reads=SSTOK, writes=("ssm_out1",))
                    S.flush(k.block)
                S.op("act", lambda e: e.copy(out=SS[:, 0, :, :], in_=S0[:, :, :]),
                     reads=("s5_s0",) + SSTOK, writes=SSTOK)
                scan_chunk(512, NS)
                with nc.allow_non_contiguous_dma(reason="ssm state output"):
                    S.dma("sp", lambda e: e.dma_start(out=st_view(k.new_ssm_re_sample[j]), in_=SS[:, NS, :, 0]),
                          reads=SSTOK, writes=("ssm_out2",))
                    S.dma("sp", lambda e: e.dma_start(out=st_view(k.new_ssm_im_sample[j]), in_=SS[:, NS, :, 1]),
                          reads=SSTOK, writes=("ssm_out3",))
                    S.flush(k.block)
            for oc2 in range(NCH // 2 if stop not in (2, 3, 4, 5) else 0):
                slot = nw[0] % 2
                nw[0] += 1
                S.dma("pool", lambda e, slot=slot, oc2=oc2: e.dma_start(out=wa[slot][:, :, :], in_=Wa[:, :, oc2 * 256:(oc2 + 1) * 256]),
                      writes=(("s5_wa", slot),))
                S.dma("pool", lambda e, slot=slot, oc2=oc2: e.dma_start(out=wb[slot][:, :, :], in_=Wb[:, :, oc2 * 256:(oc2 + 1) * 256]),
                      writes=(("s5_wb", slot),))
                for half in range(2):
                    oc = oc2 * 2 + half
                    par = oc % 2
                    S.dma("sp", lambda e, par=par, oc=oc, c0=c0, n=ncols: e.dma_start(out=xr[par][:, 0:n], in_=k.XT[:, oc, c0:c0 + n]),
                          reads=(("XT", ti),), writes=(("s5_xr", par),))
                    for which, wt_, wtok in ((0, wa, "s5_wa"), (1, wb, "s5_wb")):
                        bank = par * 2 + which
                        for kc in range(NCH):
                            for (a, b, smp) in segs:
                                o = k.ps[SBK[par]][:, which * 8:which * 8 + NS] if smp else k.ps[bank][:, 0:512]
                                tok = ("ps", SBK[par]) if smp else ("ps", bank)
                                S.op("pe", lambda e, o=o, w=wt_[slot], kc=kc, half=half, a=a, b=b: e.matmul(
                                    out=o, lhsT=w[:, kc, half * 128:(half + 1) * 128], rhs=gb[:, kc, a:b],
                                    start=(kc == 0), stop=(kc == NCH - 1)),
                                    reads=((wtok, slot), "s5_gb"), writes=(tok,))
                    for (a, b, smp) in segs:
                        if smp:
                            ain, bin_ = k.ps[SBK[par]][:, 0:NS], k.ps[SBK[par]][:, 8:8 + NS]
                            at_, bt_ = ("ps", SBK[par]), ("ps", SBK[par])
                        else:
                            ain, bin_ = k.ps[par * 2][:, 0:512], k.ps[par * 2 + 1][:, 0:512]
                            at_, bt_ = ("ps", par * 2), ("ps", par * 2 + 1)
                        sgt = ("s5_sg", par, smp)
                        S.op("act", lambda e, bin_=bin_, a=a, b=b, par=par: e.activation(out=sg[par][:, a:b], in_=bin_, func=AF.Sigmoid),
                             reads=(bt_,), writes=(sgt,))
                        S.op("dve", lambda e, ain=ain, a=a, b=b, par=par: e.tensor_tensor(
                            out=sg[par][:, a:b], in0=sg[par][:, a:b], in1=ain, op=ALU.mult),
                            reads=(sgt, at_), writes=(sgt,))
                        S.op("pool", lambda e, a=a, b=b, par=par: e.tensor_tensor(
                            out=xr[par][:, a:b], in0=xr[par][:, a:b], in1=sg[par][:, a:b], op=ALU.add),
                            reads=(sgt, ("s5_xr", par)), writes=(("s5_xr", par),))
                    S.dma("sp", lambda e, par=par, oc=oc, c0=c0, n=ncols: e.dma_start(out=k.XT[:, oc, c0:c0 + n], in_=xr[par][:, 0:n]),
                          reads=(("s5_xr", par),), writes=(("XT", ti),))
            S.flush(k.block)
    S.barrier()


def phase_sb(k, L, j):
    nc, S, cfg = k.nc, k.S, k.cfg
    T, TT = cfg.T, cfg.TT
    NH = 16
    SCALE = 1.0 / math.sqrt(128.0)
    Wqkv = k.w_qkv[j].rearrange("(kc p) f -> p kc f", p=128)
    Wo = k.w_o[j].rearrange("(kc p) f -> p kc f", p=128)
    with ExitStack() as es:
        A_ = lambda name, shape, dt: es.enter_context(nc.sbuf_tensor(f"{name}_L{L}", shape, dt))
        gq = A_("sb_gq", [128, 1], F32)
        gk = A_("sb_gk", [128, 1], F32)
        bcol = A_("sb_bcol", [128, NH], F32)
        brow = A_("sb_brow", [128, NH, NS], F32)
        tri = A_("sb_tri", [128, 128], BF16)
        trif = A_("sb_trif", [128, 128], F32)
        mlt = A_("sb_mlt", [128, 128], F32)
        m8 = A_("sb_m8", [128, NH, NS], F32)
        one_col = A_("sb_one", [128, 1], F32)
        qsT = A_("sb_qsT", [128, NH, NS], BF16)
        ksT = A_("sb_ksT", [128, NH, NS], BF16)
        vs = A_("sb_vs", [NS, D], BF16)
        osT = A_("sb_osT", [128, NH, NS], BF16)
        with nc.allow_non_contiguous_dma(reason="tiny sb params"):
            S.dma("sp", lambda e: e.dma_start(out=gq[:, :], in_=k.sb_q_norm[j].rearrange("(p o) -> p o", o=1)), writes=("sb_gq",))
            S.dma("sp", lambda e: e.dma_start(out=gk[:, :], in_=k.sb_k_norm[j].rearrange("(p o) -> p o", o=1)), writes=("sb_gk",))
            S.dma("sp", lambda e: e.dma_start(out=bcol[:, :], in_=k.sb_bias[j:j + 1, :].broadcast_to([128, NH])), writes=("sb_bcol",))
            S.dma("sp", lambda e: e.dma_start(out=trif[:, :], in_=k.c_tri), writes=("sb_trif",))
            S.dma("sp", lambda e: e.dma_start(out=mlt[:, :], in_=k.c_mlt), writes=("sb_mlt",))
            S.flush(k.block)
        S.op("dve", lambda e: e.tensor_copy(out=tri[:, :], in_=trif[:, :]), reads=("sb_trif",), writes=("sb_tri",))
        S.op("dve", lambda e: e.memset(one_col[:, :], 1.0), writes=("sb_one",))
        S.op("dve", lambda e: e.tensor_copy(out=brow[:, :, :], in_=bcol[:, :].unsqueeze(2).broadcast_to([128, NH, NS])),
             reads=("sb_bcol",), writes=("sb_brow",))
        S.op("dve", lambda e: e.tensor_copy(out=m8[:, :, :], in_=mlt[:, 0:NS].unsqueeze(1).broadcast_to([128, NH, NS])),
             reads=("sb_mlt",), writes=("sb_m8",))
        SBK = (4, 5)
        with ExitStack() as aes:
            B_ = lambda name, shape, dt: aes.enter_context(nc.sbuf_tensor(f"{name}_L{L}", shape, dt))
            xt = B_("sa_x", [128, NCH, 520], F32)
            hb = B_("sa_hb", [128, NCH, 520], BF16)
            sq = [B_(f"sa_sq{i}", [128, 520], F32) for i in range(2)]
            rstd = B_("sa_rstd", [128, 520], F32)
            wqk = [B_(f"sa_wqk{i}", [128, NCH, 256], BF16) for i in range(2)]
            wv = [B_(f"sa_wv{i}", [128, NCH, 512], BF16) for i in range(2)]
            hsq = [B_(f"sa_hsq{i}", [128, 520], F32) for i in range(2)]
            hrs = [B_(f"sa_hrs{i}", [128, 520], F32) for i in range(2)]
            knf = [B_(f"sa_knf{i}", [128, 520], F32) for i in range(3)]
            qkb = [B_(f"sa_qkb{i}", [128, 520], BF16) for i in range(2)]
            ktok = B_("sa_ktok", [128, 4, D], F32)
            kstok = B_("sa_kstok", [NS, D], F32)
            vtok = [B_(f"sa_vtok{i}", [128, D], F32) for i in range(2)]
            gcol = k.g_mix[:, L * NCH:(L + 1) * NCH]
            nw = 0
            nv = 0
            nh_ = 0
            for ti in range(cfg.ntile):
                c0, ncols = tile_cols(cfg, ti)
                segs = mm_cols(ncols)
                last = ncols > 512
                S.dma("sp", lambda e, c0=c0, n=ncols: e.dma_start(out=xt[:, :, 0:n], in_=k.XT[:, :, c0:c0 + n]),
                      reads=(("XT", ti),), writes=("sa_x",))
                rmsnorm_tile(k, xt, "sa_x", ncols, gcol, hb, "sa_hb", sq, 6, (7, 0), rstd)
                def a1(fh, par, slot, half, f2):
                    if half == 0:
                        S.dma("pool", lambda e: e.dma_start(out=wqk[slot][:, :, :], in_=Wqkv[:, :, f2 * 256:(f2 + 1) * 256]),
                              writes=(("sa_wqk", slot),))
                    bank = par
                    for kc in range(NCH):
                        for (a, b, smp) in segs:
                            o = k.ps[SBK[par]][:, 0:NS] if smp else k.ps[bank][:, 0:512]
                            tok = ("ps", SBK[par]) if smp else ("ps", bank)
                            S.op("pe", lambda e, o=o, kc=kc, a=a, b=b: e.matmul(
                                out=o, lhsT=wqk[slot][:, kc, half * 128:(half + 1) * 128], rhs=hb[:, kc, a:b],
                                start=(kc == 0), stop=(kc == NCH - 1)),
                                reads=(("sa_wqk", slot), "sa_hb"), writes=(tok,))

                def a2(fh, par, ks):
                    isk = fh >= NH
                    h = fh % NH
                    bank = par
                    for (a, b, smp) in segs:
                        pin = k.ps[SBK[par]][:, 0:NS] if smp else k.ps[bank][:, 0:512]
                        ptok = ("ps", SBK[par]) if smp else ("ps", bank)
                        sso = k.ps[SBK[par]][:, 8:8 + NS] if smp else k.ps[2 + par][:, 0:512]
                        sstok = ("ps", SBK[par]) if smp else ("ps", 2 + par)
                        S.op("act", lambda e, pin=pin, a=a, b=b: e.activation(out=hsq[par][:, a:b], in_=pin, func=AF.Square),
                             reads=(ptok,), writes=(("sa_hsq", par, smp),))
                        S.op("pe", lambda e, sso=sso, a=a, b=b: e.matmul(
                            out=sso, lhsT=k.ones_f[:, :], rhs=hsq[par][:, a:b], start=True, stop=True),
                            reads=(("sa_hsq", par, smp), "ones_f"), writes=(sstok,))
                        S.op("act", lambda e, sso=sso, a=a, b=b: e.activation(
                            out=hrs[par][:, a:b], in_=sso, func=AF.Sqrt, scale=1.0 / 128.0, bias=k.eps_col[:, 0:1]),
                            reads=(sstok,), writes=(("sa_hrs", par, smp),))
                        S.op("dve", lambda e, a=a, b=b: e.reciprocal(out=hrs[par][:, a:b], in_=hrs[par][:, a:b]),
                             reads=(("sa_hrs", par, smp),), writes=(("sa_hrs", par, smp),))
                        if isk:
                            S.op("dve", lambda e, pin=pin, a=a, b=b: e.scalar_tensor_tensor(
                                out=knf[ks][:, a:b], in0=pin, scalar=gk[:, 0:1], in1=hrs[par][:, a:b], op0=ALU.mult, op1=ALU.mult),
                                reads=(ptok, ("sa_hrs", par, smp), "sb_gk"), writes=(("sa_knf", ks, smp),))
                            S.op("act", lambda e, a=a, b=b: e.copy(out=qkb[par][:, a:b], in_=knf[ks][:, a:b]),
                                 reads=(("sa_knf", ks, smp),), writes=(("sa_qkb", par, smp),))
                        else:
                            S.op("dve", lambda e, pin=pin, a=a, b=b: e.scalar_tensor_tensor(
                                out=qkb[par][:, a:b], in0=pin, scalar=gq[:, 0:1], in1=hrs[par][:, a:b], op0=ALU.mult, op1=ALU.mult),
                                reads=(ptok, ("sa_hrs", par, smp), "sb_gq"), writes=(("sa_qkb", par, smp),))
                        if smp:
                            dst = ksT if isk else qsT
                            S.op("act", lambda e, dst=dst: e.copy(out=dst[:, h, :], in_=qkb[par][:, 512:520]),
                                 reads=(("sa_qkb", par, smp),), writes=("sb_ksT" if isk else "sb_qsT",))
                    dsc = k.KTs if isk else k.QTs
                    S.dma("sp", lambda e: e.dma_start(out=dsc[h, :, c0:c0 + 512], in_=qkb[par][:, 0:512]),
                          reads=(("sa_qkb", par, False),), writes=(("qk_scr", isk, h, ti),))

                def a3(fh, par, ks):
                    isk = fh >= NH
                    h = fh % NH
                    if not isk:
                        return
                    for tb in range(4):
                        S.op("pe", lambda e, tb=tb: e.transpose(
                            out=k.ps[6][:, tb * 128:(tb + 1) * 128], in_=knf[ks][:, tb * 128:(tb + 1) * 128], identity=k.ident[:, :]),
                            reads=(("sa_knf", ks, False),), writes=(("ps", 6),))
                    S.op("act", lambda e: e.copy(
                        out=ktok[:, :, h * 128:(h + 1) * 128], in_=k.ps[6][:, :].rearrange("p (b d) -> p b d", b=4)),
                        reads=(("ps", 6),), writes=("sa_ktok",))
                    if last:
                        S.op("pe", lambda e: e.transpose(
                            out=k.ps[7][0:NS, 0:128], in_=knf[ks][:, 512:520], identity=k.ident[:, :]),
                            reads=(("sa_knf", ks, True),), writes=(("ps", 7),))
                        S.op("act", lambda e: e.copy(out=kstok[:, h * 128:(h + 1) * 128], in_=k.ps[7][0:NS, 0:128]),
                             reads=(("ps", 7),), writes=("sa_kstok",))

                atasks = []
                for f2 in range(NH):
                    slot = nw % 2
                    nw += 1
                    for half in range(2):
                        atasks.append(dict(fh=f2 * 2 + half, par=nh_ % 2, ks=nh_ % 3, slot=slot, half=half, f2=f2))
                        nh_ += 1
                ASKEW = getattr(cfg, "sa_skew", 0)
                for n in range(len(atasks) + 2 * ASKEW):
                    if n < len(atasks):
                        t_ = atasks[n]
                        a1(t_["fh"], t_["par"], t_["slot"], t_["half"], t_["f2"])
                    if 0 <= n - ASKEW < len(atasks):
                        t_ = atasks[n - ASKEW]
                        a2(t_["fh"], t_["par"], t_["ks"])
                    if 0 <= n - 2 * ASKEW < len(atasks):
                        t_ = atasks[n - 2 * ASKEW]
                        a3(t_["fh"], t_["par"], t_["ks"])
                S.dma("sp", lambda e, c0=c0: e.dma_start(
                    out=k.new_k_prompt[j, c0:c0 + 512, :].rearrange("(b p) f -> p b f", p=128), in_=ktok[:, :, :]),
                    reads=("sa_ktok",), writes=(("kout", ti),))
                if last:
                    S.dma("sp", lambda e: e.dma_start(out=k.new_k_sample[j], in_=kstok[:, :]), reads=("sa_kstok",), writes=("ksout",))
                blocks = [(128 * b4, 128) for b4 in range(4)] + ([(512, NS)] if last else [])
                for vs4 in range(4):
                    slot = nv % 2
                    nv += 1
                    S.dma("pool", lambda e, slot=slot, vs4=vs4: e.dma_start(
                        out=wv[slot][:, :, :], in_=Wqkv[:, :, 4096 + vs4 * 512:4096 + (vs4 + 1) * 512]),
                        writes=(("sa_wv", slot),))
                    for bi, (b0, nr) in enumerate(blocks):
                        bank = bi if bi < 4 else 7
                        for kc in range(NCH):
                            S.op("pe", lambda e, bank=bank, nr=nr, b0=b0, kc=kc, slot=slot: e.matmul(
                                out=k.ps[bank][0:nr, 0:512], lhsT=hb[:, kc, b0:b0 + nr], rhs=wv[slot][:, kc, :],
                                start=(kc == 0), stop=(kc == NCH - 1)),
                                reads=(("sa_wv", slot), "sa_hb"), writes=(("ps", bank),))
                        vslot = bi % 2
                        eng = "act" if bi % 2 == 0 else "dve"
                        if eng == "act":
                            S.op("act", lambda e, bank=bank, nr=nr, vslot=vslot, vs4=vs4: e.copy(
                                out=vtok[vslot][0:nr, vs4 * 512:(vs4 + 1) * 512], in_=k.ps[bank][0:nr, 0:512]),
                                reads=(("ps", bank),), writes=(("sa_vtok", vslot, vs4),))
                        else:
                            S.op("dve", lambda e, bank=bank, nr=nr, vslot=vslot, vs4=vs4: e.tensor_copy(
                                out=vtok[vslot][0:nr, vs4 * 512:(vs4 + 1) * 512], in_=k.ps[bank][0:nr, 0:512]),
                                reads=(("ps", bank),), writes=(("sa_vtok", vslot, vs4),))
                        if bi == 4:
                            S.op("act", lambda e, vslot=vslot, vs4=vs4: e.copy(
                                out=vs[0:NS, vs4 * 512:(vs4 + 1) * 512], in_=vtok[vslot][0:NS, vs4 * 512:(vs4 + 1) * 512]),
                                reads=(("sa_vtok", vslot, vs4),), writes=("sb_vs",))
                        dst = k.new_v_sample[j][:, vs4 * 512:(vs4 + 1) * 512] if bi == 4 else \
                            k.new_v_prompt[j, c0 + b0:c0 + b0 + 128, vs4 * 512:(vs4 + 1) * 512]
                        S.dma("sp", lambda e, dst=dst, nr=nr, vslot=vslot, vs4=vs4: e.dma_start(
                            out=dst, in_=vtok[vslot][0:nr, vs4 * 512:(vs4 + 1) * 512]),
                            reads=(("sa_vtok", vslot, vs4),), writes=(("vout", ti, bi, vs4),))
                S.flush(k.block)
        S.barrier()
        with ExitStack() as bes:
            B_ = lambda name, shape, dt: bes.enter_context(nc.sbuf_tensor(f"{name}_L{L}", shape, dt))
            NSL = 3
            qh = [B_(f"sbq{i}", [128, T], BF16) for i in range(2)]
            kh = [B_(f"sbk{i}", [128, T], BF16) for i in range(2)]
            vh = [B_(f"sbv{i}", [128, T // 128, 128], BF16) for i in range(2)]
            E = [B_(f"sbE{i}", [128, 512], BF16) for i in range(NSL)]
            X = [B_(f"sbX{i}", [128, 512], BF16) for i in range(NSL)]
            W = [B_(f"sbW{i}", [128, 512], BF16) for i in range(NSL)]
            SPR = [B_(f"sbSP{i}", [128, 512], BF16) for i in range(32)]
            ones_b = B_("sb_onesb", [128, 128], BF16)
            mltb = B_("sb_mltb", [128, 128], BF16)
            oT = [B_(f"sboT{i}", [128, 512], BF16) for i in range(2)]
            S.op("dve", lambda e: e.memset(ones_b[:, :], 1.0), writes=("sb_onesb",))
            S.op("dve", lambda e: e.tensor_copy(out=mltb[:, :], in_=mlt[:, :]), reads=("sb_mlt",), writes=("sb_mltb",))
            def load_head(h):
                hs_ = h % 2
                S.dma("sp", lambda e: e.dma_start(out=qh[hs_][:, :], in_=k.QTs[h, :, 0:T]),
                      reads=tuple(("qk_scr", False, h, ti) for ti in range(cfg.ntile)), writes=(("sbq", hs_),))
                S.dma("sp", lambda e: e.dma_start(out=kh[hs_][:, :], in_=k.KTs[h, :, 0:T]),
                      reads=tuple(("qk_scr", True, h, ti) for ti in range(cfg.ntile)), writes=(("sbk", hs_),))
                S.dma("pool", lambda e: e.dma_start(
                    out=vh[hs_][:, :, :], in_=k.new_v_prompt[j, :, h * 128:(h + 1) * 128].rearrange("(b p) d -> p b d", p=128)),
                    writes=(("sbv", hs_),))

            tasks = []
            npair = 0
            nq = 0
            for h in range(NH if "B" not in getattr(cfg, "sb_skip", "") else 0):
                for qt in range(cfg.ntile):
                    obank = 6 + nq % 2
                    osl = nq % 2
                    nq += 1
                    kb_hi = qt * 4 + 3
                    prev = []
                    for i, kb in enumerate(range(kb_hi, -1, -1)):
                        Q0 = qt * 512
                        lo = max(kb * 128, Q0) - Q0
                        t_ = dict(h=h, hs_=h % 2, qt=qt, Q0=Q0, obank=obank, osl=osl, kb=kb, kb_hi=kb_hi, i=i, lo=lo,
                                  diag=(kb * 128 >= Q0), sl=npair % NSL, prev=list(prev),
                                  first_of_head=(qt == 0 and i == 0), prefetch=(qt == 0 and i == 3), spi=(nq % 2) * 16 + i)
                        prev.append((t_["spi"], lo))
                        npair += 1
                        tasks.append(t_)

            def st1(t_):
                h, hs_, kb, lo, Q0, sl = t_["h"], t_["hs_"], t_["kb"], t_["lo"], t_["Q0"], t_["sl"]
                if t_["first_of_head"] and h == 0:
                    load_head(0)
                if t_["prefetch"] and h + 1 < NH:
                    load_head(h + 1)
                zb = sl
                sp = SPR[t_["spi"]]
                sptok = ("sbSP", t_["spi"])
                S.op("pe", lambda e: e.matmul(
                    out=k.ps[zb][:, lo:512], lhsT=kh[hs_][:, kb * 128:(kb + 1) * 128], rhs=qh[hs_][:, Q0 + lo:Q0 + 512],
                    start=True, stop=True), reads=(("sbk", hs_), ("sbq", hs_)), writes=(("ps", zb),))
                S.op("act", lambda e: e.activation(
                    out=E[sl][:, lo:512], in_=k.ps[zb][:, lo:512], func=AF.Exp, scale=SCALE, bias=bcol[:, h:h + 1]),
                    reads=(("ps", zb), "sb_bcol"), writes=(("sbE", sl),))
                S.op("act", lambda e: e.activation(
                    out=sp[:, lo:512], in_=E[sl][:, lo:512], func=AF.Ln, scale=1.0, bias=one_col[:, 0:1]),
                    reads=(("sbE", sl), "sb_one"), writes=(sptok,))
                if t_["diag"]:
                    S.op("pool", lambda e: e.tensor_tensor(
                        out=sp[:, lo:lo + 128], in0=sp[:, lo:lo + 128], in1=mltb[:, :], op=ALU.mult),
                        reads=(sptok, "sb_mltb"), writes=(sptok,))

            def st2(t_):
                lo, sl, prev = t_["lo"], t_["sl"], t_["prev"]
                sbk_ = 3 + sl
                sp = SPR[t_["spi"]]
                sptok = ("sbSP", t_["spi"])
                S.op("pe", lambda e: e.matmul(
                    out=k.ps[sbk_][:, lo:512], lhsT=tri[:, :], rhs=sp[:, lo:512], start=True, stop=(len(prev) == 0),
                    skip_group_check=True), reads=("sb_tri", sptok), writes=(("ps", sbk_),))
                for pi_, (pidx, plo) in enumerate(prev):
                    S.op("pe", lambda e, pidx=pidx, plo=plo, lastp=(pi_ == len(prev) - 1): e.matmul(
                        out=k.ps[sbk_][:, plo:512], lhsT=ones_b[:, :], rhs=SPR[pidx][:, plo:512], start=False, stop=lastp,
                        skip_group_check=True), reads=("sb_onesb", ("sbSP", pidx)), writes=(("ps", sbk_),))
                S.op("act", lambda e: e.activation(
                    out=X[sl][:, lo:512], in_=k.ps[sbk_][:, lo:512], func=AF.Exp, scale=-1.0),
                    reads=(("ps", sbk_),), writes=(("sbX", sl),))
                if t_["diag"]:
                    S.op("pool", lambda e: e.tensor_tensor(
                        out=X[sl][:, lo:lo + 128], in0=X[sl][:, lo:lo + 128], in1=mltb[:, :], op=ALU.mult),
                        reads=(("sbX", sl), "sb_mltb"), writes=(("sbX", sl),))
                S.op("dve", lambda e: e.tensor_tensor(
                    out=W[sl][:, lo:512], in0=E[sl][:, lo:512], in1=X[sl][:, lo:512], op=ALU.mult),
                    reads=(("sbE", sl), ("sbX", sl)), writes=(("sbW", sl),))

            def st3(t_):
                h, hs_, kb, lo, Q0, sl, obank, osl = (t_[x] for x in ("h", "hs_", "kb", "lo", "Q0", "sl", "obank", "osl"))
                S.op("pe", lambda e: e.matmul(
                    out=k.ps[obank][:, lo:512], lhsT=vh[hs_][:, kb, :], rhs=W[sl][:, lo:512],
                    start=(kb == t_["kb_hi"]), stop=(kb == 0), skip_group_check=True),
                    reads=(("sbv", hs_), ("sbW", sl)), writes=(("ps", obank),))
                if kb == 0:
                    S.op("act", lambda e: e.copy(out=oT[osl][:, :], in_=k.ps[obank][:, :]),
                         reads=(("ps", obank),), writes=(("sboT", osl),))
                    S.dma("sp", lambda e: e.dma_start(out=k.OTs[:, h, Q0:Q0 + 512], in_=oT[osl][:, :]),
                          reads=(("sboT", osl),), writes=(("o_scr", h, t_["qt"]),))

            nt_ = len(tasks)
            for n in range(nt_ + 2):
                if n < nt_:
                    st1(tasks[n])
                if 0 <= n - 1 < nt_:
                    st2(tasks[n - 1])
                if 0 <= n - 2 < nt_:
                    st3(tasks[n - 2])
                if n % 16 == 15:
                    S.flush(k.block)
            S.flush(k.block)
        S.barrier()
        with ExitStack() as ces:
            B_ = lambda name, shape, dt: ces.enter_context(nc.sbuf_tensor(f"{name}_L{L}", shape, dt))
            NP = cfg.npages
            ptb = B_("sc_ptb", [128, NP], I32)
            idx = B_("sc_idx", [128, NP], I32)
            iot = B_("sc_iot", [128, 1], I32)
            kp = [B_(f"sc_kp{i}", [128, D], F32) for i in range(2)]
            vp = [B_(f"sc_vp{i}", [128, D], BF16) for i in range(4)]
            ktp = [B_(f"sc_ktp{i}", [128, NH, 128], BF16) for i in range(2)]
            ZB = [B_(f"sc_zb{i}", [128, 128], F32) for i in range(4)]
            Es = [B_(f"sc_E{i}", [128, 128], F32) for i in range(4)]
            SPs = [B_(f"sc_SP{i}", [128, 128], BF16) for i in range(4)]
            Xs = [B_(f"sc_X{i}", [128, 128], F32) for i in range(4)]
            Ws = [B_(f"sc_W{i}", [128, 128], BF16) for i in range(4)]
            ACCs = B_("sc_ACC", [128, 128], F32)
            S.dma("sp", lambda e: e.dma_start(out=ptb[:, :], in_=k.page_table[0:1, :].broadcast_to([128, NP])), writes=("sc_ptb",))
            S.op("pool", lambda e: e.iota(iot[:, :], pattern=[[0, 1]], base=0, channel_multiplier=1), writes=("sc_iot",))
            S.op("pool", lambda e: e.tensor_scalar(out=idx[:, :], in0=ptb[:, :], scalar1=128, scalar2=None, op0=ALU.mult),
                 reads=("sc_ptb",), writes=("sc_idx",))
            S.op("pool", lambda e: e.tensor_tensor(out=idx[:, :], in0=idx[:, :], in1=iot[:, 0:1].broadcast_to([128, NP]), op=ALU.add),
                 reads=("sc_idx", "sc_iot"), writes=("sc_idx",))
            S.op("pool", lambda e: e.memset(ACCs[:, :], 0.0), writes=("sc_ACC",))
            OB = 7
            NC4 = 4
            first_pv = [True]
            ck = k.cache_k[j]
            cv = k.cache_v[j]

            def blk(bi):
                if bi == 0:
                    return dict(nk=NS, new=True, kt=lambda h: ksT[:, h, :], v=lambda h: vs[0:NS, h * 128:(h + 1) * 128],
                                ktoks=("sb_ksT",), vtoks=("sb_vs",), pg=None)
                pg = NP - bi
                ksl, vsl = bi % 2, bi % NC4
                return dict(nk=128, new=False, kt=lambda h: ktp[ksl][:, h, :], v=lambda h: vp[vsl][:, h * 128:(h + 1) * 128],
                            ktoks=(("sc_ktp", ksl),), vtoks=(("sc_vp", vsl),), pg=pg, ksl=ksl, vsl=vsl)

            def c1(bi):
                d_ = blk(bi)
                if d_["new"]:
                    return
                pg, ksl, vsl = d_["pg"], d_["ksl"], d_["vsl"]
                S.dma("pool", lambda e: e.indirect_dma_start(
                    out=kp[ksl][:, :], out_offset=None, in_=ck, in_offset=bass.IndirectOffsetOnAxis(ap=idx[:, pg:pg + 1], axis=0)),
                    reads=("sc_idx",), writes=(("sc_kp", ksl),))
                S.dma("pool", lambda e: e.indirect_dma_start(
                    out=vp[vsl][:, :], out_offset=None, in_=cv, in_offset=bass.IndirectOffsetOnAxis(ap=idx[:, pg:pg + 1], axis=0)),
                    reads=("sc_idx",), writes=(("sc_vp", vsl),))
                for g in range(4):
                    bank = g % 2
                    for cc in range(4):
                        h = g * 4 + cc
                        S.op("pe", lambda e, bank=bank, cc=cc, h=h: e.transpose(
                            out=k.ps[bank][:, cc * 128:(cc + 1) * 128], in_=kp[ksl][:, h * 128:(h + 1) * 128], identity=k.ident[:, :]),
                            reads=(("sc_kp", ksl),), writes=(("ps", bank),))
                    if g % 2 == 0:
                        S.op("act", lambda e, bank=bank, g=g: e.copy(
                            out=ktp[ksl][:, g * 4:(g + 1) * 4, :], in_=k.ps[bank][:, :].rearrange("p (a b) -> p a b", a=4)),
                            reads=(("ps", bank),), writes=(("sc_ktp", ksl),))
                    else:
                        S.op("dve", lambda e, bank=bank, g=g: e.tensor_copy(
                            out=ktp[ksl][:, g * 4:(g + 1) * 4, :], in_=k.ps[bank][:, :].rearrange("p (a b) -> p a b", a=4)),
                            reads=(("ps", bank),), writes=(("sc_ktp", ksl),))

            def c2(bi):
                d_ = blk(bi)
                nk, sl = d_["nk"], bi % NC4
                zbk = 2 + bi % 2
                for h in range(NH):
                    S.op("pe", lambda e, h=h: e.matmul(
                        out=k.ps[zbk][0:nk, h * NS:(h + 1) * NS], lhsT=d_["kt"](h), rhs=qsT[:, h, :], start=True, stop=True),
                        reads=d_["ktoks"] + ("sb_qsT",), writes=(("ps", zbk),))
                S.op("dve", lambda e: e.scalar_tensor_tensor(
                    out=ZB[sl][0:nk, :], in0=k.ps[zbk][0:nk, 0:128], scalar=SCALE, in1=brow[0:nk, :, :].rearrange("p h t -> p (h t)"),
                    op0=ALU.mult, op1=ALU.add), reads=(("ps", zbk), "sb_brow"), writes=(("sc_zb", sl),))
                S.op("act", lambda e: e.activation(out=Es[sl][0:nk, :], in_=ZB[sl][0:nk, :], func=AF.Exp),
                     reads=(("sc_zb", sl),), writes=(("sc_E", sl),))
                S.op("act", lambda e: e.activation(out=SPs[sl][0:nk, :], in_=Es[sl][0:nk, :], func=AF.Ln, scale=1.0, bias=one_col[0:nk, 0:1]),
                     reads=(("sc_E", sl), "sb_one"), writes=(("sc_SP", sl),))
                if d_["new"]:
                    S.op("dve", lambda e: e.tensor_tensor(
                        out=SPs[sl][0:nk, :], in0=SPs[sl][0:nk, :], in1=m8[0:nk, :, :].rearrange("p h t -> p (h t)"), op=ALU.mult),
                        reads=(("sc_SP", sl), "sb_m8"), writes=(("sc_SP", sl),))

            def c3(bi):
                d_ = blk(bi)
                nk, sl = d_["nk"], bi % NC4
                sbk_ = 4 + bi % 2
                S.op("pe", lambda e: e.matmul(
                    out=k.ps[sbk_][0:nk, 0:128], lhsT=tri[0:nk, 0:nk], rhs=SPs[sl][0:nk, :], start=True, stop=False),
                    reads=("sb_tri", ("sc_SP", sl)), writes=(("ps", sbk_),))
                S.op("pe", lambda e: e.matmul(
                    out=k.ps[sbk_][0:nk, 0:128], lhsT=k.ones_f[:, 0:nk], rhs=ACCs[:, :], start=False, stop=True),
                    reads=("ones_f", "sc_ACC"), writes=(("ps", sbk_),))
                S.op("pool", lambda e: e.tensor_tensor(out=ACCs[0:nk, :], in0=ACCs[0:nk, :], in1=SPs[sl][0:nk, :], op=ALU.add),
                     reads=("sc_ACC", ("sc_SP", sl)), writes=("sc_ACC",))
                S.op("act", lambda e: e.activation(out=Xs[sl][0:nk, :], in_=k.ps[sbk_][0:nk, 0:128], func=AF.Exp, scale=-1.0),
                     reads=(("ps", sbk_),), writes=(("sc_X", sl),))
                if d_["new"]:
                    S.op("dve", lambda e: e.tensor_tensor(
                        out=Xs[sl][0:nk, :], in0=Xs[sl][0:nk, :], in1=m8[0:nk, :, :].rearrange("p h t -> p (h t)"), op=ALU.mult),
                        reads=(("sc_X", sl), "sb_m8"), writes=(("sc_X", sl),))
                S.op("dve", lambda e: e.tensor_tensor(out=Ws[sl][0:nk, :], in0=Es[sl][0:nk, :], in1=Xs[sl][0:nk, :], op=ALU.mult),
                     reads=(("sc_E", sl), ("sc_X", sl)), writes=(("sc_W", sl),))

            def c4(bi, final):
                d_ = blk(bi)
                nk, sl = d_["nk"], bi % NC4
                for h in range(NH):
                    st = first_pv[0] and h == 0
                    S.op("pe", lambda e, h=h, st=st: e.matmul(
                        out=k.ps[OB][:, h * NS:(h + 1) * NS], lhsT=d_["v"](h), rhs=Ws[sl][0:nk, h * NS:(h + 1) * NS],
                        start=st, stop=(final and h == NH - 1), skip_group_check=True),
                        reads=d_["vtoks"] + (("sc_W", sl),), writes=(("ps", OB),))
                first_pv[0] = False

            NB = NP + 1
            for n in range(NB + 3):
                if n < NB:
                    c1(n)
                if 0 <= n - 1 < NB:
                    c2(n - 1)
                if 0 <= n - 2 < NB:
                    c3(n - 2)
                if 0 <= n - 3 < NB:
                    c4(n - 3, n - 3 == NB - 1)
                if n % 8 == 7:
                    S.flush(k.block)
            S.op("act", lambda e: e.copy(out=osT[:, :, :], in_=k.ps[OB][:, 0:128].rearrange("p (h t) -> p h t", h=NH)),
                 reads=(("ps", OB),), writes=("sb_osT",))
            S.flush(k.block)
        S.barrier()
        with ExitStack() as des:
            B_ = lambda name, shape, dt: des.enter_context(nc.sbuf_tensor(f"{name}_L{L}", shape, dt))
            ot = B_("sd_o", [128, NH, 520], BF16)
            wo = [B_(f"sd_wo{i}", [128, NCH, 256], BF16) for i in range(2)]
            xr = [B_(f"sd_xr{i}", [128, 520], F32) for i in range(2)]
            nw = 0
            for ti in range(cfg.ntile if "D" not in getattr(cfg, "sb_skip", "") else 0):
                c0, ncols = tile_cols(cfg, ti)
                segs = mm_cols(ncols)
                last = ncols > 512
                S.dma("sp", lambda e, c0=c0: e.dma_start(out=ot[:, :, 0:512], in_=k.OTs[:, :, c0:c0 + 512]),
                      reads=tuple(("o_scr", h, ti) for h in range(NH)), writes=("sd_o",))
                if last:
                    S.op("dve", lambda e: e.tensor_copy(out=ot[:, :, 512:520], in_=osT[:, :, :]), reads=("sb_osT",), writes=("sd_o",))
                for oc2 in range(NCH // 2):
                    slot = nw % 2
                    nw += 1
                    S.dma("pool", lambda e, slot=slot, oc2=oc2: e.dma_start(out=wo[slot][:, :, :], in_=Wo[:, :, oc2 * 256:(oc2 + 1) * 256]),
                          writes=(("sd_wo", slot),))
                    for half in range(2):
                        oc = oc2 * 2 + half
                        par = oc % 2
                        S.dma("sp", lambda e, par=par, oc=oc, c0=c0, n=ncols: e.dma_start(out=xr[par][:, 0:n], in_=k.XT[:, oc, c0:c0 + n]),
                              reads=(("XT", ti),), writes=(("sd_xr", par),))
                        for kc in range(NCH):
                            for (a, b, smp) in segs:
                                o = k.ps[SBK[par]][:, 0:NS] if smp else k.ps[par][:, 0:512]
                                tok = ("ps", SBK[par]) if smp else ("ps", par)
                                S.op("pe", lambda e, o=o, slot=slot, kc=kc, half=half, a=a, b=b: e.matmul(
                                    out=o, lhsT=wo[slot][:, kc, half * 128:(half + 1) * 128], rhs=ot[:, kc, a:b],
                                    start=(kc == 0), stop=(kc == NCH - 1)),
                                    reads=(("sd_wo", slot), "sd_o"), writes=(tok,))
                        for (a, b, smp) in segs:
                            o = k.ps[SBK[par]][:, 0:NS] if smp else k.ps[par][:, 0:512]
                            tok = ("ps", SBK[par]) if smp else ("ps", par)
                            S.op("dve", lambda e, o=o, par=par, a=a, b=b: e.tensor_tensor(
                                out=xr[par][:, a:b], in0=xr[par][:, a:b], in1=o, op=ALU.add),
                                reads=(tok, ("sd_xr", par)), writes=(("sd_xr", par),))
                        S.dma("sp", lambda e, par=par, oc=oc, c0=c0, n=ncols: e.dma_start(out=k.XT[:, oc, c0:c0 + n], in_=xr[par][:, 0:n]),
                              reads=(("sd_xr", par),), writes=(("XT", ti),))
                S.flush(k.block)
    S.barrier()


def phase_transpose_out(k):
    nc, S, cfg = k.nc, k.S, k.cfg
    with ExitStack() as es:
        xi = [es.enter_context(nc.sbuf_tensor(f"to_in{i}", [128, NCH, 520], F32)) for i in range(2)]
        xo = [es.enter_context(nc.sbuf_tensor(f"to_out{i}", [128, D], F32)) for i in range(2)]
        nb = 0
        for ti in range(cfg.ntile):
            c0, ncols = tile_cols(cfg, ti)
            xin = xi[ti % 2]
            itok = ("to_in", ti % 2)
            S.dma("sp", lambda e, xin=xin, c0=c0, n=ncols: e.dma_start(out=xin[:, :, 0:n], in_=k.XT[:, :, c0:c0 + n]),
                  reads=(("XT", ti),), writes=(itok,))
            blocks = [(c0 + 128 * j, 128, 128 * j, False) for j in range(4)]
            if ncols > 512:
                blocks.append((0, NS, 512, True))
            for (r0, nr, ic0, smp) in blocks:
                slot = nb % 2
                nb += 1
                otok = ("to_out", slot)
                for g in range(4):
                    bank = g
                    for cc in range(4):
                        c = g * 4 + cc
                        S.op("pe", lambda e, b=bank, cc=cc, c=c, nr=nr, ic0=ic0, xin=xin: e.transpose(
                            out=k.ps[b][0:nr, cc * 128:(cc + 1) * 128], in_=xin[:, c, ic0:ic0 + nr],
                            identity=k.ident[:, :]),
                            reads=(itok,), writes=(("ps", bank),))
                    eng = "act" if g % 2 == 0 else "dve"
                    src_ap = k.ps[bank][0:nr, :]
                    dst_ap = xo[slot][0:nr, g * 512:(g + 1) * 512]
                    if eng == "act":
                        S.op("act", lambda e, s=src_ap, d=dst_ap: e.copy(out=d, in_=s),
                             reads=(("ps", bank),), writes=(otok,))
                    else:
                        S.op("dve", lambda e, s=src_ap, d=dst_ap: e.tensor_copy(out=d, in_=s),
                             reads=(("ps", bank),), writes=(otok,))
                dst = k.y_sample[0:NS, :] if smp else k.y_prompt[r0:r0 + nr, :]
                S.dma("sp", lambda e, d=dst, slot=slot, nr=nr: e.dma_start(out=d, in_=xo[slot][0:nr, :]),
                      reads=(otok,), writes=(("yout", nb),))
        S.flush(k.block)
    S.barrier()


def build(cfg):
    nc = bass.Bass("TRN2", target_bir_lowering=False)
    k = K()
    k.nc, k.cfg = nc, cfg
    T = cfg.T
    dram = lambda name, shape, dt, kind: nc.dram_tensor(name, list(shape), dt, kind=kind).ap()
    k.x_prompt = dram("x_prompt", [T, D], F32, "ExternalInput")
    k.x_sample = dram("x_sample", [NS, D], F32, "ExternalInput")
    k.norm_mix = dram("norm_mix", [len(cfg.kinds), D], F32, "ExternalInput")
    k.norm_ffn = dram("norm_ffn", [len(cfg.kinds), D], F32, "ExternalInput")
    k.w_gate = dram("w_ffn_gate", [len(cfg.kinds), D, DFF], F32, "ExternalInput")
    k.w_up = dram("w_ffn_up", [len(cfg.kinds), D, DFF], F32, "ExternalInput")
    k.w_down = dram("w_ffn_down", [len(cfg.kinds), DFF, D], F32, "ExternalInput")
    k.c_ident = dram("c_ident", [128, 128], F32, "ExternalInput")
    k.c_invc = dram("c_invc", [128, NCH, PBUF], F32, "ExternalInput")
    npl = max(cfg.n_pool, 1)
    k.cache_pool = dram("cache_pool", [npl, PBUF, D], F32, "ExternalInput")
    k.w_pool = dram("w_pool", [npl, 4, 512, 512], F32, "ExternalInput")
    k.pool_scale = dram("pool_scale", [npl, D], F32, "ExternalInput")
    nss = max(cfg.n_ssm, 1)
    k.state_re = dram("state_ssm_re", [nss, 128, 64], F32, "ExternalInput")
    k.state_im = dram("state_ssm_im", [nss, 128, 64], F32, "ExternalInput")
    k.ssm_a_re = dram("ssm_a_re", [nss, 128, 64], F32, "ExternalInput")
    k.ssm_a_im = dram("ssm_a_im", [nss, 128, 64], F32, "ExternalInput")
    k.ssm_log_dt = dram("ssm_log_dt", [nss, 128], F32, "ExternalInput")
    k.ssm_d = dram("ssm_d", [nss, D], F32, "ExternalInput")
    k.b_re_w = dram("b_re_w", [nss, 128, NCH, 128], F32, "ExternalInput")
    k.b_im_w = dram("b_im_w", [nss, 128, NCH, 128], F32, "ExternalInput")
    k.c_re_w = dram("c_re_w", [nss, 128, 64, 64], F32, "ExternalInput")
    k.c_im_w = dram("c_im_w", [nss, 128, 64, 64], F32, "ExternalInput")
    k.c_msk = dram("c_msk", [128, 2], F32, "ExternalInput")
    k.c_sel = dram("c_sel", [64, NCH, 128], F32, "ExternalInput")
    k.w_glu_a = dram("w_glu_a", [nss, D, D], F32, "ExternalInput")
    k.w_glu_b = dram("w_glu_b", [nss, D, D], F32, "ExternalInput")
    k.new_ssm_re_prompt = dram("new_ssm_re_prompt", [nss, 128, 64], F32, "ExternalOutput")
    k.new_ssm_im_prompt = dram("new_ssm_im_prompt", [nss, 128, 64], F32, "ExternalOutput")
    k.new_ssm_re_sample = dram("new_ssm_re_sample", [nss, 128, 64], F32, "ExternalOutput")
    k.new_ssm_im_sample = dram("new_ssm_im_sample", [nss, 128, 64], F32, "ExternalOutput")
    nsb = max(cfg.n_sb, 1)
    if cfg.n_sb > 0:
        k.cache_k = dram("cache_k", [nsb, cfg.nphys * 128, D], F32, "ExternalInput")
        k.cache_v = dram("cache_v", [nsb, cfg.nphys * 128, D], F32, "ExternalInput")
        k.page_table = dram("page_table", [1, max(cfg.npages, 1)], I32, "ExternalInput")
        k.w_qkv = dram("w_qkv", [nsb, D, 3 * D], F32, "ExternalInput")
        k.w_o = dram("w_o", [nsb, D, D], F32, "ExternalInput")
        k.sb_q_norm = dram("sb_q_norm", [nsb, 128], F32, "ExternalInput")
        k.sb_k_norm = dram("sb_k_norm", [nsb, 128], F32, "ExternalInput")
        k.sb_bias = dram("sb_bias", [nsb, 16], F32, "ExternalInput")
        k.c_tri = dram("c_tri", [128, 128], F32, "ExternalInput")
        k.c_mlt = dram("c_mlt", [128, 128], F32, "ExternalInput")
        k.new_k_prompt = dram("new_k_prompt", [nsb, T, D], F32, "ExternalOutput")
        k.new_v_prompt = dram("new_v_prompt", [nsb, T, D], F32, "ExternalOutput")
        k.new_k_sample = dram("new_k_sample", [nsb, NS, D], F32, "ExternalOutput")
        k.new_v_sample = dram("new_v_sample", [nsb, NS, D], F32, "ExternalOutput")
        k.QTs = dram("scr_qt", [16, 128, cfg.TT], BF16, "Internal")
        k.KTs = dram("scr_kt", [16, 128, cfg.TT], BF16, "Internal")
        k.OTs = dram("scr_ot", [128, 16, cfg.TT], BF16, "Internal")
    k.new_pool_prompt = dram("new_pool_prompt", [npl, PBUF, D], F32, "ExternalOutput")
    k.new_pool_sample = dram("new_pool_sample", [npl, PBUF, D], F32, "ExternalOutput")
    k.y_prompt = dram("y_prompt", [T, D], F32, "ExternalOutput")
    k.y_sample = dram("y_sample", [NS, D], F32, "ExternalOutput")
    k.XT = dram("scr_xt", [128, NCH, cfg.TT], F32, "Internal")

    with ExitStack() as es:
        S = Sched(nc, es)
        k.S = S
        A = lambda name, shape, dt: es.enter_context(nc.sbuf_tensor(name, shape, dt))
        k.ident = A("ident", [128, 128], F32)
        k.ones_f = A("ones_f", [128, 128], F32)
        k.eps_col = A("eps_col", [128, 1], F32)
        nl = len(cfg.kinds)
        k.g_mix = A("g_mix", [128, nl * NCH], F32)
        k.g_ffn = A("g_ffn", [128, nl * NCH], F32)
        k.invc = A("invc", [128, NCH, PBUF], F32)
        k.pscale = A("pscale", [128, npl * NCH], F32)
        k.ps = [es.enter_context(nc.psum_tensor(f"psb{i}", [128, 512], F32)) for i in range(8)]
        with nc.Block() as block:
            k.block = block
            S.dma("sp", lambda e: e.dma_start(out=k.ident[:, :], in_=k.c_ident), writes=("ident",))
            S.op("dve", lambda e: e.memset(k.ones_f[:, :], 1.0), writes=("ones_f",))
            S.op("dve", lambda e: e.memset(k.eps_col[:, :], EPS), writes=("eps",))
            with nc.allow_non_contiguous_dma(reason="tiny gain vectors"):
                S.dma("sp", lambda e: e.dma_start(
                    out=k.g_mix[:, :].rearrange("p (l c) -> p l c", l=nl),
                    in_=k.norm_mix.rearrange("l (c p) -> p l c", p=128)), writes=("g_mix",))
                S.dma("sp", lambda e: e.dma_start(
                    out=k.g_ffn[:, :].rearrange("p (l c) -> p l c", l=nl),
                    in_=k.norm_ffn.rearrange("l (c p) -> p l c", p=128)), writes=("g_ffn",))
                S.dma("sp", lambda e: e.dma_start(
                    out=k.pscale[:, :].rearrange("p (l c) -> p l c", l=npl),
                    in_=k.pool_scale.rearrange("l (c p) -> p l c", p=128)), writes=("pscale",))
                S.dma("sp", lambda e: e.dma_start(out=k.invc[:, :, :], in_=k.c_invc), writes=("invc",))
                S.flush(block)
            S.barrier()
            phase_transpose_in(k)
            cnt = [0, 0, 0]
            for L, kind in enumerate(cfg.kinds):
                if kind == 0:
                    phase_pool(k, L, cnt[0])
                if kind == 1:
                    phase_s5(k, L, cnt[1])
                if kind == 2:
                    phase_sb(k, L, cnt[2])
                if kind in (0, 1, 2):
                    cnt[kind] += 1
                if not cfg.skip_ffn:
                    phase_ffn(k, L)
            phase_transpose_out(k)
            S.barrier()
            S.flush(block)
    k.ninst = dict(S.ninst)
    return nc, k


def _consts():
    invc = np.zeros((128, NCH, PBUF), np.float32)
    for c in range(NCH):
        w = 2 << (c // 4)
        for t in range(PBUF):
            invc[:, c, t] = 1.0 / min(t + 1, w)
    sel = np.zeros((64, NCH, 128), np.float32)
    for jj in range(NCH):
        for p in range(128):
            sel[4 * jj + p // 32, jj, p] = 1.0
    msk = np.zeros((128, 2), np.float32)
    for p in range(128):
        msk[p, (p // 32) % 2] = 1.0
    s_ = np.arange(128)
    tri = (s_[:, None] >= s_[None, :]).astype(np.float32)
    mlt = (s_[:, None] < s_[None, :]).astype(np.float32)
    return {"c_ident": np.eye(128, dtype=np.float32), "c_invc": invc, "c_sel": sel, "c_msk": msk,
            "c_tri": tri, "c_mlt": mlt}


def _s5_layouts(b_re, b_im, c_re, c_im):
    Ls = b_re.shape[0]
    out = {}
    for name, b in (("b_re_w", b_re), ("b_im_w", b_im)):
        w = np.zeros((Ls, 128, NCH, 2, 64), np.float32)
        for p in range(128):
            q, parp, cc = p // 32, (p // 16) % 2, p % 16
            for jj in range(NCH):
                g = 8 * jj + 2 * q + parp
                w[:, p, jj, parp, :] = b[:, g, :, cc]
        out[name] = w.reshape(Ls, 128, NCH, 128)
    for name, c in (("c_re_w", c_re), ("c_im_w", c_im)):
        w = np.zeros((Ls, 2, 64, 64, 2, 2, 16), np.float32)
        for par in range(2):
            for gp in range(64):
                g = 2 * gp + par
                w[:, par, :, gp, gp % 2, par, :] = np.transpose(c[:, g, :, :], (0, 2, 1))
        out[name] = w.reshape(Ls, 128, 64, 64)
    return out


_BUILD_CACHE = {}


def kernel(x_prompt, x_sample, cache_pool, state_ssm_re, state_ssm_im, cache_k, cache_v, page_table,
           norm_mix, norm_ffn, w_ffn_gate, w_ffn_up, w_ffn_down, w_pool, pool_scale,
           ssm_a_re, ssm_a_im, ssm_b_re, ssm_b_im, ssm_c_re, ssm_c_im, ssm_d, ssm_log_dt,
           w_glu_a, w_glu_b, w_qkv, w_o, sb_q_norm, sb_k_norm, sb_bias):
    f32 = lambda a: np.ascontiguousarray(np.asarray(a), dtype=np.float32)
    x_prompt, x_sample = f32(x_prompt), f32(x_sample)
    B, T, _ = x_prompt.shape
    Bs = x_sample.shape[0]
    nphys = np.asarray(cache_k).shape[1]
    npages = np.asarray(page_table).shape[1]
    depth = np.asarray(norm_mix).shape[0]
    kinds = tuple(i % 3 for i in range(depth))
    cfg = Cfg(T=T, npages=npages, nphys=nphys, kinds=kinds)
    key = (T, npages, nphys, kinds)
    if key not in _BUILD_CACHE:
        _BUILD_CACHE[key] = build(cfg)
    nc, kk = _BUILD_CACHE[key]
    ncores = 8
    shared = {
        "norm_mix": f32(norm_mix), "norm_ffn": f32(norm_ffn),
        "w_ffn_gate": f32(w_ffn_gate), "w_ffn_up": f32(w_ffn_up), "w_ffn_down": f32(w_ffn_down),
        "w_pool": f32(w_pool), "pool_scale": f32(pool_scale),
        "ssm_a_re": f32(ssm_a_re), "ssm_a_im": f32(ssm_a_im), "ssm_d": f32(ssm_d), "ssm_log_dt": f32(ssm_log_dt),
        "w_glu_a": f32(w_glu_a), "w_glu_b": f32(w_glu_b), "w_qkv": f32(w_qkv), "w_o": f32(w_o),
        "sb_q_norm": f32(sb_q_norm), "sb_k_norm": f32(sb_k_norm), "sb_bias": f32(sb_bias),
        "cache_k": f32(cache_k).reshape(cfg.n_sb, nphys * 128, D),
        "cache_v": f32(cache_v).reshape(cfg.n_sb, nphys * 128, D),
    }
    shared.update(_consts())
    shared.update(_s5_layouts(f32(ssm_b_re), f32(ssm_b_im), f32(ssm_c_re), f32(ssm_c_im)))
    cache_pool, state_ssm_re, state_ssm_im = f32(cache_pool), f32(state_ssm_re), f32(state_ssm_im)
    pt = np.ascontiguousarray(np.asarray(page_table), dtype=np.int32)
    in_maps = []
    for c in range(ncores):
        b, s = c % B, c % Bs
        m = dict(shared)
        m["x_prompt"] = x_prompt[b]
        m["x_sample"] = x_sample[s]
        m["cache_pool"] = np.ascontiguousarray(cache_pool[:, s])
        m["state_ssm_re"] = np.ascontiguousarray(state_ssm_re[:, s])
        m["state_ssm_im"] = np.ascontiguousarray(state_ssm_im[:, s])
        m["page_table"] = pt[s:s + 1]
        in_maps.append(m)
    res = run_bass_kernel_spmd(nc, in_maps, core_ids=list(range(ncores)))
    R = res.results
    H, Dh = 16, 128
    pc = list(range(B))
    y_prompt = np.stack([R[c]["y_prompt"] for c in pc], 0)
    y_sample = np.stack([R[c]["y_sample"] for c in range(Bs)], 0)
    npp = np.stack([R[c]["new_pool_prompt"] for c in pc], 1)
    nps = np.stack([R[c]["new_pool_sample"] for c in range(Bs)], 1)
    srp = np.stack([R[c]["new_ssm_re_prompt"] for c in pc], 1)
    sip = np.stack([R[c]["new_ssm_im_prompt"] for c in pc], 1)
    srs = np.stack([R[c]["new_ssm_re_sample"] for c in range(Bs)], 1)
    sis = np.stack([R[c]["new_ssm_im_sample"] for c in range(Bs)], 1)
    kp = np.stack([R[c]["new_k_prompt"] for c in pc], 1).reshape(cfg.n_sb, B, T, H, Dh)
    vp = np.stack([R[c]["new_v_prompt"] for c in pc], 1).reshape(cfg.n_sb, B, T, H, Dh)
    ks = np.stack([R[c]["new_k_sample"] for c in range(Bs)], 1).reshape(cfg.n_sb, Bs, NS, H, Dh)
    vs = np.stack([R[c]["new_v_sample"] for c in range(Bs)], 1).reshape(cfg.n_sb, Bs, NS, H, Dh)
    return (y_prompt, y_sample, npp, nps, srp, sip, srs, sis, kp, vp, ks, vs)
```

```python
from contextlib import ExitStack
import math
import numpy as np
import concourse.bass as bass
import concourse.mybir as mybir
from concourse.bass_utils import run_bass_kernel_spmd

F32 = mybir.dt.float32
BF16 = mybir.dt.bfloat16
I32 = mybir.dt.int32
AF = mybir.ActivationFunctionType
ALU = mybir.AluOpType

D = 2048
NCH = 16
DFF = 5632
NFC = 44
NS = 8
PBUF = 15
EPS = 1e-6
SEM_EPOCH = 30000


class Sched:
    ENGS = ("pe", "act", "dve", "pool", "sp")

    def __init__(self, nc, es, n_dma=24, n_spare=16):
        self.nc = nc
        self.eobj = {"pe": nc.tensor, "act": nc.scalar, "dve": nc.vector, "pool": nc.gpsimd, "sp": nc.sync}
        self.q = {e: [] for e in self.ENGS}
        self.spare = [es.enter_context(nc.semaphore(f"esem{i}")) for i in range(n_spare)]
        self.esem = {}
        self.epoch = {e: 0 for e in self.ENGS}
        self.cnt = {e: 0 for e in self.ENGS}
        for e in ("pe", "act", "dve", "pool"):
            self.esem[(e, 0)] = self.spare.pop()
        self.dsem = [es.enter_context(nc.semaphore(f"dsem{i}")) for i in range(2 * n_dma)]
        self.dcnt = [0] * (2 * n_dma)
        self.dpool = {"sp": list(range(0, n_dma)), "pool": list(range(n_dma, 2 * n_dma))}
        self.dnext = {"sp": 0, "pool": 0}
        self.waited = {e: {} for e in self.ENGS}
        self.lastw = {}
        self.readers = {}
        self.ninst = {e: 0 for e in self.ENGS}

    def _sem(self, key):
        if key[0] == "d":
            return self.dsem[key[1]]
        return self.esem[(key[1], key[2])]

    def _deps(self, reads, writes):
        deps = {}
        def add(d):
            if d is None:
                return
            k, v = d
            if deps.get(k, 0) < v:
                deps[k] = v
        for r in reads:
            add(self.lastw.get(r))
        for w in writes:
            add(self.lastw.get(w))
            for k, v in self.readers.get(w, {}).items():
                add((k, v))
        return deps

    def _emit_waits(self, eng, deps):
        for k, v in deps.items():
            if eng == "pe" and k[0] == "e" and k[1] == "pe":
                continue
            if self.waited[eng].get(k, 0) >= v:
                continue
            self.waited[eng][k] = v
            sem = self._sem(k)
            self.q[eng].append(lambda e, s=sem, vv=v: e.wait_ge(s, vv))
            self.ninst[eng] += 1

    def _record(self, me, reads, writes):
        k, v = me
        for r in reads:
            self.readers.setdefault(r, {})[k] = v
        for w in writes:
            self.lastw[w] = me
            self.readers[w] = {}

    def op(self, eng, fn, reads=(), writes=()):
        self._emit_waits(eng, self._deps(reads, writes))
        if self.cnt[eng] >= SEM_EPOCH:
            self.epoch[eng] += 1
            self.cnt[eng] = 0
            self.esem[(eng, self.epoch[eng])] = self.spare.pop()
        self.cnt[eng] += 1
        key = ("e", eng, self.epoch[eng])
        sem = self.esem[(eng, self.epoch[eng])]
        self.q[eng].append(lambda e, f=fn, s=sem: f(e).then_inc(s, 1))
        self.ninst[eng] += 1
        self._record((key, self.cnt[eng]), reads, writes)

    def dma(self, eng, fn, reads=(), writes=()):
        pool_ = self.dpool[eng]
        i = pool_[self.dnext[eng]]
        self.dnext[eng] = (self.dnext[eng] + 1) % len(pool_)
        deps = self._deps(reads, writes)
        if self.dcnt[i] > 0:
            k = ("d", i)
            if deps.get(k, 0) < self.dcnt[i]:
                deps[k] = self.dcnt[i]
        self._emit_waits(eng, deps)
        self.dcnt[i] += 16
        sem = self.dsem[i]
        self.q[eng].append(lambda e, f=fn, s=sem: f(e).then_inc(s, 16))
        self.ninst[eng] += 1
        self._record((("d", i), self.dcnt[i]), reads, writes)

    def barrier(self):
        for eng in self.ENGS:
            deps = {}
            for e2 in ("pe", "act", "dve", "pool"):
                if self.cnt[e2] > 0:
                    deps[("e", e2, self.epoch[e2])] = self.cnt[e2]
            for i, c in enumerate(self.dcnt):
                if c > 0:
                    deps[("d", i)] = c
            if eng == "pe":
                pass
            self._emit_waits(eng, deps)

    def flush(self, block):
        reg = {"pe": block.tensor, "act": block.scalar, "dve": block.vector, "pool": block.gpsimd, "sp": block.sync}
        for eng in self.ENGS:
            items = self.q[eng]
            self.q[eng] = []
            if not items:
                continue
            def body(e, items=items):
                for f in items:
                    f(e)
            reg[eng](body)


class Cfg:
    def __init__(self, T=2048, npages=128, nphys=1280, kinds=(0, 1, 2, 0)):
        self.T = T
        self.npages = npages
        self.nphys = nphys
        self.kinds = tuple(kinds)
        self.TT = T + NS
        self.ntile = T // 512
        self.n_pool = self.kinds.count(0)
        self.n_ssm = self.kinds.count(1)
        self.n_sb = self.kinds.count(2)
        self.skip_ffn = False


class K:
    pass


def tile_cols(cfg, i):
    n = 512 + (NS if i == cfg.ntile - 1 else 0)
    return 512 * i, n


def mm_cols(n):
    segs = [(0, 512, False)]
    if n > 512:
        segs.append((512, n, True))
    return segs


def phase_transpose_in(k):
    nc, S, cfg = k.nc, k.S, k.cfg
    with ExitStack() as es:
        xin = [es.enter_context(nc.sbuf_tensor(f"ti_in{i}", [128, D], F32)) for i in range(2)]
        xo = [es.enter_context(nc.sbuf_tensor(f"ti_out{i}", [128, NCH, 520], F32)) for i in range(2)]
        nblk = cfg.T // 128
        for ti in range(cfg.ntile):
            c0, ncols = tile_cols(cfg, ti)
            o = xo[ti % 2]
            otok = ("ti_out", ti % 2)
            blocks = [(c0 + 128 * j, 128, 128 * j, False) for j in range(4)]
            if ncols > 512:
                blocks.append((0, NS, 512, True))
            for bi, (r0, nr, oc0, smp) in enumerate(blocks):
                slot = (ti * 5 + bi) % 2
                src = k.x_sample[0:NS, :] if smp else k.x_prompt[r0:r0 + nr, :]
                S.dma("sp", lambda e, s=src, d=xin[slot], nr=nr: e.dma_start(out=d[0:nr, :], in_=s),
                      reads=(), writes=(("ti_in", slot),))
                for g in range(4):
                    bank = g
                    for cc in range(4):
                        c = g * 4 + cc
                        S.op("pe", lambda e, b=bank, cc=cc, c=c, slot=slot, nr=nr:
                             e.transpose(out=k.ps[b][:, cc * 128:cc * 128 + nr],
                                         in_=xin[slot][0:nr, c * 128:(c + 1) * 128],
                                         identity=k.ident[0:nr, 0:nr]),
                             reads=(("ti_in", slot),), writes=(("ps", bank),))
                    eng = "act" if g % 2 == 0 else "dve"
                    src_ap = k.ps[bank][:, :].rearrange("p (c t) -> p c t", c=4)[:, :, 0:nr]
                    dst_ap = o[:, g * 4:(g + 1) * 4, oc0:oc0 + nr]
                    if eng == "act":
                        S.op("act", lambda e, s=src_ap, d=dst_ap: e.copy(out=d, in_=s),
                             reads=(("ps", bank),), writes=(otok,))
                    else:
                        S.op("dve", lambda e, s=src_ap, d=dst_ap: e.tensor_copy(out=d, in_=s),
                             reads=(("ps", bank),), writes=(otok,))
            S.dma("sp", lambda e, o=o, c0=c0, n=ncols: e.dma_start(
                out=k.XT[:, :, c0:c0 + n], in_=o[:, :, 0:n]),
                reads=(otok,), writes=(("XT", ti),))
        S.flush(k.block)
    S.barrier()


def rmsnorm_tile(k, x, xtok, ncols, gcol, out_h, htok, sq, bank_ss, smp_cols, rstd, out_dtype_note=None):
    S = k.S
    segs = mm_cols(ncols)
    for c in range(NCH):
        s = sq[c % 2]
        stok = ("sq", id(sq), c % 2)
        S.op("act", lambda e, s=s, c=c: e.activation(out=s[:, 0:ncols], in_=x[:, c, 0:ncols], func=AF.Square),
             reads=(xtok,), writes=(stok,))
        for (a, b, smp) in segs:
            if smp:
                bk, c0 = smp_cols
                out_ap = k.ps[bk][:, c0:c0 + (b - a)]
                wt = ("ps", bk)
            else:
                out_ap = k.ps[bank_ss][:, 0:512]
                wt = ("ps", bank_ss)
            S.op("pe", lambda e, o=out_ap, s=s, a=a, b=b, c=c: e.matmul(
                out=o, lhsT=k.ones_f[:, :], rhs=s[:, a:b], start=(c == 0), stop=(c == NCH - 1)),
                reads=(stok,), writes=(wt,))
    rtok = ("rstd", id(rstd))
    for (a, b, smp) in segs:
        if smp:
            bk, c0 = smp_cols
            in_ap = k.ps[bk][:, c0:c0 + (b - a)]
            rt = ("ps", bk)
        else:
            in_ap = k.ps[bank_ss][:, 0:512]
            rt = ("ps", bank_ss)
        S.op("act", lambda e, i=in_ap, a=a, b=b: e.activation(
            out=rstd[:, a:b], in_=i, func=AF.Sqrt, scale=1.0 / D, bias=k.eps_col[:, 0:1]),
            reads=(rt,), writes=(rtok,))
    S.op("dve", lambda e: e.reciprocal(out=rstd[:, 0:ncols], in_=rstd[:, 0:ncols]),
         reads=(rtok,), writes=(rtok,))
    for c in range(NCH):
        S.op("dve", lambda e, c=c: e.scalar_tensor_tensor(
            out=out_h[:, c, 0:ncols], in0=x[:, c, 0:ncols], scalar=gcol[:, c:c + 1], in1=rstd[:, 0:ncols],
            op0=ALU.mult, op1=ALU.mult),
            reads=(xtok, rtok), writes=(htok,))


def phase_ffn(k, L):
    nc, S, cfg = k.nc, k.S, k.cfg
    Wg = k.w_gate[L].rearrange("(kc p) f -> p kc f", p=128)
    Wu = k.w_up[L].rearrange("(kc p) f -> p kc f", p=128)
    Wd = k.w_down[L].rearrange("(fc p) d -> p fc d", p=128)
    with ExitStack() as es:
        A = lambda name, shape, dt: es.enter_context(nc.sbuf_tensor(f"{name}_L{L}", shape, dt))
        xt = A("ffn_x", [128, NCH, 520], F32)
        ht = A("ffn_h", [128, NCH, 520], BF16)
        at = A("ffn_a", [128, NFC, 520], BF16)
        wg = [A(f"ffn_wg{i}", [128, NCH, 256], BF16) for i in range(2)]
        wu = [A(f"ffn_wu{i}", [128, NCH, 256], BF16) for i in range(2)]
        wd = [A(f"ffn_wd{i}", [128, NFC, 256], BF16) for i in range(2)]
        sq = [A(f"ffn_sq{i}", [128, 520], F32) for i in range(2)]
        rstd = A("ffn_rstd", [128, 520], F32)
        sg = [A(f"ffn_sg{i}", [128, 520], F32) for i in range(2)]
        gcol = k.g_ffn[:, L * NCH:(L + 1) * NCH]
        SBK = (4, 5)
        nw = 0
        nd = 0
        for ti in range(cfg.ntile):
            c0, ncols = tile_cols(cfg, ti)
            segs = mm_cols(ncols)
            S.dma("sp", lambda e, c0=c0, n=ncols: e.dma_start(out=xt[:, :, 0:n], in_=k.XT[:, :, c0:c0 + n]),
                  reads=(("XT", ti),), writes=("ffn_x",))
            rmsnorm_tile(k, xt, "ffn_x", ncols, gcol, ht, "ffn_h", sq, 6, (7, 0), rstd)
            for fc2 in range(NFC // 2):
                slot = nw % 2
                nw += 1
                S.dma("pool", lambda e, slot=slot, fc2=fc2: e.dma_start(
                    out=wg[slot][:, :, :], in_=Wg[:, :, fc2 * 256:(fc2 + 1) * 256]),
                    reads=(), writes=(("wg", slot),))
                S.dma("pool", lambda e, slot=slot, fc2=fc2: e.dma_start(
                    out=wu[slot][:, :, :], in_=Wu[:, :, fc2 * 256:(fc2 + 1) * 256]),
                    reads=(), writes=(("wu", slot),))
                for half in range(2):
                    fc = fc2 * 2 + half
                    par = fc % 2
                    for which, wt_, wtok in ((0, wg, "wg"), (1, wu, "wu")):
                        bank = par * 2 + which
                        for kc in range(NCH):
                            for (a, b, smp) in segs:
                                if smp:
                                    col = which * 8
                                    o = k.ps[SBK[par]][:, col:col + NS]
                                    tok = ("ps", SBK[par])
                                else:
                                    o = k.ps[bank][:, 0:512]
                                    tok = ("ps", bank)
                                S.op("pe", lambda e, o=o, w=wt_[slot], kc=kc, half=half, a=a, b=b: e.matmul(
                                    out=o, lhsT=w[:, kc, half * 128:(half + 1) * 128], rhs=ht[:, kc, a:b],
                                    start=(kc == 0), stop=(kc == NCH - 1)),
                                    reads=((wtok, slot), "ffn_h"), writes=(tok,))
                    for (a, b, smp) in segs:
                        if smp:
                            gin = k.ps[SBK[par]][:, 0:NS]
                            uin = k.ps[SBK[par]][:, 8:8 + NS]
                            gt, ut = ("ps", SBK[par]), ("ps", SBK[par])
                        else:
                            gin = k.ps[par * 2][:, 0:512]
                            uin = k.ps[par * 2 + 1][:, 0:512]
                            gt, ut = ("ps", par * 2), ("ps", par * 2 + 1)
                        sgt = ("ffn_sg", par, smp)
                        S.op("act", lambda e, gin=gin, a=a, b=b, par=par: e.activation(
                            out=sg[par][:, a:b], in_=gin, func=AF.Silu),
                            reads=(gt,), writes=(sgt,))
                        S.op("dve", lambda e, uin=uin, a=a, b=b, par=par, fc=fc: e.tensor_tensor(
                            out=at[:, fc, a:b], in0=sg[par][:, a:b], in1=uin, op=ALU.mult),
                            reads=(sgt, ut), writes=(("ffn_a", fc),))
            for dc2 in range(NCH // 2):
                slot = nd % 2
                nd += 1
                S.dma("pool", lambda e, slot=slot, dc2=dc2: e.dma_start(
                    out=wd[slot][:, :, :], in_=Wd[:, :, dc2 * 256:(dc2 + 1) * 256]),
                    reads=(), writes=(("wd", slot),))
                for half in range(2):
                    dc = dc2 * 2 + half
                    bank = dc % 2
                    for fc in range(NFC):
                        for (a, b, smp) in segs:
                            if smp:
                                o = k.ps[SBK[dc % 2]][:, 32:32 + NS]
                                tok = ("ps", SBK[dc % 2])
                            else:
                                o = k.ps[bank][:, 0:512]
                                tok = ("ps", bank)
                            S.op("pe", lambda e, o=o, slot=slot, fc=fc, half=half, a=a, b=b: e.matmul(
                                out=o, lhsT=wd[slot][:, fc, half * 128:(half + 1) * 128], rhs=at[:, fc, a:b],
                                start=(fc == 0), stop=(fc == NFC - 1)),
                                reads=(("wd", slot), ("ffn_a", fc)), writes=(tok,))
                    for (a, b, smp) in segs:
                        if smp:
                            din = k.ps[SBK[dc % 2]][:, 32:32 + NS]
                            tok = ("ps", SBK[dc % 2])
                        else:
                            din = k.ps[bank][:, 0:512]
                            tok = ("ps", bank)
                        S.op("dve", lambda e, din=din, dc=dc, a=a, b=b: e.tensor_tensor(
                            out=xt[:, dc, a:b], in0=xt[:, dc, a:b], in1=din, op=ALU.add),
                            reads=(tok, "ffn_x"), writes=("ffn_x",))
            S.dma("sp", lambda e, c0=c0, n=ncols: e.dma_start(out=k.XT[:, :, c0:c0 + n], in_=xt[:, :, 0:n]),
                  reads=("ffn_x",), writes=(("XT", ti),))
            S.flush(k.block)
    S.barrier()


def phase_pool(k, L, j):
    nc, S, cfg = k.nc, k.S, k.cfg
    HB = PBUF + 520
    with ExitStack() as es:
        A_ = lambda name, shape, dt: es.enter_context(nc.sbuf_tensor(f"{name}_L{L}", shape, dt))
        xt = A_("pl_x", [128, NCH, 520], F32)
        hb = A_("pl_h", [128, NCH, HB], F32)
        wa = A_("pl_wa", [128, NCH, HB], F32)
        wb = A_("pl_wb", [128, NCH, HB], F32)
        pb = A_("pl_p", [128, NCH, 520], BF16)
        hs = A_("pl_hs", [128, NCH, 24], F32)
        sa = A_("pl_sa", [128, NCH, 24], F32)
        sb_ = A_("pl_sb", [128, NCH, 24], F32)
        wp = A_("pl_wp", [128, NCH, 512], BF16)
        sq = [A_(f"pl_sq{i}", [128, 520], F32) for i in range(2)]
        rstd = A_("pl_rstd", [128, 520], F32)
        tmp = A_("pl_tmp", [128, NCH, PBUF], F32)
        cin = A_("pl_cin", [PBUF, D], F32)
        pout = [A_(f"pl_po{i}", [PBUF, D], F32) for i in range(2)]
        gcol = k.g_mix[:, L * NCH:(L + 1) * NCH]
        SBK = (4, 5)
        S.dma("pool", lambda e: e.dma_start(out=wp[:, :, :], in_=k.w_pool[j].rearrange("g (kc p) o -> p (g kc) o", p=128)),
              writes=("pl_wp",))
        S.op("pool", lambda e: e.memset(hb[:, :, 0:PBUF], 0.0), writes=("pl_h",))
        S.dma("sp", lambda e: e.dma_start(out=cin[:, :], in_=k.cache_pool[j]), writes=("pl_cin",))
        for g in range(4):
            for cc in range(4):
                c = g * 4 + cc
                S.op("pe", lambda e, g=g, cc=cc, c=c: e.transpose(
                    out=k.ps[g][:, cc * 128:cc * 128 + PBUF], in_=cin[0:PBUF, c * 128:(c + 1) * 128],
                    identity=k.ident[0:PBUF, 0:PBUF]), reads=("pl_cin",), writes=(("ps", g),))
            S.op("act", lambda e, g=g: e.copy(
                out=hs[:, g * 4:(g + 1) * 4, 0:PBUF],
                in_=k.ps[g][:, :].rearrange("p (c t) -> p c t", c=4)[:, :, 0:PBUF]),
                reads=(("ps", g),), writes=("pl_hs",))

        def windows(eng, h, a, b, n, toks):
            th, ta, tb = toks
            S.op(eng, lambda e: e.tensor_tensor(out=a[:, :, 1:n], in0=h[:, :, 1:n], in1=h[:, :, 0:n - 1], op=ALU.add),
                 reads=(th,), writes=(ta,))
            S.op(eng, lambda e: e.tensor_tensor(out=b[:, 4:16, 3:n], in0=a[:, 4:16, 3:n], in1=a[:, 4:16, 1:n - 2], op=ALU.add),
                 reads=(ta,), writes=(tb,))
            S.op(eng, lambda e: e.tensor_tensor(out=a[:, 8:16, 7:n], in0=b[:, 8:16, 7:n], in1=b[:, 8:16, 3:n - 4], op=ALU.add),
                 reads=(tb,), writes=(ta,))
            S.op(eng, lambda e: e.tensor_tensor(out=b[:, 12:16, 15:n], in0=a[:, 12:16, 15:n], in1=a[:, 12:16, 7:n - 8], op=ALU.add),
                 reads=(ta,), writes=(tb,))

        for ti in range(cfg.ntile):
            c0, ncols = tile_cols(cfg, ti)
            segs = mm_cols(ncols)
            last = ncols > 512
            S.dma("sp", lambda e, c0=c0, n=ncols: e.dma_start(out=xt[:, :, 0:n], in_=k.XT[:, :, c0:c0 + n]),
                  reads=(("XT", ti),), writes=("pl_x",))
            if ti > 0:
                S.op("pool", lambda e: e.tensor_copy(out=hb[:, :, 0:PBUF], in_=hb[:, :, 512:512 + PBUF]),
                     reads=("pl_h",), writes=("pl_h",))
            rmsnorm_tile(k, xt, "pl_x", ncols, gcol, hb[:, :, PBUF:HB], "pl_h", sq, 6, (7, 0), rstd)
            windows("pool", hb, wa, wb, PBUF + 512, ("pl_h", "pl_wa", "pl_wb"))
            for g in range(4):
                w = 2 << g
                src = wa if g % 2 == 0 else wb
                stok = "pl_wa" if g % 2 == 0 else "pl_wb"
                S.op("dve", lambda e, g=g, w=w, src=src: e.scalar_tensor_tensor(
                    out=pb[:, g * 4:(g + 1) * 4, 0:512], in0=src[:, g * 4:(g + 1) * 4, PBUF:PBUF + 512],
                    scalar=1.0 / w, in1=hb[:, g * 4:(g + 1) * 4, PBUF:PBUF + 512], op0=ALU.mult, op1=ALU.subtract),
                    reads=(stok, "pl_h"), writes=("pl_p",))
                if ti == 0:
                    S.op("dve", lambda e, g=g, src=src: e.tensor_tensor(
                        out=tmp[:, g * 4:(g + 1) * 4, :], in0=src[:, g * 4:(g + 1) * 4, PBUF:2 * PBUF],
                        in1=k.invc[:, g * 4:(g + 1) * 4, :], op=ALU.mult),
                        reads=(stok,), writes=("pl_tmp",))
                    S.op("dve", lambda e, g=g: e.tensor_tensor(
                        out=pb[:, g * 4:(g + 1) * 4, 0:PBUF], in0=tmp[:, g * 4:(g + 1) * 4, :],
                        in1=hb[:, g * 4:(g + 1) * 4, PBUF:2 * PBUF], op=ALU.subtract),
                        reads=("pl_tmp", "pl_h"), writes=("pl_p",))
            if last:
                S.op("dve", lambda e: e.tensor_copy(out=hs[:, :, PBUF:PBUF + NS], in_=hb[:, :, PBUF + 512:PBUF + 520]),
                     reads=("pl_h",), writes=("pl_hs",))
                windows("dve", hs, sa, sb_, PBUF + NS, ("pl_hs", "pl_sa", "pl_sb"))
                for g in range(4):
                    w = 2 << g
                    src = sa if g % 2 == 0 else sb_
                    stok = "pl_sa" if g % 2 == 0 else "pl_sb"
                    S.op("dve", lambda e, g=g, w=w, src=src: e.scalar_tensor_tensor(
                        out=pb[:, g * 4:(g + 1) * 4, 512:520], in0=src[:, g * 4:(g + 1) * 4, PBUF:PBUF + NS],
                        scalar=1.0 / w, in1=hs[:, g * 4:(g + 1) * 4, PBUF:PBUF + NS], op0=ALU.mult, op1=ALU.subtract),
                        reads=(stok, "pl_hs"), writes=("pl_p",))
            n_out = 0
            for g in range(4):
                for oc in range(4):
                    c = g * 4 + oc
                    bank = n_out % 4
                    sbank = SBK[n_out % 2]
                    n_out += 1
                    for kc in range(4):
                        for (a, b, smp) in segs:
                            o = k.ps[sbank][:, 0:NS] if smp else k.ps[bank][:, 0:512]
                            tok = ("ps", sbank) if smp else ("ps", bank)
                            S.op("pe", lambda e, o=o, g=g, kc=kc, oc=oc, a=a, b=b: e.matmul(
                                out=o, lhsT=wp[:, g * 4 + kc, oc * 128:(oc + 1) * 128], rhs=pb[:, g * 4 + kc, a:b],
                                start=(kc == 0), stop=(kc == 3)),
                                reads=("pl_wp", "pl_p"), writes=(tok,))
                    for (a, b, smp) in segs:
                        o = k.ps[sbank][:, 0:NS] if smp else k.ps[bank][:, 0:512]
                        tok = ("ps", sbank) if smp else ("ps", bank)
                        S.op("dve", lambda e, o=o, c=c, a=a, b=b: e.scalar_tensor_tensor(
                            out=xt[:, c, a:b], in0=o, scalar=k.pscale[:, j * NCH + c:j * NCH + c + 1], in1=xt[:, c, a:b],
                            op0=ALU.mult, op1=ALU.add),
                            reads=(tok, "pl_x"), writes=("pl_x",))
            S.dma("sp", lambda e, c0=c0, n=ncols: e.dma_start(out=k.XT[:, :, c0:c0 + n], in_=xt[:, :, 0:n]),
                  reads=("pl_x",), writes=(("XT", ti),))
            if last:
                for oi, (src, c_lo, dst) in enumerate(((hb, 512, k.new_pool_prompt[j]), (hs, NS, k.new_pool_sample[j]))):
                    stok = "pl_h" if oi == 0 else "pl_hs"
                    for g in range(4):
                        for cc in range(4):
                            c = g * 4 + cc
                            S.op("pe", lambda e, g=g, cc=cc, c=c, src=src, c_lo=c_lo: e.transpose(
                                out=k.ps[g][0:PBUF, cc * 128:(cc + 1) * 128], in_=src[:, c, c_lo:c_lo + PBUF],
                                identity=k.ident[:, :]), reads=(stok,), writes=(("ps", g),))
                        S.op("act", lambda e, g=g, oi=oi: e.copy(
                            out=pout[oi][0:PBUF, g * 512:(g + 1) * 512], in_=k.ps[g][0:PBUF, :]),
                            reads=(("ps", g),), writes=(("pl_po", oi),))
                    S.dma("sp", lambda e, dst=dst, oi=oi: e.dma_start(out=dst, in_=pout[oi][0:PBUF, :]),
                          reads=(("pl_po", oi),), writes=(("pool_out", j, oi),))
            S.flush(k.block)
    S.barrier()


def phase_s5(k, L, j):
    nc, S, cfg = k.nc, k.S, k.cfg
    TC = getattr(cfg, 's5_tc', 32)
    G2 = 64
    TWO_PI_HI = 6.28125
    TWO_PI_LO = 2.0 * math.pi - 6.28125
    with ExitStack() as es:
        A_ = lambda name, shape, dt: es.enter_context(nc.sbuf_tensor(f"{name}_L{L}", shape, dt))
        AR2 = A_("s5_ar2", [128, G2, 2], F32)
        AI2 = A_("s5_ai2", [128, G2, 2], F32)
        LB = [[A_(f"s5_lb{ri}{eo}", [128, NCH, 128], BF16) for eo in range(2)] for ri in range(2)]
        LBa = [[A_(f"s5_lba{ri}{eo}", [128, NCH, 128], BF16) for eo in range(2)] for ri in range(2)]
        AR2q = A_("s5_ar2q", [128, G2, 2], F32)
        AI2q = A_("s5_ai2q", [128, G2, 2], F32)
        LCR = A_("s5_lcr", [128, G2, 64], BF16)
        LCI = A_("s5_lci", [128, G2, 64], BF16)
        Dg = A_("s5_dg", [128, NCH, 128], F32)
        S0 = A_("s5_s0", [128, G2, 2], F32)
        with ExitStack() as pes:
            P_ = lambda name, shape, dt: pes.enter_context(nc.sbuf_tensor(f"{name}_L{L}", shape, dt))
            names = ["ar", "ai", "dt", "th", "mag", "kk", "r", "sh", "sn", "cs", "abr", "abi", "den", "nr", "t1", "t2", "fr", "fi",
                     "far", "fai", "a2r", "a2i"]
            v = {n: P_("s5p_" + n, [128, G2], F32) for n in names}
            fT = [P_(f"s5p_fT{i}", [64, 128], F32) for i in range(2)]
            FW = [P_(f"s5p_fw{i}", [128, NCH, 128], F32) for i in range(2)]
            BW = [P_(f"s5p_bw{i}", [128, NCH, 128], F32) for i in range(2)]
            TM = [P_(f"s5p_tm{i}", [128, NCH, 128], F32) for i in range(2)]
            sel = P_("s5p_sel", [64, NCH, 128], F32)
            msk = P_("s5p_msk", [128, 2], F32)
            LBf = [P_(f"s5p_lbf{i}", [128, NCH, 128], F32) for i in range(2)]
            LCf = [P_(f"s5p_lcf{i}", [128, G2, 64], F32) for i in range(2)]
            dcol = P_("s5p_dcol", [128, NCH], F32)
            st_view = lambda ap2d: ap2d.rearrange("(gp par) n -> (par n) gp", par=2)
            with nc.allow_non_contiguous_dma(reason="small ssm parameter tables"):
                S.dma("sp", lambda e: e.dma_start(out=v["ar"][:, :], in_=st_view(k.ssm_a_re[j])), writes=("p_ar",))
                S.dma("sp", lambda e: e.dma_start(out=v["ai"][:, :], in_=st_view(k.ssm_a_im[j])), writes=("p_ai",))
                for par in range(2):
                    S.dma("sp", lambda e, par=par: e.dma_start(
                        out=v["dt"][par * 64:(par + 1) * 64, :],
                        in_=k.ssm_log_dt[j:j + 1, :].rearrange("o (gp par) -> o gp par", par=2)[:, :, par].broadcast_to([64, G2])),
                        writes=("p_dt",))
                S.dma("sp", lambda e: e.dma_start(out=S0[:, :, 0], in_=st_view(k.state_re[j])), writes=("s5_s0",))
                S.dma("sp", lambda e: e.dma_start(out=S0[:, :, 1], in_=st_view(k.state_im[j])), writes=("s5_s0",))
                S.dma("sp", lambda e: e.dma_start(out=dcol[:, :], in_=k.ssm_d[j].rearrange("(c p) -> p c", p=128)), writes=("p_dcol",))
                S.dma("sp", lambda e: e.dma_start(out=sel[:, :, :], in_=k.c_sel), writes=("p_sel",))
                S.dma("sp", lambda e: e.dma_start(out=BW[0][:, :, :], in_=k.b_re_w[j]), writes=("p_bw0",))
                S.dma("sp", lambda e: e.dma_start(out=BW[1][:, :, :], in_=k.b_im_w[j]), writes=("p_bw1",))
                S.dma("sp", lambda e: e.dma_start(out=LCf[0][:, :, :], in_=k.c_re_w[j]), writes=("p_lcf0",))
                S.dma("sp", lambda e: e.dma_start(out=LCf[1][:, :, :], in_=k.c_im_w[j]), writes=("p_lcf1",))
                S.dma("sp", lambda e: e.dma_start(out=msk[:, :], in_=k.c_msk), writes=("p_msk",))
                S.flush(k.block)
            def tt(o, a, b, op, rd, wr, eng="dve"):
                S.op(eng, lambda e: e.tensor_tensor(out=o, in0=a, in1=b, op=op), reads=rd, writes=wr)
            def ts(o, a, s1, op0, rd, wr, s2=None, op1=None, eng="dve"):
                if op1 is None:
                    S.op(eng, lambda e: e.tensor_scalar(out=o, in0=a, scalar1=s1, scalar2=None, op0=op0), reads=rd, writes=wr)
                else:
                    S.op(eng, lambda e: e.tensor_scalar(out=o, in0=a, scalar1=s1, scalar2=s2, op0=op0, op1=op1), reads=rd, writes=wr)
            def act(o, a, f, rd, wr, scale=1.0):
                S.op("act", lambda e: e.activation(out=o, in_=a, func=f, scale=scale), reads=rd, writes=wr)
            V = lambda n: v[n][:, :]
            act(V("dt"), V("dt"), AF.Exp, ("p_dt",), ("p_dt",))
            tt(V("t1"), V("dt"), V("ar"), ALU.mult, ("p_dt", "p_ar"), ("p_t1",))
            act(V("mag"), V("t1"), AF.Exp, ("p_t1",), ("p_mag",))
            tt(V("th"), V("dt"), V("ai"), ALU.mult, ("p_dt", "p_ai"), ("p_th",))
            ts(V("kk"), V("th"), math.pi, ALU.is_gt, ("p_th",), ("p_kk",))
            for m in (1, 2, 3):
                ts(V("t2"), V("th"), (2 * m + 1) * math.pi, ALU.is_gt, ("p_th",), ("p_t2",))
                tt(V("kk"), V("kk"), V("t2"), ALU.add, ("p_kk", "p_t2"), ("p_kk",))
            ts(V("t2"), V("kk"), -TWO_PI_HI, ALU.mult, ("p_kk",), ("p_t2",))
            tt(V("r"), V("th"), V("t2"), ALU.add, ("p_th", "p_t2"), ("p_r",))
            ts(V("t2"), V("kk"), -TWO_PI_LO, ALU.mult, ("p_kk",), ("p_t2",))
            tt(V("r"), V("r"), V("t2"), ALU.add, ("p_r", "p_t2"), ("p_r",))
            act(V("sn"), V("r"), AF.Sin, ("p_r",), ("p_sn",))
            act(V("sh"), V("r"), AF.Sin, ("p_r",), ("p_sh",), scale=0.5)
            tt(V("cs"), V("sh"), V("sh"), ALU.mult, ("p_sh",), ("p_cs",))
            ts(V("cs"), V("cs"), -2.0, ALU.mult, ("p_cs",), ("p_cs",), s2=1.0, op1=ALU.add)
            tt(V("abr"), V("mag"), V("cs"), ALU.mult, ("p_mag", "p_cs"), ("p_abr",))
            tt(V("abi"), V("mag"), V("sn"), ALU.mult, ("p_mag", "p_sn"), ("p_abi",))
            S.op("dve", lambda e: e.tensor_copy(out=AR2[:, :, 0], in_=V("abr")), reads=("p_abr",), writes=("s5_ar2",))
            S.op("dve", lambda e: e.tensor_copy(out=AR2[:, :, 1], in_=V("abr")), reads=("p_abr",), writes=("s5_ar2",))
            S.op("dve", lambda e: e.tensor_copy(out=AI2[:, :, 1], in_=V("abi")), reads=("p_abi",), writes=("s5_ai2",))
            ts(AI2[:, :, 0], V("abi"), -1.0, ALU.mult, ("p_abi",), ("s5_ai2",))
            tt(V("den"), V("ar"), V("ar"), ALU.mult, ("p_ar",), ("p_den",))
            tt(V("t1"), V("ai"), V("ai"), ALU.mult, ("p_ai",), ("p_t1",))
            tt(V("den"), V("den"), V("t1"), ALU.add, ("p_den", "p_t1"), ("p_den",))
            S.op("dve", lambda e: e.reciprocal(out=V("den"), in_=V("den")), reads=("p_den",), writes=("p_den",))
            ts(V("nr"), V("abr"), -1.0, ALU.add, ("p_abr",), ("p_nr",))
            tt(V("t1"), V("nr"), V("ar"), ALU.mult, ("p_nr", "p_ar"), ("p_t1",))
            tt(V("t2"), V("abi"), V("ai"), ALU.mult, ("p_abi", "p_ai"), ("p_t2",))
            tt(V("t1"), V("t1"), V("t2"), ALU.add, ("p_t1", "p_t2"), ("p_t1",))
            tt(V("fr"), V("t1"), V("den"), ALU.mult, ("p_t1", "p_den"), ("p_fr",))
            tt(V("t1"), V("abi"), V("ar"), ALU.mult, ("p_abi", "p_ar"), ("p_t1",))
            tt(V("t2"), V("nr"), V("ai"), ALU.mult, ("p_nr", "p_ai"), ("p_t2",))
            tt(V("t1"), V("t1"), V("t2"), ALU.subtract, ("p_t1", "p_t2"), ("p_t1",))
            tt(V("fi"), V("t1"), V("den"), ALU.mult, ("p_t1", "p_den"), ("p_fi",))
            tt(V("t1"), V("abr"), V("abr"), ALU.mult, ("p_abr",), ("p_t1",))
            tt(V("t2"), V("abi"), V("abi"), ALU.mult, ("p_abi",), ("p_t2",))
            tt(V("a2r"), V("t1"), V("t2"), ALU.subtract, ("p_t1", "p_t2"), ("p_a2r",))
            tt(V("t1"), V("abr"), V("abi"), ALU.mult, ("p_abr", "p_abi"), ("p_t1",))
            ts(V("a2i"), V("t1"), 2.0, ALU.mult, ("p_t1",), ("p_a2i",))
            S.op("dve", lambda e: e.tensor_copy(out=AR2q[:, :, 0], in_=V("a2r")), reads=("p_a2r",), writes=("s5_ar2q",))
            S.op("dve", lambda e: e.tensor_copy(out=AR2q[:, :, 1], in_=V("a2r")), reads=("p_a2r",), writes=("s5_ar2q",))
            S.op("dve", lambda e: e.tensor_copy(out=AI2q[:, :, 1], in_=V("a2i")), reads=("p_a2i",), writes=("s5_ai2q",))
            ts(AI2q[:, :, 0], V("a2i"), -1.0, ALU.mult, ("p_a2i",), ("s5_ai2q",))
            tt(V("t1"), V("abr"), V("fr"), ALU.mult, ("p_abr", "p_fr"), ("p_t1",))
            tt(V("t2"), V("abi"), V("fi"), ALU.mult, ("p_abi", "p_fi"), ("p_t2",))
            tt(V("far"), V("t1"), V("t2"), ALU.subtract, ("p_t1", "p_t2"), ("p_far",))
            tt(V("t1"), V("abr"), V("fi"), ALU.mult, ("p_abr", "p_fi"), ("p_t1",))
            tt(V("t2"), V("abi"), V("fr"), ALU.mult, ("p_abi", "p_fr"), ("p_t2",))
            tt(V("fai"), V("t1"), V("t2"), ALU.add, ("p_t1", "p_t2"), ("p_fai",))
            FWr, FWi, BWr, BWi = (t[:, :, :] for t in (FW[0], FW[1], BW[0], BW[1]))
            for (nr_, ni_, LBdst) in (("fr", "fi", LB), ("far", "fai", LBa)):
                for i, nm in enumerate((nr_, ni_)):
                    S.op("pe", lambda e, nm=nm, i=i: e.transpose(out=k.ps[i][0:64, 0:128], in_=v[nm][:, :], identity=k.ident[:, :]),
                         reads=("p_" + nm,), writes=(("ps", i),))
                    S.op("act", lambda e, i=i: e.copy(out=fT[i][:, :], in_=k.ps[i][0:64, 0:128]),
                         reads=(("ps", i),), writes=(("p_fT", i),))
                    for q4 in range(4):
                        bank = 2 + (i * 4 + q4) % 4
                        for jj4 in range(4):
                            jj = q4 * 4 + jj4
                            S.op("pe", lambda e, i=i, jj=jj, jj4=jj4, bank=bank: e.matmul(
                                out=k.ps[bank][:, jj4 * 128:(jj4 + 1) * 128], lhsT=sel[:, jj, :], rhs=fT[i][:, :],
                                start=True, stop=True), reads=("p_sel", ("p_fT", i)), writes=(("ps", bank),))
                        S.op("act", lambda e, i=i, q4=q4, bank=bank: e.copy(
                            out=FW[i][:, q4 * 4:(q4 + 1) * 4, :], in_=k.ps[bank][:, :].rearrange("p (a b) -> p a b", a=4)),
                            reads=(("ps", bank),), writes=(("p_fw", i),))
                tt(TM[0][:, :, :], FWr, BWr, ALU.mult, (("p_fw", 0), "p_bw0"), ("p_tm0",))
                tt(TM[1][:, :, :], FWi, BWi, ALU.mult, (("p_fw", 1), "p_bw1"), ("p_tm1",))
                tt(LBf[0][:, :, :], TM[0][:, :, :], TM[1][:, :, :], ALU.subtract, ("p_tm0", "p_tm1"), ("p_lbf0",))
                tt(TM[0][:, :, :], FWr, BWi, ALU.mult, (("p_fw", 0), "p_bw1"), ("p_tm0",))
                tt(TM[1][:, :, :], FWi, BWr, ALU.mult, (("p_fw", 1), "p_bw0"), ("p_tm1",))
                tt(LBf[1][:, :, :], TM[0][:, :, :], TM[1][:, :, :], ALU.add, ("p_tm0", "p_tm1"), ("p_lbf1",))
                for ri in range(2):
                    for eo in range(2):
                        ts(LBdst[ri][eo][:, :, :], LBf[ri][:, :, :], msk[:, eo:eo + 1], ALU.mult, (f"p_lbf{ri}", "p_msk"), ("s5_lb",))
            S.op("dve", lambda e: e.tensor_copy(out=LCR[:, :, :], in_=LCf[0][:, :, :]), reads=("p_lcf0",), writes=("s5_lcr",))
            ts(LCI[:, :, :], LCf[1][:, :, :], -1.0, ALU.mult, ("p_lcf1",), ("s5_lci",))
            for c in range(NCH):
                ts(Dg[:, c, :], k.ident[:, :], dcol[:, c:c + 1], ALU.mult, ("ident", "p_dcol"), ("s5_dg",))
            S.flush(k.block)
        S.barrier()
        if getattr(cfg, "s5_stop", 0) == 1:
            return
        xt = A_("s5_x", [128, NCH, 520], F32)
        hb = A_("s5_hb", [128, NCH, 521], BF16)
        gb = A_("s5_gb", [128, NCH, 520], BF16)
        BU = A_("s5_bu", [128, TC, G2, 2], F32)
        SS = A_("s5_ss", [128, TC + 2, G2, 2], F32)
        SSb = A_("s5_ssb", [128, TC, G2, 2], BF16)
        t1 = A_("s5_t1", [128, 2, G2, 2], F32)
        t2 = A_("s5_t2", [128, 2, G2, 2], F32)
        SSTOK = (("s5_ss", "dve"),)
        sq = [A_(f"s5_sq{i}", [128, 520], F32) for i in range(2)]
        rstd = A_("s5_rstd", [128, 520], F32)
        wa = [A_(f"s5_wa{i}", [128, NCH, 128], BF16) for i in range(2)]
        wb = [A_(f"s5_wb{i}", [128, NCH, 128], BF16) for i in range(2)]
        sg = [A_(f"s5_sg{i}", [128, 520], F32) for i in range(2)]
        xr = [A_(f"s5_xr{i}", [128, 520], F32) for i in range(2)]
        gcol = k.g_mix[:, L * NCH:(L + 1) * NCH]
        Wa = k.w_glu_a[j].rearrange("(kc p) f -> p kc f", p=128)
        Wb = k.w_glu_b[j].rearrange("(kc p) f -> p kc f", p=128)
        SBK = (4, 5)
        S.op("pool", lambda e: e.memset(SS[:, 0:2, :, :], 0.0), writes=SSTOK)
        S.op("pool", lambda e: e.memset(hb[:, :, 0:1], 0.0), writes=("s5_hb",))
        nbank = [0]
        nw = [0]

        stop = getattr(cfg, "s5_stop", 0)

        def scan_chunk(col0, nt, pair):
            if stop == 2:
                return
            hc = col0 + 1
            for g8 in range(G2 // 8):
                banks = (nbank[0] % 4, (nbank[0] + 1) % 4)
                nbank[0] += 2
                for hh in range(2):
                    bank = banks[hh]
                    for a in range(2):
                        for q2 in range(2):
                            gp = g8 * 8 + a * 4 + hh * 2 + q2
                            jj = gp // 4
                            for ri in range(2):
                                col = ((a * 2 + q2) * 2 + ri) * nt
                                S.op("pe", lambda e, bank=bank, col=col, ri=ri, jj=jj, q2=q2, hh=hh: e.matmul(
                                    out=k.ps[bank][:, col:col + nt],
                                    lhsT=LB[ri][q2][64 * hh:64 * hh + 64, jj, :], rhs=hb[64 * hh:64 * hh + 64, jj, hc:hc + nt],
                                    start=True, stop=(not pair)),
                                    reads=("s5_lb", "s5_hb"), writes=(("ps", bank),))
                                if pair:
                                    S.op("pe", lambda e, bank=bank, col=col, ri=ri, jj=jj, q2=q2, hh=hh: e.matmul(
                                        out=k.ps[bank][:, col:col + nt],
                                        lhsT=LBa[ri][q2][64 * hh:64 * hh + 64, jj, :], rhs=hb[64 * hh:64 * hh + 64, jj, hc - 1:hc - 1 + nt],
                                        start=False, stop=True),
                                        reads=("s5_lb", "s5_hb"), writes=(("ps", bank),))
                    for a in range(2):
                        gp0 = g8 * 8 + a * 4 + hh * 2
                        S.op("act", lambda e, bank=bank, a=a, gp0=gp0: e.copy(
                            out=BU[:, 0:nt, gp0:gp0 + 2, :].rearrange("p t g r -> p g r t"),
                            in_=k.ps[bank][:, a * 4 * nt:(a + 1) * 4 * nt].rearrange("p (g r t) -> p g r t", g=2, r=2)),
                            reads=(("ps", bank),), writes=("s5_bu",))
            if stop == 3:
                return
            sst = ("s5_ss", "dve")
            if pair:
                bc = lambda ap3: ap3.unsqueeze(1).broadcast_to([128, 2, G2, 2])
                for t in range(0, nt, 2):
                    S.op("dve", lambda e, t=t: e.tensor_tensor(
                        out=t1[:, :, :, :], in0=bc(AR2q[:, :, :]), in1=SS[:, t:t + 2, :, :], op=ALU.mult),
                        reads=("s5_ar2q", sst), writes=("s5_t1",))
                    S.op("dve", lambda e, t=t: e.tensor_tensor(
                        out=t2[:, :, :, :], in0=bc(AI2q[:, :, :]), in1=SS[:, t:t + 2, :, ::-1], op=ALU.mult),
                        reads=("s5_ai2q", sst), writes=("s5_t2",))
                    S.op("dve", lambda e: e.tensor_tensor(out=t1[:, :, :, :], in0=t1[:, :, :, :], in1=t2[:, :, :, :], op=ALU.add),
                         reads=("s5_t1", "s5_t2"), writes=("s5_t1",))
                    S.op("dve", lambda e, t=t: e.tensor_tensor(
                        out=SS[:, t + 2:t + 4, :, :], in0=t1[:, :, :, :], in1=BU[:, t:t + 2, :, :], op=ALU.add),
                        reads=("s5_t1", "s5_bu"), writes=(sst,))
            else:
                for t in range(nt):
                    S.op("dve", lambda e, t=t: e.tensor_tensor(
                        out=t1[:, 0, :, :], in0=AR2[:, :, :], in1=SS[:, t + 1, :, :], op=ALU.mult),
                        reads=("s5_ar2", sst), writes=("s5_t1",))
                    S.op("dve", lambda e, t=t: e.tensor_tensor(
                        out=t2[:, 0, :, :], in0=AI2[:, :, :], in1=SS[:, t + 1, :, ::-1], op=ALU.mult),
                        reads=("s5_ai2", sst), writes=("s5_t2",))
                    S.op("dve", lambda e: e.tensor_tensor(out=t1[:, 0, :, :], in0=t1[:, 0, :, :], in1=t2[:, 0, :, :], op=ALU.add),
                         reads=("s5_t1", "s5_t2"), writes=("s5_t1",))
                    S.op("dve", lambda e, t=t: e.tensor_tensor(
                        out=SS[:, t + 2, :, :], in0=t1[:, 0, :, :], in1=BU[:, t, :, :], op=ALU.add),
                        reads=("s5_t1", "s5_bu"), writes=(sst,))
            if stop == 4:
                return
            S.op("act", lambda e: e.copy(out=SSb[:, 0:nt, :, :], in_=SS[:, 2:nt + 2, :, :]),
                 reads=SSTOK, writes=("s5_ssb",))
            for jj in range(NCH):
                bank = nbank[0] % 4
                nbank[0] += 1
                for hh in range(2):
                    first = True
                    for q2 in range(2):
                        gp = jj * 4 + hh * 2 + q2
                        for ri, LC in enumerate((LCR, LCI)):
                            S.op("pe", lambda e, bank=bank, hh=hh, gp=gp, ri=ri, LC=LC, first=first: e.matmul(
                                out=k.ps[bank][64 * hh:64 * hh + 64, 0:nt], lhsT=LC[:, gp, :], rhs=SSb[:, 0:nt, gp, ri],
                                start=first, stop=False),
                                reads=("s5_lcr", "s5_lci", "s5_ssb"), writes=(("ps", bank),))
                            first = False
                    S.op("pe", lambda e, bank=bank, hh=hh, jj=jj: e.matmul(
                        out=k.ps[bank][64 * hh:64 * hh + 64, 0:nt], lhsT=Dg[:, jj, 64 * hh:64 * hh + 64], rhs=xt[:, jj, col0:col0 + nt],
                        start=False, stop=True),
                        reads=("s5_dg", "s5_x"), writes=(("ps", bank),))
                S.op("act", lambda e, bank=bank, jj=jj: e.activation(
                    out=gb[:, jj, col0:col0 + nt], in_=k.ps[bank][:, 0:nt], func=AF.Gelu),
                    reads=(("ps", bank),), writes=("s5_gb",))

        st_view = lambda ap2d: ap2d.rearrange("(gp par) n -> (par n) gp", par=2)
        for ti in range(cfg.ntile):
            c0, ncols = tile_cols(cfg, ti)
            segs = mm_cols(ncols)
            last = ncols > 512
            S.dma("sp", lambda e, c0=c0, n=ncols: e.dma_start(out=xt[:, :, 0:n], in_=k.XT[:, :, c0:c0 + n]),
                  reads=(("XT", ti),), writes=("s5_x",))
            rmsnorm_tile(k, xt, "s5_x", ncols, gcol, xt, "s5_x", sq, 6, (7, 0), rstd)
            if ti > 0:
                S.op("act", lambda e: e.copy(out=hb[:, :, 0:1], in_=hb[:, :, 512:513]), reads=("s5_hb",), writes=("s5_hb",))
            for c in range(NCH):
                S.op("act", lambda e, c=c, n=ncols: e.copy(out=hb[:, c, 1:n + 1], in_=xt[:, c, 0:n]),
                     reads=("s5_x",), writes=("s5_hb",))
            for tc in range(512 // TC):
                scan_chunk(tc * TC, TC, True)
                S.op("act", lambda e: e.copy(out=SS[:, 0:2, :, :], in_=SS[:, TC:TC + 2, :, :]),
                     reads=SSTOK, writes=SSTOK)
                S.flush(k.block)
            if last:
                with nc.allow_non_contiguous_dma(reason="ssm state output"):
                    S.dma("sp", lambda e: e.dma_start(out=st_view(k.new_ssm_re_prompt[j]), in_=SS[:, 1, :, 0]),
                          reads=SSTOK, writes=("ssm_out0",))
                    S.dma("sp", lambda e: e.dma_start(out=st_view(k.new_ssm_im_prompt[j]), in_=SS[:, 1, :, 1]),
                          reads=SSTOK, writes=("ssm_out1",))
                    S.flush(k.block)
                S.op("act", lambda e: e.copy(out=SS[:, 1, :, :], in_=S0[:, :, :]),
                     reads=("s5_s0",) + SSTOK, writes=SSTOK)
                scan_chunk(512, NS, False)
                with nc.allow_non_contiguous_dma(reason="ssm state output"):
                    S.dma("sp", lambda e: e.dma_start(out=st_view(k.new_ssm_re_sample[j]), in_=SS[:, NS + 1, :, 0]),
                          reads=SSTOK, writes=("ssm_out2",))
                    S.dma("sp", lambda e: e.dma_start(out=st_view(k.new_ssm_im_sample[j]), in_=SS[:, NS + 1, :, 1]),
                          reads=SSTOK, writes=("ssm_out3",))
                    S.flush(k.block)
            for oc2 in range(NCH if stop not in (2, 3, 4, 5) else 0):
                slot = nw[0] % 2
                nw[0] += 1
                S.dma("pool", lambda e, slot=slot, oc2=oc2: e.dma_start(out=wa[slot][:, :, :], in_=Wa[:, :, oc2 * 128:(oc2 + 1) * 128]),
                      writes=(("s5_wa", slot),))
                S.dma("pool", lambda e, slot=slot, oc2=oc2: e.dma_start(out=wb[slot][:, :, :], in_=Wb[:, :, oc2 * 128:(oc2 + 1) * 128]),
                      writes=(("s5_wb", slot),))
                for half in range(1):
                    oc = oc2
                    par = oc % 2
                    S.dma("sp", lambda e, par=par, oc=oc, c0=c0, n=ncols: e.dma_start(out=xr[par][:, 0:n], in_=k.XT[:, oc, c0:c0 + n]),
                          reads=(), writes=(("s5_xr", par),))
                    for which, wt_, wtok in ((0, wa, "s5_wa"), (1, wb, "s5_wb")):
                        bank = par * 2 + which
                        for kc in range(NCH):
                            for (a, b, smp) in segs:
                                o = k.ps[SBK[par]][:, which * 8:which * 8 + NS] if smp else k.ps[bank][:, 0:512]
                                tok = ("ps", SBK[par]) if smp else ("ps", bank)
                                S.op("pe", lambda e, o=o, w=wt_[slot], kc=kc, half=half, a=a, b=b: e.matmul(
                                    out=o, lhsT=w[:, kc, half * 128:(half + 1) * 128], rhs=gb[:, kc, a:b],
                                    start=(kc == 0), stop=(kc == NCH - 1)),
                                    reads=((wtok, slot), "s5_gb"), writes=(tok,))
                    for (a, b, smp) in segs:
                        if smp:
                            ain, bin_ = k.ps[SBK[par]][:, 0:NS], k.ps[SBK[par]][:, 8:8 + NS]
                            at_, bt_ = ("ps", SBK[par]), ("ps", SBK[par])
                        else:
                            ain, bin_ = k.ps[par * 2][:, 0:512], k.ps[par * 2 + 1][:, 0:512]
                            at_, bt_ = ("ps", par * 2), ("ps", par * 2 + 1)
                        sgt = ("s5_sg", par, smp)
                        S.op("act", lambda e, bin_=bin_, a=a, b=b, par=par: e.activation(out=sg[par][:, a:b], in_=bin_, func=AF.Sigmoid),
                             reads=(bt_,), writes=(sgt,))
                        S.op("dve", lambda e, ain=ain, a=a, b=b, par=par: e.tensor_tensor(
                            out=sg[par][:, a:b], in0=sg[par][:, a:b], in1=ain, op=ALU.mult),
                            reads=(sgt, at_), writes=(sgt,))
                        S.op("pool", lambda e, a=a, b=b, par=par: e.tensor_tensor(
                            out=xr[par][:, a:b], in0=xr[par][:, a:b], in1=sg[par][:, a:b], op=ALU.add),
                            reads=(sgt, ("s5_xr", par)), writes=(("s5_xr", par),))
                    S.dma("sp", lambda e, par=par, oc=oc, c0=c0, n=ncols: e.dma_start(out=k.XT[:, oc, c0:c0 + n], in_=xr[par][:, 0:n]),
                          reads=(("s5_xr", par),), writes=(("XT", ti),))
            S.flush(k.block)
    S.barrier()


def phase_sb(k, L, j):
    nc, S, cfg = k.nc, k.S, k.cfg
    T, TT = cfg.T, cfg.TT
    NH = 16
    SCALE = 1.0 / math.sqrt(128.0)
    Wqkv = k.w_qkv[j].rearrange("(kc p) f -> p kc f", p=128)
    Wo = k.w_o[j].rearrange("(kc p) f -> p kc f", p=128)
    with ExitStack() as es:
        A_ = lambda name, shape, dt: es.enter_context(nc.sbuf_tensor(f"{name}_L{L}", shape, dt))
        gq = A_("sb_gq", [128, 1], F32)
        gk = A_("sb_gk", [128, 1], F32)
        bcol = A_("sb_bcol", [128, NH], F32)
        brow = A_("sb_brow", [128, NH, NS], F32)
        tri = A_("sb_tri", [128, 128], BF16)
        trif = A_("sb_trif", [128, 128], F32)
        mlt = A_("sb_mlt", [128, 128], F32)
        m8 = A_("sb_m8", [128, NH, NS], F32)
        one_col = A_("sb_one", [128, 1], F32)
        qsT = A_("sb_qsT", [128, NH, NS], BF16)
        ksT = A_("sb_ksT", [128, NH, NS], BF16)
        vs = A_("sb_vs", [NS, D], BF16)
        osT = A_("sb_osT", [128, NH, NS], BF16)
        with nc.allow_non_contiguous_dma(reason="tiny sb params"):
            S.dma("sp", lambda e: e.dma_start(out=gq[:, :], in_=k.sb_q_norm[j].rearrange("(p o) -> p o", o=1)), writes=("sb_gq",))
            S.dma("sp", lambda e: e.dma_start(out=gk[:, :], in_=k.sb_k_norm[j].rearrange("(p o) -> p o", o=1)), writes=("sb_gk",))
            S.dma("sp", lambda e: e.dma_start(out=bcol[:, :], in_=k.sb_bias[j:j + 1, :].broadcast_to([128, NH])), writes=("sb_bcol",))
            S.dma("sp", lambda e: e.dma_start(out=trif[:, :], in_=k.c_tri), writes=("sb_trif",))
            S.dma("sp", lambda e: e.dma_start(out=mlt[:, :], in_=k.c_mlt), writes=("sb_mlt",))
            S.flush(k.block)
        S.op("dve", lambda e: e.tensor_copy(out=tri[:, :], in_=trif[:, :]), reads=("sb_trif",), writes=("sb_tri",))
        S.op("dve", lambda e: e.memset(one_col[:, :], 1.0), writes=("sb_one",))
        S.op("dve", lambda e: e.tensor_copy(out=brow[:, :, :], in_=bcol[:, :].unsqueeze(2).broadcast_to([128, NH, NS])),
             reads=("sb_bcol",), writes=("sb_brow",))
        S.op("dve", lambda e: e.tensor_copy(out=m8[:, :, :], in_=mlt[:, 0:NS].unsqueeze(1).broadcast_to([128, NH, NS])),
             reads=("sb_mlt",), writes=("sb_m8",))
        SBK = (4, 5)
        with ExitStack() as aes:
            B_ = lambda name, shape, dt: aes.enter_context(nc.sbuf_tensor(f"{name}_L{L}", shape, dt))
            xt = B_("sa_x", [128, NCH, 520], F32)
            hb = B_("sa_hb", [128, NCH, 520], BF16)
            sq = [B_(f"sa_sq{i}", [128, 520], F32) for i in range(2)]
            rstd = B_("sa_rstd", [128, 520], F32)
            wqk = [B_(f"sa_wqk{i}", [128, NCH, 256], BF16) for i in range(2)]
            wv = [B_(f"sa_wv{i}", [128, NCH, 512], BF16) for i in range(2)]
            hsq = [B_(f"sa_hsq{i}", [128, 520], F32) for i in range(2)]
            hrs = [B_(f"sa_hrs{i}", [128, 520], F32) for i in range(2)]
            knf = [B_(f"sa_knf{i}", [128, 520], F32) for i in range(3)]
            qkb = [B_(f"sa_qkb{i}", [128, 520], BF16) for i in range(2)]
            ktok = B_("sa_ktok", [128, 4, D], F32)
            kstok = B_("sa_kstok", [NS, D], F32)
            vtok = [B_(f"sa_vtok{i}", [128, D], F32) for i in range(2)]
            gcol = k.g_mix[:, L * NCH:(L + 1) * NCH]
            nw = 0
            nv = 0
            nh_ = 0
            for ti in range(cfg.ntile):
                c0, ncols = tile_cols(cfg, ti)
                segs = mm_cols(ncols)
                last = ncols > 512
                S.dma("sp", lambda e, c0=c0, n=ncols: e.dma_start(out=xt[:, :, 0:n], in_=k.XT[:, :, c0:c0 + n]),
                      reads=(("XT", ti),), writes=("sa_x",))
                rmsnorm_tile(k, xt, "sa_x", ncols, gcol, hb, "sa_hb", sq, 6, (7, 0), rstd)
                def a1(fh, par, slot, half, f2):
                    if half == 0:
                        S.dma("pool", lambda e: e.dma_start(out=wqk[slot][:, :, :], in_=Wqkv[:, :, f2 * 256:(f2 + 1) * 256]),
                              writes=(("sa_wqk", slot),))
                    bank = par
                    for kc in range(NCH):
                        for (a, b, smp) in segs:
                            o = k.ps[SBK[par]][:, 0:NS] if smp else k.ps[bank][:, 0:512]
                            tok = ("ps", SBK[par]) if smp else ("ps", bank)
                            S.op("pe", lambda e, o=o, kc=kc, a=a, b=b: e.matmul(
                                out=o, lhsT=wqk[slot][:, kc, half * 128:(half + 1) * 128], rhs=hb[:, kc, a:b],
                                start=(kc == 0), stop=(kc == NCH - 1)),
                                reads=(("sa_wqk", slot), "sa_hb"), writes=(tok,))

                def a2(fh, par, ks):
                    isk = fh >= NH
                    h = fh % NH
                    bank = par
                    for (a, b, smp) in segs:
                        pin = k.ps[SBK[par]][:, 0:NS] if smp else k.ps[bank][:, 0:512]
                        ptok = ("ps", SBK[par]) if smp else ("ps", bank)
                        sso = k.ps[SBK[par]][:, 8:8 + NS] if smp else k.ps[2 + par][:, 0:512]
                        sstok = ("ps", SBK[par]) if smp else ("ps", 2 + par)
                        S.op("act", lambda e, pin=pin, a=a, b=b: e.activation(out=hsq[par][:, a:b], in_=pin, func=AF.Square),
                             reads=(ptok,), writes=(("sa_hsq", par, smp),))
                        S.op("pe", lambda e, sso=sso, a=a, b=b: e.matmul(
                            out=sso, lhsT=k.ones_f[:, :], rhs=hsq[par][:, a:b], start=True, stop=True),
                            reads=(("sa_hsq", par, smp), "ones_f"), writes=(sstok,))
                        S.op("act", lambda e, sso=sso, a=a, b=b: e.activation(
                            out=hrs[par][:, a:b], in_=sso, func=AF.Sqrt, scale=1.0 / 128.0, bias=k.eps_col[:, 0:1]),
                            reads=(sstok,), writes=(("sa_hrs", par, smp),))
                        S.op("dve", lambda e, a=a, b=b: e.reciprocal(out=hrs[par][:, a:b], in_=hrs[par][:, a:b]),
                             reads=(("sa_hrs", par, smp),), writes=(("sa_hrs", par, smp),))
                        if isk:
                            S.op("dve", lambda e, pin=pin, a=a, b=b: e.scalar_tensor_tensor(
                                out=knf[ks][:, a:b], in0=pin, scalar=gk[:, 0:1], in1=hrs[par][:, a:b], op0=ALU.mult, op1=ALU.mult),
                                reads=(ptok, ("sa_hrs", par, smp), "sb_gk"), writes=(("sa_knf", ks, smp),))
                            S.op("act", lambda e, a=a, b=b: e.copy(out=qkb[par][:, a:b], in_=knf[ks][:, a:b]),
                                 reads=(("sa_knf", ks, smp),), writes=(("sa_qkb", par, smp),))
                        else:
                            S.op("dve", lambda e, pin=pin, a=a, b=b: e.scalar_tensor_tensor(
                                out=qkb[par][:, a:b], in0=pin, scalar=gq[:, 0:1], in1=hrs[par][:, a:b], op0=ALU.mult, op1=ALU.mult),
                                reads=(ptok, ("sa_hrs", par, smp), "sb_gq"), writes=(("sa_qkb", par, smp),))
                        if smp:
                            dst = ksT if isk else qsT
                            S.op("act", lambda e, dst=dst: e.copy(out=dst[:, h, :], in_=qkb[par][:, 512:520]),
                                 reads=(("sa_qkb", par, smp),), writes=("sb_ksT" if isk else "sb_qsT",))
                    dsc = k.KTs if isk else k.QTs
                    S.dma("sp", lambda e: e.dma_start(out=dsc[h, :, c0:c0 + 512], in_=qkb[par][:, 0:512]),
                          reads=(("sa_qkb", par, False),), writes=(("qk_scr", isk, h, ti),))

                def a3(fh, par, ks):
                    isk = fh >= NH
                    h = fh % NH
                    if not isk:
                        return
                    for tb in range(4):
                        S.op("pe", lambda e, tb=tb: e.transpose(
                            out=k.ps[6][:, tb * 128:(tb + 1) * 128], in_=knf[ks][:, tb * 128:(tb + 1) * 128], identity=k.ident[:, :]),
                            reads=(("sa_knf", ks, False),), writes=(("ps", 6),))
                    S.op("act", lambda e: e.copy(
                        out=ktok[:, :, h * 128:(h + 1) * 128], in_=k.ps[6][:, :].rearrange("p (b d) -> p b d", b=4)),
                        reads=(("ps", 6),), writes=("sa_ktok",))
                    if last:
                        S.op("pe", lambda e: e.transpose(
                            out=k.ps[7][0:NS, 0:128], in_=knf[ks][:, 512:520], identity=k.ident[:, :]),
                            reads=(("sa_knf", ks, True),), writes=(("ps", 7),))
                        S.op("act", lambda e: e.copy(out=kstok[:, h * 128:(h + 1) * 128], in_=k.ps[7][0:NS, 0:128]),
                             reads=(("ps", 7),), writes=("sa_kstok",))

                atasks = []
                for f2 in range(NH):
                    slot = nw % 2
                    nw += 1
                    for half in range(2):
                        atasks.append(dict(fh=f2 * 2 + half, par=nh_ % 2, ks=nh_ % 3, slot=slot, half=half, f2=f2))
                        nh_ += 1
                ASKEW = getattr(cfg, "sa_skew", 0)
                for n in range(len(atasks) + 2 * ASKEW):
                    if n < len(atasks):
                        t_ = atasks[n]
                        a1(t_["fh"], t_["par"], t_["slot"], t_["half"], t_["f2"])
                    if 0 <= n - ASKEW < len(atasks):
                        t_ = atasks[n - ASKEW]
                        a2(t_["fh"], t_["par"], t_["ks"])
                    if 0 <= n - 2 * ASKEW < len(atasks):
                        t_ = atasks[n - 2 * ASKEW]
                        a3(t_["fh"], t_["par"], t_["ks"])
                S.dma("sp", lambda e, c0=c0: e.dma_start(
                    out=k.new_k_prompt[j, c0:c0 + 512, :].rearrange("(b p) f -> p b f", p=128), in_=ktok[:, :, :]),
                    reads=("sa_ktok",), writes=(("kout", ti),))
                if last:
                    S.dma("sp", lambda e: e.dma_start(out=k.new_k_sample[j], in_=kstok[:, :]), reads=("sa_kstok",), writes=("ksout",))
                blocks = [(128 * b4, 128) for b4 in range(4)] + ([(512, NS)] if last else [])
                for vs4 in range(4):
                    slot = nv % 2
                    nv += 1
                    S.dma("pool", lambda e, slot=slot, vs4=vs4: e.dma_start(
                        out=wv[slot][:, :, :], in_=Wqkv[:, :, 4096 + vs4 * 512:4096 + (vs4 + 1) * 512]),
                        writes=(("sa_wv", slot),))
                    for bi, (b0, nr) in enumerate(blocks):
                        bank = bi if bi < 4 else 7
                        for kc in range(NCH):
                            S.op("pe", lambda e, bank=bank, nr=nr, b0=b0, kc=kc, slot=slot: e.matmul(
                                out=k.ps[bank][0:nr, 0:512], lhsT=hb[:, kc, b0:b0 + nr], rhs=wv[slot][:, kc, :],
                                start=(kc == 0), stop=(kc == NCH - 1)),
                                reads=(("sa_wv", slot), "sa_hb"), writes=(("ps", bank),))
                        vslot = bi % 2
                        eng = "act" if bi % 2 == 0 else "dve"
                        if eng == "act":
                            S.op("act", lambda e, bank=bank, nr=nr, vslot=vslot, vs4=vs4: e.copy(
                                out=vtok[vslot][0:nr, vs4 * 512:(vs4 + 1) * 512], in_=k.ps[bank][0:nr, 0:512]),
                                reads=(("ps", bank),), writes=(("sa_vtok", vslot, vs4),))
                        else:
                            S.op("dve", lambda e, bank=bank, nr=nr, vslot=vslot, vs4=vs4: e.tensor_copy(
                                out=vtok[vslot][0:nr, vs4 * 512:(vs4 + 1) * 512], in_=k.ps[bank][0:nr, 0:512]),
                                reads=(("ps", bank),), writes=(("sa_vtok", vslot, vs4),))
                        if bi == 4:
                            S.op("act", lambda e, vslot=vslot, vs4=vs4: e.copy(
                                out=vs[0:NS, vs4 * 512:(vs4 + 1) * 512], in_=vtok[vslot][0:NS, vs4 * 512:(vs4 + 1) * 512]),
                                reads=(("sa_vtok", vslot, vs4),), writes=("sb_vs",))
                        dst = k.new_v_sample[j][:, vs4 * 512:(vs4 + 1) * 512] if bi == 4 else \
                            k.new_v_prompt[j, c0 + b0:c0 + b0 + 128, vs4 * 512:(vs4 + 1) * 512]
                        S.dma("sp", lambda e, dst=dst, nr=nr, vslot=vslot, vs4=vs4: e.dma_start(
                            out=dst, in_=vtok[vslot][0:nr, vs4 * 512:(vs4 + 1) * 512]),
                            reads=(("sa_vtok", vslot, vs4),), writes=(("vout", ti, bi, vs4),))
                S.flush(k.block)
        S.barrier()
        with ExitStack() as bes:
            B_ = lambda name, shape, dt: bes.enter_context(nc.sbuf_tensor(f"{name}_L{L}", shape, dt))
            NSL = 3
            qh = [B_(f"sbq{i}", [128, T], BF16) for i in range(2)]
            kh = [B_(f"sbk{i}", [128, T], BF16) for i in range(2)]
            vh = [B_(f"sbv{i}", [128, T // 128, 128], BF16) for i in range(2)]
            E = [B_(f"sbE{i}", [128, 512], BF16) for i in range(NSL)]
            X = [B_(f"sbX{i}", [128, 512], BF16) for i in range(NSL)]
            W = [B_(f"sbW{i}", [128, 512], BF16) for i in range(NSL)]
            SPR = [B_(f"sbSP{i}", [128, 512], BF16) for i in range(32)]
            ones_b = B_("sb_onesb", [128, 128], BF16)
            mltb = B_("sb_mltb", [128, 128], BF16)
            oT = [B_(f"sboT{i}", [128, 512], BF16) for i in range(2)]
            S.op("dve", lambda e: e.memset(ones_b[:, :], 1.0), writes=("sb_onesb",))
            S.op("dve", lambda e: e.tensor_copy(out=mltb[:, :], in_=mlt[:, :]), reads=("sb_mlt",), writes=("sb_mltb",))
            def load_head(h):
                hs_ = h % 2
                S.dma("sp", lambda e: e.dma_start(out=qh[hs_][:, :], in_=k.QTs[h, :, 0:T]),
                      reads=tuple(("qk_scr", False, h, ti) for ti in range(cfg.ntile)), writes=(("sbq", hs_),))
                S.dma("sp", lambda e: e.dma_start(out=kh[hs_][:, :], in_=k.KTs[h, :, 0:T]),
                      reads=tuple(("qk_scr", True, h, ti) for ti in range(cfg.ntile)), writes=(("sbk", hs_),))
                S.dma("pool", lambda e: e.dma_start(
                    out=vh[hs_][:, :, :], in_=k.new_v_prompt[j, :, h * 128:(h + 1) * 128].rearrange("(b p) d -> p b d", p=128)),
                    writes=(("sbv", hs_),))

            tasks = []
            npair = 0
            nq = 0
            for h in range(NH if "B" not in getattr(cfg, "sb_skip", "") else 0):
                for qt in range(cfg.ntile):
                    obank = 6 + nq % 2
                    osl = nq % 2
                    nq += 1
                    kb_hi = qt * 4 + 3
                    prev = []
                    for i, kb in enumerate(range(kb_hi, -1, -1)):
                        Q0 = qt * 512
                        lo = max(kb * 128, Q0) - Q0
                        t_ = dict(h=h, hs_=h % 2, qt=qt, Q0=Q0, obank=obank, osl=osl, kb=kb, kb_hi=kb_hi, i=i, lo=lo,
                                  diag=(kb * 128 >= Q0), sl=npair % NSL, prev=list(prev),
                                  first_of_head=(qt == 0 and i == 0), prefetch=(qt == 0 and i == 3), spi=(nq % 2) * 16 + i)
                        prev.append((t_["spi"], lo))
                        npair += 1
                        tasks.append(t_)

            def st1(t_):
                h, hs_, kb, lo, Q0, sl = t_["h"], t_["hs_"], t_["kb"], t_["lo"], t_["Q0"], t_["sl"]
                if t_["first_of_head"] and h == 0:
                    load_head(0)
                if t_["prefetch"] and h + 1 < NH:
                    load_head(h + 1)
                zb = sl
                sp = SPR[t_["spi"]]
                sptok = ("sbSP", t_["spi"])
                S.op("pe", lambda e: e.matmul(
                    out=k.ps[zb][:, lo:512], lhsT=kh[hs_][:, kb * 128:(kb + 1) * 128], rhs=qh[hs_][:, Q0 + lo:Q0 + 512],
                    start=True, stop=True), reads=(("sbk", hs_), ("sbq", hs_)), writes=(("ps", zb),))
                S.op("act", lambda e: e.activation(
                    out=E[sl][:, lo:512], in_=k.ps[zb][:, lo:512], func=AF.Exp, scale=SCALE, bias=bcol[:, h:h + 1]),
                    reads=(("ps", zb), "sb_bcol"), writes=(("sbE", sl),))
                S.op("act", lambda e: e.activation(
                    out=sp[:, lo:512], in_=E[sl][:, lo:512], func=AF.Ln, scale=1.0, bias=one_col[:, 0:1]),
                    reads=(("sbE", sl), "sb_one"), writes=(sptok,))
                if t_["diag"]:
                    S.op("pool", lambda e: e.tensor_tensor(
                        out=sp[:, lo:lo + 128], in0=sp[:, lo:lo + 128], in1=mltb[:, :], op=ALU.mult),
                        reads=(sptok, "sb_mltb"), writes=(sptok,))

            def st2(t_):
                lo, sl, prev = t_["lo"], t_["sl"], t_["prev"]
                sbk_ = 3 + sl
                sp = SPR[t_["spi"]]
                sptok = ("sbSP", t_["spi"])
                S.op("pe", lambda e: e.matmul(
                    out=k.ps[sbk_][:, lo:512], lhsT=tri[:, :], rhs=sp[:, lo:512], start=True, stop=(len(prev) == 0),
                    skip_group_check=True), reads=("sb_tri", sptok), writes=(("ps", sbk_),))
                for pi_, (pidx, plo) in enumerate(prev):
                    S.op("pe", lambda e, pidx=pidx, plo=plo, lastp=(pi_ == len(prev) - 1): e.matmul(
                        out=k.ps[sbk_][:, plo:512], lhsT=ones_b[:, :], rhs=SPR[pidx][:, plo:512], start=False, stop=lastp,
                        skip_group_check=True), reads=("sb_onesb", ("sbSP", pidx)), writes=(("ps", sbk_),))
                S.op("act", lambda e: e.activation(
                    out=X[sl][:, lo:512], in_=k.ps[sbk_][:, lo:512], func=AF.Exp, scale=-1.0),
                    reads=(("ps", sbk_),), writes=(("sbX", sl),))
                if t_["diag"]:
                    S.op("pool", lambda e: e.tensor_tensor(
                        out=X[sl][:, lo:lo + 128], in0=X[sl][:, lo:lo + 128], in1=mltb[:, :], op=ALU.mult),
                        reads=(("sbX", sl), "sb_mltb"), writes=(("sbX", sl),))
                S.op("dve", lambda e: e.tensor_tensor(
                    out=W[sl][:, lo:512], in0=E[sl][:, lo:512], in1=X[sl][:, lo:512], op=ALU.mult),
                    reads=(("sbE", sl), ("sbX", sl)), writes=(("sbW", sl),))

            def st3(t_):
                h, hs_, kb, lo, Q0, sl, obank, osl = (t_[x] for x in ("h", "hs_", "kb", "lo", "Q0", "sl", "obank", "osl"))
                S.op("pe", lambda e: e.matmul(
                    out=k.ps[obank][:, lo:512], lhsT=vh[hs_][:, kb, :], rhs=W[sl][:, lo:512],
                    start=(kb == t_["kb_hi"]), stop=(kb == 0), skip_group_check=True),
                    reads=(("sbv", hs_), ("sbW", sl)), writes=(("ps", obank),))
                if kb == 0:
                    S.op("act", lambda e: e.copy(out=oT[osl][:, :], in_=k.ps[obank][:, :]),
                         reads=(("ps", obank),), writes=(("sboT", osl),))
                    S.dma("sp", lambda e: e.dma_start(out=k.OTs[:, h, Q0:Q0 + 512], in_=oT[osl][:, :]),
                          reads=(("sboT", osl),), writes=(("o_scr", h, t_["qt"]),))

            nt_ = len(tasks)
            for n in range(nt_ + 2):
                if n < nt_:
                    st1(tasks[n])
                if 0 <= n - 1 < nt_:
                    st2(tasks[n - 1])
                if 0 <= n - 2 < nt_:
                    st3(tasks[n - 2])
                if n % 16 == 15:
                    S.flush(k.block)
            S.flush(k.block)
        S.barrier()
        with ExitStack() as ces:
            B_ = lambda name, shape, dt: ces.enter_context(nc.sbuf_tensor(f"{name}_L{L}", shape, dt))
            NP = cfg.npages
            ptb = B_("sc_ptb", [128, NP], I32)
            idx = B_("sc_idx", [128, NP], I32)
            iot = B_("sc_iot", [128, 1], I32)
            kp = [B_(f"sc_kp{i}", [128, D], F32) for i in range(2)]
            vp = [B_(f"sc_vp{i}", [128, D], BF16) for i in range(4)]
            ktp = [B_(f"sc_ktp{i}", [128, NH, 128], BF16) for i in range(2)]
            ZB = [B_(f"sc_zb{i}", [128, 128], F32) for i in range(4)]
            Es = [B_(f"sc_E{i}", [128, 128], F32) for i in range(4)]
            SPs = [B_(f"sc_SP{i}", [128, 128], BF16) for i in range(4)]
            Xs = [B_(f"sc_X{i}", [128, 128], F32) for i in range(4)]
            Ws = [B_(f"sc_W{i}", [128, 128], BF16) for i in range(4)]
            ACCs = B_("sc_ACC", [128, 128], F32)
            S.dma("sp", lambda e: e.dma_start(out=ptb[:, :], in_=k.page_table[0:1, :].broadcast_to([128, NP])), writes=("sc_ptb",))
            S.op("pool", lambda e: e.iota(iot[:, :], pattern=[[0, 1]], base=0, channel_multiplier=1), writes=("sc_iot",))
            S.op("pool", lambda e: e.tensor_scalar(out=idx[:, :], in0=ptb[:, :], scalar1=128, scalar2=None, op0=ALU.mult),
                 reads=("sc_ptb",), writes=("sc_idx",))
            S.op("pool", lambda e: e.tensor_tensor(out=idx[:, :], in0=idx[:, :], in1=iot[:, 0:1].broadcast_to([128, NP]), op=ALU.add),
                 reads=("sc_idx", "sc_iot"), writes=("sc_idx",))
            S.op("pool", lambda e: e.memset(ACCs[:, :], 0.0), writes=("sc_ACC",))
            OB = 7
            NC4 = 4
            first_pv = [True]
            ck = k.cache_k[j]
            cv = k.cache_v[j]

            def blk(bi):
                if bi == 0:
                    return dict(nk=NS, new=True, kt=lambda h: ksT[:, h, :], v=lambda h: vs[0:NS, h * 128:(h + 1) * 128],
                                ktoks=("sb_ksT",), vtoks=("sb_vs",), pg=None)
                pg = NP - bi
                ksl, vsl = bi % 2, bi % NC4
                return dict(nk=128, new=False, kt=lambda h: ktp[ksl][:, h, :], v=lambda h: vp[vsl][:, h * 128:(h + 1) * 128],
                            ktoks=(("sc_ktp", ksl),), vtoks=(("sc_vp", vsl),), pg=pg, ksl=ksl, vsl=vsl)

            def c1(bi):
                d_ = blk(bi)
                if d_["new"]:
                    return
                pg, ksl, vsl = d_["pg"], d_["ksl"], d_["vsl"]
                S.dma("pool", lambda e: e.indirect_dma_start(
                    out=kp[ksl][:, :], out_offset=None, in_=ck, in_offset=bass.IndirectOffsetOnAxis(ap=idx[:, pg:pg + 1], axis=0)),
                    reads=("sc_idx",), writes=(("sc_kp", ksl),))
                S.dma("pool", lambda e: e.indirect_dma_start(
                    out=vp[vsl][:, :], out_offset=None, in_=cv, in_offset=bass.IndirectOffsetOnAxis(ap=idx[:, pg:pg + 1], axis=0)),
                    reads=("sc_idx",), writes=(("sc_vp", vsl),))
                for g in range(4):
                    bank = g % 2
                    for cc in range(4):
                        h = g * 4 + cc
                        S.op("pe", lambda e, bank=bank, cc=cc, h=h: e.transpose(
                            out=k.ps[bank][:, cc * 128:(cc + 1) * 128], in_=kp[ksl][:, h * 128:(h + 1) * 128], identity=k.ident[:, :]),
                            reads=(("sc_kp", ksl),), writes=(("ps", bank),))
                    if g % 2 == 0:
                        S.op("act", lambda e, bank=bank, g=g: e.copy(
                            out=ktp[ksl][:, g * 4:(g + 1) * 4, :], in_=k.ps[bank][:, :].rearrange("p (a b) -> p a b", a=4)),
                            reads=(("ps", bank),), writes=(("sc_ktp", ksl),))
                    else:
                        S.op("dve", lambda e, bank=bank, g=g: e.tensor_copy(
                            out=ktp[ksl][:, g * 4:(g + 1) * 4, :], in_=k.ps[bank][:, :].rearrange("p (a b) -> p a b", a=4)),
                            reads=(("ps", bank),), writes=(("sc_ktp", ksl),))

            def c2(bi):
                d_ = blk(bi)
                nk, sl = d_["nk"], bi % NC4
                zbk = 2 + bi % 2
                for h in range(NH):
                    S.op("pe", lambda e, h=h: e.matmul(
                        out=k.ps[zbk][0:nk, h * NS:(h + 1) * NS], lhsT=d_["kt"](h), rhs=qsT[:, h, :], start=True, stop=True),
                        reads=d_["ktoks"] + ("sb_qsT",), writes=(("ps", zbk),))
                S.op("dve", lambda e: e.scalar_tensor_tensor(
                    out=ZB[sl][0:nk, :], in0=k.ps[zbk][0:nk, 0:128], scalar=SCALE, in1=brow[0:nk, :, :].rearrange("p h t -> p (h t)"),
                    op0=ALU.mult, op1=ALU.add), reads=(("ps", zbk), "sb_brow"), writes=(("sc_zb", sl),))
                S.op("act", lambda e: e.activation(out=Es[sl][0:nk, :], in_=ZB[sl][0:nk, :], func=AF.Exp),
                     reads=(("sc_zb", sl),), writes=(("sc_E", sl),))
                S.op("act", lambda e: e.activation(out=SPs[sl][0:nk, :], in_=Es[sl][0:nk, :], func=AF.Ln, scale=1.0, bias=one_col[0:nk, 0:1]),
                     reads=(("sc_E", sl), "sb_one"), writes=(("sc_SP", sl),))
                if d_["new"]:
                    S.op("dve", lambda e: e.tensor_tensor(
                        out=SPs[sl][0:nk, :], in0=SPs[sl][0:nk, :], in1=m8[0:nk, :, :].rearrange("p h t -> p (h t)"), op=ALU.mult),
                        reads=(("sc_SP", sl), "sb_m8"), writes=(("sc_SP", sl),))

            def c3(bi):
                d_ = blk(bi)
                nk, sl = d_["nk"], bi % NC4
                sbk_ = 4 + bi % 2
                S.op("pe", lambda e: e.matmul(
                    out=k.ps[sbk_][0:nk, 0:128], lhsT=tri[0:nk, 0:nk], rhs=SPs[sl][0:nk, :], start=True, stop=False),
                    reads=("sb_tri", ("sc_SP", sl)), writes=(("ps", sbk_),))
                S.op("pe", lambda e: e.matmul(
                    out=k.ps[sbk_][0:nk, 0:128], lhsT=k.ones_f[:, 0:nk], rhs=ACCs[:, :], start=False, stop=True),
                    reads=("ones_f", "sc_ACC"), writes=(("ps", sbk_),))
                S.op("pool", lambda e: e.tensor_tensor(out=ACCs[0:nk, :], in0=ACCs[0:nk, :], in1=SPs[sl][0:nk, :], op=ALU.add),
                     reads=("sc_ACC", ("sc_SP", sl)), writes=("sc_ACC",))
                S.op("act", lambda e: e.activation(out=Xs[sl][0:nk, :], in_=k.ps[sbk_][0:nk, 0:128], func=AF.Exp, scale=-1.0),
                     reads=(("ps", sbk_),), writes=(("sc_X", sl),))
                if d_["new"]:
                    S.op("dve", lambda e: e.tensor_tensor(
                        out=Xs[sl][0:nk, :], in0=Xs[sl][0:nk, :], in1=m8[0:nk, :, :].rearrange("p h t -> p (h t)"), op=ALU.mult),
                        reads=(("sc_X", sl), "sb_m8"), writes=(("sc_X", sl),))
                S.op("dve", lambda e: e.tensor_tensor(out=Ws[sl][0:nk, :], in0=Es[sl][0:nk, :], in1=Xs[sl][0:nk, :], op=ALU.mult),
                     reads=(("sc_E", sl), ("sc_X", sl)), writes=(("sc_W", sl),))

            def c4(bi, final):
                d_ = blk(bi)
                nk, sl = d_["nk"], bi % NC4
                for h in range(NH):
                    st = first_pv[0] and h == 0
                    S.op("pe", lambda e, h=h, st=st: e.matmul(
                        out=k.ps[OB][:, h * NS:(h + 1) * NS], lhsT=d_["v"](h), rhs=Ws[sl][0:nk, h * NS:(h + 1) * NS],
                        start=st, stop=(final and h == NH - 1), skip_group_check=True),
                        reads=d_["vtoks"] + (("sc_W", sl),), writes=(("ps", OB),))
                first_pv[0] = False

            NB = NP + 1
            for n in range(NB + 3):
                if n < NB:
                    c1(n)
                if 0 <= n - 1 < NB:
                    c2(n - 1)
                if 0 <= n - 2 < NB:
                    c3(n - 2)
                if 0 <= n - 3 < NB:
                    c4(n - 3, n - 3 == NB - 1)
                if n % 8 == 7:
                    S.flush(k.block)
            S.op("act", lambda e: e.copy(out=osT[:, :, :], in_=k.ps[OB][:, 0:128].rearrange("p (h t) -> p h t", h=NH)),
                 reads=(("ps", OB),), writes=("sb_osT",))
            S.flush(k.block)
        S.barrier()
        with ExitStack() as des:
            B_ = lambda name, shape, dt: des.enter_context(nc.sbuf_tensor(f"{name}_L{L}", shape, dt))
            ot = B_("sd_o", [128, NH, 520], BF16)
            wo = [B_(f"sd_wo{i}", [128, NCH, 256], BF16) for i in range(2)]
            xr = [B_(f"sd_xr{i}", [128, 520], F32) for i in range(2)]
            nw = 0
            for ti in range(cfg.ntile if "D" not in getattr(cfg, "sb_skip", "") else 0):
                c0, ncols = tile_cols(cfg, ti)
                segs = mm_cols(ncols)
                last = ncols > 512
                S.dma("sp", lambda e, c0=c0: e.dma_start(out=ot[:, :, 0:512], in_=k.OTs[:, :, c0:c0 + 512]),
                      reads=tuple(("o_scr", h, ti) for h in range(NH)), writes=("sd_o",))
                if last:
                    S.op("dve", lambda e: e.tensor_copy(out=ot[:, :, 512:520], in_=osT[:, :, :]), reads=("sb_osT",), writes=("sd_o",))
                for oc2 in range(NCH // 2):
                    slot = nw % 2
                    nw += 1
                    S.dma("pool", lambda e, slot=slot, oc2=oc2: e.dma_start(out=wo[slot][:, :, :], in_=Wo[:, :, oc2 * 256:(oc2 + 1) * 256]),
                          writes=(("sd_wo", slot),))
                    for half in range(2):
                        oc = oc2 * 2 + half
                        par = oc % 2
                        S.dma("sp", lambda e, par=par, oc=oc, c0=c0, n=ncols: e.dma_start(out=xr[par][:, 0:n], in_=k.XT[:, oc, c0:c0 + n]),
                              reads=(), writes=(("sd_xr", par),))
                        for kc in range(NCH):
                            for (a, b, smp) in segs:
                                o = k.ps[SBK[par]][:, 0:NS] if smp else k.ps[par][:, 0:512]
                                tok = ("ps", SBK[par]) if smp else ("ps", par)
                                S.op("pe", lambda e, o=o, slot=slot, kc=kc, half=half, a=a, b=b: e.matmul(
                                    out=o, lhsT=wo[slot][:, kc, half * 128:(half + 1) * 128], rhs=ot[:, kc, a:b],
                                    start=(kc == 0), stop=(kc == NCH - 1)),
                                    reads=(("sd_wo", slot), "sd_o"), writes=(tok,))
                        for (a, b, smp) in segs:
                            o = k.ps[SBK[par]][:, 0:NS] if smp else k.ps[par][:, 0:512]
                            tok = ("ps", SBK[par]) if smp else ("ps", par)
                            S.op("dve", lambda e, o=o, par=par, a=a, b=b: e.tensor_tensor(
                                out=xr[par][:, a:b], in0=xr[par][:, a:b], in1=o, op=ALU.add),
                                reads=(tok, ("sd_xr", par)), writes=(("sd_xr", par),))
                        S.dma("sp", lambda e, par=par, oc=oc, c0=c0, n=ncols: e.dma_start(out=k.XT[:, oc, c0:c0 + n], in_=xr[par][:, 0:n]),
                              reads=(("sd_xr", par),), writes=(("XT", ti),))
                S.flush(k.block)
    S.barrier()


def phase_transpose_out(k):
    nc, S, cfg = k.nc, k.S, k.cfg
    with ExitStack() as es:
        xi = [es.enter_context(nc.sbuf_tensor(f"to_in{i}", [128, NCH, 520], F32)) for i in range(2)]
        xo = [es.enter_context(nc.sbuf_tensor(f"to_out{i}", [128, D], F32)) for i in range(2)]
        nb = 0
        for ti in range(cfg.ntile):
            c0, ncols = tile_cols(cfg, ti)
            xin = xi[ti % 2]
            itok = ("to_in", ti % 2)
            S.dma("sp", lambda e, xin=xin, c0=c0, n=ncols: e.dma_start(out=xin[:, :, 0:n], in_=k.XT[:, :, c0:c0 + n]),
                  reads=(("XT", ti),), writes=(itok,))
            blocks = [(c0 + 128 * j, 128, 128 * j, False) for j in range(4)]
            if ncols > 512:
                blocks.append((0, NS, 512, True))
            for (r0, nr, ic0, smp) in blocks:
                slot = nb % 2
                nb += 1
                otok = ("to_out", slot)
                for g in range(4):
                    bank = g
                    for cc in range(4):
                        c = g * 4 + cc
                        S.op("pe", lambda e, b=bank, cc=cc, c=c, nr=nr, ic0=ic0, xin=xin: e.transpose(
                            out=k.ps[b][0:nr, cc * 128:(cc + 1) * 128], in_=xin[:, c, ic0:ic0 + nr],
                            identity=k.ident[:, :]),
                            reads=(itok,), writes=(("ps", bank),))
                    eng = "act" if g % 2 == 0 else "dve"
                    src_ap = k.ps[bank][0:nr, :]
                    dst_ap = xo[slot][0:nr, g * 512:(g + 1) * 512]
                    if eng == "act":
                        S.op("act", lambda e, s=src_ap, d=dst_ap: e.copy(out=d, in_=s),
                             reads=(("ps", bank),), writes=(otok,))
                    else:
                        S.op("dve", lambda e, s=src_ap, d=dst_ap: e.tensor_copy(out=d, in_=s),
                             reads=(("ps", bank),), writes=(otok,))
                dst = k.y_sample[0:NS, :] if smp else k.y_prompt[r0:r0 + nr, :]
                S.dma("sp", lambda e, d=dst, slot=slot, nr=nr: e.dma_start(out=d, in_=xo[slot][0:nr, :]),
                      reads=(otok,), writes=(("yout", nb),))
        S.flush(k.block)
    S.barrier()


def build(cfg):
    nc = bass.Bass("TRN2", target_bir_lowering=False)
    k = K()
    k.nc, k.cfg = nc, cfg
    T = cfg.T
    dram = lambda name, shape, dt, kind: nc.dram_tensor(name, list(shape), dt, kind=kind).ap()
    k.x_prompt = dram("x_prompt", [T, D], F32, "ExternalInput")
    k.x_sample = dram("x_sample", [NS, D], F32, "ExternalInput")
    k.norm_mix = dram("norm_mix", [len(cfg.kinds), D], F32, "ExternalInput")
    k.norm_ffn = dram("norm_ffn", [len(cfg.kinds), D], F32, "ExternalInput")
    k.w_gate = dram("w_ffn_gate", [len(cfg.kinds), D, DFF], F32, "ExternalInput")
    k.w_up = dram("w_ffn_up", [len(cfg.kinds), D, DFF], F32, "ExternalInput")
    k.w_down = dram("w_ffn_down", [len(cfg.kinds), DFF, D], F32, "ExternalInput")
    k.c_ident = dram("c_ident", [128, 128], F32, "ExternalInput")
    k.c_invc = dram("c_invc", [128, NCH, PBUF], F32, "ExternalInput")
    npl = max(cfg.n_pool, 1)
    k.cache_pool = dram("cache_pool", [npl, PBUF, D], F32, "ExternalInput")
    k.w_pool = dram("w_pool", [npl, 4, 512, 512], F32, "ExternalInput")
    k.pool_scale = dram("pool_scale", [npl, D], F32, "ExternalInput")
    nss = max(cfg.n_ssm, 1)
    k.state_re = dram("state_ssm_re", [nss, 128, 64], F32, "ExternalInput")
    k.state_im = dram("state_ssm_im", [nss, 128, 64], F32, "ExternalInput")
    k.ssm_a_re = dram("ssm_a_re", [nss, 128, 64], F32, "ExternalInput")
    k.ssm_a_im = dram("ssm_a_im", [nss, 128, 64], F32, "ExternalInput")
    k.ssm_log_dt = dram("ssm_log_dt", [nss, 128], F32, "ExternalInput")
    k.ssm_d = dram("ssm_d", [nss, D], F32, "ExternalInput")
    k.b_re_w = dram("b_re_w", [nss, 128, NCH, 128], F32, "ExternalInput")
    k.b_im_w = dram("b_im_w", [nss, 128, NCH, 128], F32, "ExternalInput")
    k.c_re_w = dram("c_re_w", [nss, 128, 64, 64], F32, "ExternalInput")
    k.c_im_w = dram("c_im_w", [nss, 128, 64, 64], F32, "ExternalInput")
    k.c_msk = dram("c_msk", [128, 2], F32, "ExternalInput")
    k.c_sel = dram("c_sel", [64, NCH, 128], F32, "ExternalInput")
    k.w_glu_a = dram("w_glu_a", [nss, D, D], F32, "ExternalInput")
    k.w_glu_b = dram("w_glu_b", [nss, D, D], F32, "ExternalInput")
    k.new_ssm_re_prompt = dram("new_ssm_re_prompt", [nss, 128, 64], F32, "ExternalOutput")
    k.new_ssm_im_prompt = dram("new_ssm_im_prompt", [nss, 128, 64], F32, "ExternalOutput")
    k.new_ssm_re_sample = dram("new_ssm_re_sample", [nss, 128, 64], F32, "ExternalOutput")
    k.new_ssm_im_sample = dram("new_ssm_im_sample", [nss, 128, 64], F32, "ExternalOutput")
    nsb = max(cfg.n_sb, 1)
    if cfg.n_sb > 0:
        k.cache_k = dram("cache_k", [nsb, cfg.nphys * 128, D], F32, "ExternalInput")
        k.cache_v = dram("cache_v", [nsb, cfg.nphys * 128, D], F32, "ExternalInput")
        k.page_table = dram("page_table", [1, max(cfg.npages, 1)], I32, "ExternalInput")
        k.w_qkv = dram("w_qkv", [nsb, D, 3 * D], F32, "ExternalInput")
        k.w_o = dram("w_o", [nsb, D, D], F32, "ExternalInput")
        k.sb_q_norm = dram("sb_q_norm", [nsb, 128], F32, "ExternalInput")
        k.sb_k_norm = dram("sb_k_norm", [nsb, 128], F32, "ExternalInput")
        k.sb_bias = dram("sb_bias", [nsb, 16], F32, "ExternalInput")
        k.c_tri = dram("c_tri", [128, 128], F32, "ExternalInput")
        k.c_mlt = dram("c_mlt", [128, 128], F32, "ExternalInput")
        k.new_k_prompt = dram("new_k_prompt", [nsb, T, D], F32, "ExternalOutput")
        k.new_v_prompt = dram("new_v_prompt", [nsb, T, D], F32, "ExternalOutput")
        k.new_k_sample = dram("new_k_sample", [nsb, NS, D], F32, "ExternalOutput")
        k.new_v_sample = dram("new_v_sample", [nsb, NS, D], F32, "ExternalOutput")
        k.QTs = dram("scr_qt", [16, 128, cfg.TT], BF16, "Internal")
        k.KTs = dram("scr_kt", [16, 128, cfg.TT], BF16, "Internal")
        k.OTs = dram("scr_ot", [128, 16, cfg.TT], BF16, "Internal")
    k.new_pool_prompt = dram("new_pool_prompt", [npl, PBUF, D], F32, "ExternalOutput")
    k.new_pool_sample = dram("new_pool_sample", [npl, PBUF, D], F32, "ExternalOutput")
    k.y_prompt = dram("y_prompt", [T, D], F32, "ExternalOutput")
    k.y_sample = dram("y_sample", [NS, D], F32, "ExternalOutput")
    k.XT = dram("scr_xt", [128, NCH, cfg.TT], F32, "Internal")

    with ExitStack() as es:
        S = Sched(nc, es)
        k.S = S
        A = lambda name, shape, dt: es.enter_context(nc.sbuf_tensor(name, shape, dt))
        k.ident = A("ident", [128, 128], F32)
        k.ones_f = A("ones_f", [128, 128], F32)
        k.eps_col = A("eps_col", [128, 1], F32)
        nl = len(cfg.kinds)
        k.g_mix = A("g_mix", [128, nl * NCH], F32)
        k.g_ffn = A("g_ffn", [128, nl * NCH], F32)
        k.invc = A("invc", [128, NCH, PBUF], F32)
        k.pscale = A("pscale", [128, npl * NCH], F32)
        k.ps = [es.enter_context(nc.psum_tensor(f"psb{i}", [128, 512], F32)) for i in range(8)]
        with nc.Block() as block:
            k.block = block
            S.dma("sp", lambda e: e.dma_start(out=k.ident[:, :], in_=k.c_ident), writes=("ident",))
            S.op("dve", lambda e: e.memset(k.ones_f[:, :], 1.0), writes=("ones_f",))
            S.op("dve", lambda e: e.memset(k.eps_col[:, :], EPS), writes=("eps",))
            with nc.allow_non_contiguous_dma(reason="tiny gain vectors"):
                S.dma("sp", lambda e: e.dma_start(
                    out=k.g_mix[:, :].rearrange("p (l c) -> p l c", l=nl),
                    in_=k.norm_mix.rearrange("l (c p) -> p l c", p=128)), writes=("g_mix",))
                S.dma("sp", lambda e: e.dma_start(
                    out=k.g_ffn[:, :].rearrange("p (l c) -> p l c", l=nl),
                    in_=k.norm_ffn.rearrange("l (c p) -> p l c", p=128)), writes=("g_ffn",))
                S.dma("sp", lambda e: e.dma_start(
                    out=k.pscale[:, :].rearrange("p (l c) -> p l c", l=npl),
                    in_=k.pool_scale.rearrange("l (c p) -> p l c", p=128)), writes=("pscale",))
                S.dma("sp", lambda e: e.dma_start(out=k.invc[:, :, :], in_=k.c_invc), writes=("invc",))
                S.flush(block)
            S.barrier()
            phase_transpose_in(k)
            cnt = [0, 0, 0]
            for L, kind in enumerate(cfg.kinds):
                if kind == 0:
                    phase_pool(k, L, cnt[0])
                if kind == 1:
                    phase_s5(k, L, cnt[1])
                if kind == 2:
                    phase_sb(k, L, cnt[2])
                if kind in (0, 1, 2):
                    cnt[kind] += 1
                if not cfg.skip_ffn:
                    phase_ffn(k, L)
            phase_transpose_out(k)
            S.barrier()
            S.flush(block)
    k.ninst = dict(S.ninst)
    return nc, k


def _consts():
    invc = np.zeros((128, NCH, PBUF), np.float32)
    for c in range(NCH):
        w = 2 << (c // 4)
        for t in range(PBUF):
            invc[:, c, t] = 1.0 / min(t + 1, w)
    sel = np.zeros((64, NCH, 128), np.float32)
    for jj in range(NCH):
        for p in range(128):
            sel[4 * jj + p // 32, jj, p] = 1.0
    msk = np.zeros((128, 2), np.float32)
    for p in range(128):
        msk[p, (p // 32) % 2] = 1.0
    s_ = np.arange(128)
    tri = (s_[:, None] >= s_[None, :]).astype(np.float32)
    mlt = (s_[:, None] < s_[None, :]).astype(np.float32)
    return {"c_ident": np.eye(128, dtype=np.float32), "c_invc": invc, "c_sel": sel, "c_msk": msk,
            "c_tri": tri, "c_mlt": mlt}


def _s5_layouts(b_re, b_im, c_re, c_im):
    Ls = b_re.shape[0]
    out = {}
    for name, b in (("b_re_w", b_re), ("b_im_w", b_im)):
        w = np.zeros((Ls, 128, NCH, 2, 64), np.float32)
        for p in range(128):
            q, parp, cc = p // 32, (p // 16) % 2, p % 16
            for jj in range(NCH):
                g = 8 * jj + 2 * q + parp
                w[:, p, jj, parp, :] = b[:, g, :, cc]
        out[name] = w.reshape(Ls, 128, NCH, 128)
    for name, c in (("c_re_w", c_re), ("c_im_w", c_im)):
        w = np.zeros((Ls, 2, 64, 64, 2, 2, 16), np.float32)
        for par in range(2):
            for gp in range(64):
                g = 2 * gp + par
                w[:, par, :, gp, gp % 2, par, :] = np.transpose(c[:, g, :, :], (0, 2, 1))
        out[name] = w.reshape(Ls, 128, 64, 64)
    return out


_BUILD_CACHE = {}


def kernel(x_prompt, x_sample, cache_pool, state_ssm_re, state_ssm_im, cache_k, cache_v, page_table,
           norm_mix, norm_ffn, w_ffn_gate, w_ffn_up, w_ffn_down, w_pool, pool_scale,
           ssm_a_re, ssm_a_im, ssm_b_re, ssm_b_im, ssm_c_re, ssm_c_im, ssm_d, ssm_log_dt,
           w_glu_a, w_glu_b, w_qkv, w_o, sb_q_norm, sb_k_norm, sb_bias):
    f32 = lambda a: np.ascontiguousarray(np.asarray(a), dtype=np.float32)
    x_prompt, x_sample = f32(x_prompt), f32(x_sample)
    B, T, _ = x_prompt.shape
    Bs = x_sample.shape[0]
    nphys = np.asarray(cache_k).shape[1]
    npages = np.asarray(page_table).shape[1]
    depth = np.asarray(norm_mix).shape[0]
    kinds = tuple(i % 3 for i in range(depth))
    cfg = Cfg(T=T, npages=npages, nphys=nphys, kinds=kinds)
    key = (T, npages, nphys, kinds)
    if key not in _BUILD_CACHE:
        _BUILD_CACHE[key] = build(cfg)
    nc, kk = _BUILD_CACHE[key]
    ncores = 8
    shared = {
        "norm_mix": f32(norm_mix), "norm_ffn": f32(norm_ffn),
        "w_ffn_gate": f32(w_ffn_gate), "w_ffn_up": f32(w_ffn_up), "w_ffn_down": f32(w_ffn_down),
        "w_pool": f32(w_pool), "pool_scale": f32(pool_scale),
        "ssm_a_re": f32(ssm_a_re), "ssm_a_im": f32(ssm_a_im), "ssm_d": f32(ssm_d), "ssm_log_dt": f32(ssm_log_dt),
        "w_glu_a": f32(w_glu_a), "w_glu_b": f32(w_glu_b), "w_qkv": f32(w_qkv), "w_o": f32(w_o),
        "sb_q_norm": f32(sb_q_norm), "sb_k_norm": f32(sb_k_norm), "sb_bias": f32(sb_bias),
        "cache_k": f32(cache_k).reshape(cfg.n_sb, nphys * 128, D),
        "cache_v": f32(cache_v).reshape(cfg.n_sb, nphys * 128, D),
    }
    shared.update(_consts())
    shared.update(_s5_layouts(f32(ssm_b_re), f32(ssm_b_im), f32(ssm_c_re), f32(ssm_c_im)))
    cache_pool, state_ssm_re, state_ssm_im = f32(cache_pool), f32(state_ssm_re), f32(state_ssm_im)
    pt = np.ascontiguousarray(np.asarray(page_table), dtype=np.int32)
    in_maps = []
    for c in range(ncores):
        b, s = c % B, c % Bs
        m = dict(shared)
        m["x_prompt"] = x_prompt[b]
        m["x_sample"] = x_sample[s]
        m["cache_pool"] = np.ascontiguousarray(cache_pool[:, s])
        m["state_ssm_re"] = np.ascontiguousarray(state_ssm_re[:, s])
        m["state_ssm_im"] = np.ascontiguousarray(state_ssm_im[:, s])
        m["page_table"] = pt[s:s + 1]
        in_maps.append(m)
    res = run_bass_kernel_spmd(nc, in_maps, core_ids=list(range(ncores)))
    R = res.results
    H, Dh = 16, 128
    pc = list(range(B))
    y_prompt = np.stack([R[c]["y_prompt"] for c in pc], 0)
    y_sample = np.stack([R[c]["y_sample"] for c in range(Bs)], 0)
    npp = np.stack([R[c]["new_pool_prompt"] for c in pc], 1)
    nps = np.stack([R[c]["new_pool_sample"] for c in range(Bs)], 1)
    srp = np.stack([R[c]["new_ssm_re_prompt"] for c in pc], 1)
    sip = np.stack([R[c]["new_ssm_im_prompt"] for c in pc], 1)
    srs = np.stack([R[c]["new_ssm_re_sample"] for c in range(Bs)], 1)
    sis = np.stack([R[c]["new_ssm_im_sample"] for c in range(Bs)], 1)
    kp = np.stack([R[c]["new_k_prompt"] for c in pc], 1).reshape(cfg.n_sb, B, T, H, Dh)
    vp = np.stack([R[c]["new_v_prompt"] for c in pc], 1).reshape(cfg.n_sb, B, T, H, Dh)
    ks = np.stack([R[c]["new_k_sample"] for c in range(Bs)], 1).reshape(cfg.n_sb, Bs, NS, H, Dh)
    vs = np.stack([R[c]["new_v_sample"] for c in range(Bs)], 1).reshape(cfg.n_sb, Bs, NS, H, Dh)
    return (y_prompt, y_sample, npp, nps, srp, sip, srs, sis, kp, vp, ks, vs)
```

```python
from contextlib import ExitStack
import math
import numpy as np
import concourse.bass as bass
import concourse.mybir as mybir
from concourse.bass_utils import run_bass_kernel_spmd

F32 = mybir.dt.float32
BF16 = mybir.dt.bfloat16
I32 = mybir.dt.int32
AF = mybir.ActivationFunctionType
ALU = mybir.AluOpType

D = 2048
NCH = 16
DFF = 5632
NFC = 44
NS = 8
PBUF = 15
EPS = 1e-6
SEM_EPOCH = 30000


class Sched:
    ENGS = ("pe", "act", "dve", "pool", "sp")

    def __init__(self, nc, es, n_dma=24, n_spare=16):
        self.nc = nc
        self.eobj = {"pe": nc.tensor, "act": nc.scalar, "dve": nc.vector, "pool": nc.gpsimd, "sp": nc.sync}
        self.q = {e: [] for e in self.ENGS}
        self.spare = [es.enter_context(nc.semaphore(f"esem{i}")) for i in range(n_spare)]
        self.esem = {}
        self.epoch = {e: 0 for e in self.ENGS}
        self.cnt = {e: 0 for e in self.ENGS}
        for e in ("pe", "act", "dve", "pool"):
            self.esem[(e, 0)] = self.spare.pop()
        self.dsem = [es.enter_context(nc.semaphore(f"dsem{i}")) for i in range(2 * n_dma)]
        self.dcnt = [0] * (2 * n_dma)
        self.dpool = {"sp": list(range(0, n_dma)), "pool": list(range(n_dma, 2 * n_dma))}
        self.dnext = {"sp": 0, "pool": 0}
        self.waited = {e: {} for e in self.ENGS}
        self.lastw = {}
        self.readers = {}
        self.ninst = {e: 0 for e in self.ENGS}

    def _sem(self, key):
        if key[0] == "d":
            return self.dsem[key[1]]
        return self.esem[(key[1], key[2])]

    def _deps(self, reads, writes):
        deps = {}
        def add(d):
            if d is None:
                return
            k, v = d
            if deps.get(k, 0) < v:
                deps[k] = v
        for r in reads:
            add(self.lastw.get(r))
        for w in writes:
            add(self.lastw.get(w))
            for k, v in self.readers.get(w, {}).items():
                add((k, v))
        return deps

    def _emit_waits(self, eng, deps):
        for k, v in deps.items():
            if eng == "pe" and k[0] == "e" and k[1] == "pe":
                continue
            if self.waited[eng].get(k, 0) >= v:
                continue
            self.waited[eng][k] = v
            sem = self._sem(k)
            self.q[eng].append(lambda e, s=sem, vv=v: e.wait_ge(s, vv))
            self.ninst[eng] += 1

    def _record(self, me, reads, writes):
        k, v = me
        for r in reads:
            self.readers.setdefault(r, {})[k] = v
        for w in writes:
            self.lastw[w] = me
            self.readers[w] = {}

    def op(self, eng, fn, reads=(), writes=()):
        self._emit_waits(eng, self._deps(reads, writes))
        if self.cnt[eng] >= SEM_EPOCH:
            self.epoch[eng] += 1
            self.cnt[eng] = 0
            self.esem[(eng, self.epoch[eng])] = self.spare.pop()
        self.cnt[eng] += 1
        key = ("e", eng, self.epoch[eng])
        sem = self.esem[(eng, self.epoch[eng])]
        self.q[eng].append(lambda e, f=fn, s=sem: f(e).then_inc(s, 1))
        self.ninst[eng] += 1
        self._record((key, self.cnt[eng]), reads, writes)

    def dma(self, eng, fn, reads=(), writes=()):
        pool_ = self.dpool[eng]
        i = pool_[self.dnext[eng]]
        self.dnext[eng] = (self.dnext[eng] + 1) % len(pool_)
        deps = self._deps(reads, writes)
        if self.dcnt[i] > 0:
            k = ("d", i)
            if deps.get(k, 0) < self.dcnt[i]:
                deps[k] = self.dcnt[i]
        self._emit_waits(eng, deps)
        self.dcnt[i] += 16
        sem = self.dsem[i]
        self.q[eng].append(lambda e, f=fn, s=sem: f(e).then_inc(s, 16))
        self.ninst[eng] += 1
        self._record((("d", i), self.dcnt[i]), reads, writes)

    def barrier(self):
        for eng in self.ENGS:
            deps = {}
            for e2 in ("pe", "act", "dve", "pool"):
                if self.cnt[e2] > 0:
                    deps[("e", e2, self.epoch[e2])] = self.cnt[e2]
            for i, c in enumerate(self.dcnt):
                if c > 0:
                    deps[("d", i)] = c
            if eng == "pe":
                pass
            self._emit_waits(eng, deps)

    def flush(self, block):
        reg = {"pe": block.tensor, "act": block.scalar, "dve": block.vector, "pool": block.gpsimd, "sp": block.sync}
        for eng in self.ENGS:
            items = self.q[eng]
            self.q[eng] = []
            if not items:
                continue
            def body(e, items=items):
                for f in items:
                    f(e)
            reg[eng](body)


class Cfg:
    def __init__(self, T=2048, npages=128, nphys=1280, kinds=(0, 1, 2, 0)):
        self.T = T
        self.npages = npages
        self.nphys = nphys
        self.kinds = tuple(kinds)
        self.TT = T + NS
        self.ntile = T // 512
        self.n_pool = self.kinds.count(0)
        self.n_ssm = self.kinds.count(1)
        self.n_sb = self.kinds.count(2)
        self.skip_ffn = False


class K:
    pass


def tile_cols(cfg, i):
    n = 512 + (NS if i == cfg.ntile - 1 else 0)
    return 512 * i, n


def mm_cols(n):
    segs = [(0, 512, False)]
    if n > 512:
        segs.append((512, n, True))
    return segs


def phase_transpose_in(k):
    nc, S, cfg = k.nc, k.S, k.cfg
    with ExitStack() as es:
        xin = [es.enter_context(nc.sbuf_tensor(f"ti_in{i}", [128, D], F32)) for i in range(2)]
        xo = [es.enter_context(nc.sbuf_tensor(f"ti_out{i}", [128, NCH, 520], F32)) for i in range(2)]
        nblk = cfg.T // 128
        for ti in range(cfg.ntile):
            c0, ncols = tile_cols(cfg, ti)
            o = xo[ti % 2]
            otok = ("ti_out", ti % 2)
            blocks = [(c0 + 128 * j, 128, 128 * j, False) for j in range(4)]
            if ncols > 512:
                blocks.append((0, NS, 512, True))
            for bi, (r0, nr, oc0, smp) in enumerate(blocks):
                slot = (ti * 5 + bi) % 2
                src = k.x_sample[0:NS, :] if smp else k.x_prompt[r0:r0 + nr, :]
                S.dma("sp", lambda e, s=src, d=xin[slot], nr=nr: e.dma_start(out=d[0:nr, :], in_=s),
                      reads=(), writes=(("ti_in", slot),))
                for g in range(4):
                    bank = g
                    for cc in range(4):
                        c = g * 4 + cc
                        S.op("pe", lambda e, b=bank, cc=cc, c=c, slot=slot, nr=nr:
                             e.transpose(out=k.ps[b][:, cc * 128:cc * 128 + nr],
                                         in_=xin[slot][0:nr, c * 128:(c + 1) * 128],
                                         identity=k.ident[0:nr, 0:nr]),
                             reads=(("ti_in", slot),), writes=(("ps", bank),))
                    eng = "act" if g % 2 == 0 else "dve"
                    src_ap = k.ps[bank][:, :].rearrange("p (c t) -> p c t", c=4)[:, :, 0:nr]
                    dst_ap = o[:, g * 4:(g + 1) * 4, oc0:oc0 + nr]
                    if eng == "act":
                        S.op("act", lambda e, s=src_ap, d=dst_ap: e.copy(out=d, in_=s),
                             reads=(("ps", bank),), writes=(otok,))
                    else:
                        S.op("dve", lambda e, s=src_ap, d=dst_ap: e.tensor_copy(out=d, in_=s),
                             reads=(("ps", bank),), writes=(otok,))
            S.dma("sp", lambda e, o=o, c0=c0, n=ncols: e.dma_start(
                out=k.XT[:, :, c0:c0 + n], in_=o[:, :, 0:n]),
                reads=(otok,), writes=(("XT", ti),))
        S.flush(k.block)
    S.barrier()


def rmsnorm_tile(k, x, xtok, ncols, gcol, out_h, htok, sq, bank_ss, smp_cols, rstd, out_dtype_note=None):
    S = k.S
    segs = mm_cols(ncols)
    xtk = xtok if callable(xtok) else (lambda c: xtok)
    for c in range(NCH):
        s = sq[c % 2]
        stok = ("sq", id(sq), c % 2)
        S.op("act", lambda e, s=s, c=c: e.activation(out=s[:, 0:ncols], in_=x[:, c, 0:ncols], func=AF.Square),
             reads=(xtk(c),), writes=(stok,))
        for (a, b, smp) in segs:
            if smp:
                bk, c0 = smp_cols
                out_ap = k.ps[bk][:, c0:c0 + (b - a)]
                wt = ("ps", bk)
            else:
                out_ap = k.ps[bank_ss][:, 0:512]
                wt = ("ps", bank_ss)
            S.op("pe", lambda e, o=out_ap, s=s, a=a, b=b, c=c: e.matmul(
                out=o, lhsT=k.ones_f[:, :], rhs=s[:, a:b], start=(c == 0), stop=(c == NCH - 1)),
                reads=(stok,), writes=(wt,))
    rtok = ("rstd", id(rstd))
    for (a, b, smp) in segs:
        if smp:
            bk, c0 = smp_cols
            in_ap = k.ps[bk][:, c0:c0 + (b - a)]
            rt = ("ps", bk)
        else:
            in_ap = k.ps[bank_ss][:, 0:512]
            rt = ("ps", bank_ss)
        S.op("act", lambda e, i=in_ap, a=a, b=b: e.activation(
            out=rstd[:, a:b], in_=i, func=AF.Sqrt, scale=1.0 / D, bias=k.eps_col[:, 0:1]),
            reads=(rt,), writes=(rtok,))
    S.op("dve", lambda e: e.reciprocal(out=rstd[:, 0:ncols], in_=rstd[:, 0:ncols]),
         reads=(rtok,), writes=(rtok,))
    for c in range(NCH):
        S.op("dve", lambda e, c=c: e.scalar_tensor_tensor(
            out=out_h[:, c, 0:ncols], in0=x[:, c, 0:ncols], scalar=gcol[:, c:c + 1], in1=rstd[:, 0:ncols],
            op0=ALU.mult, op1=ALU.mult),
            reads=(xtk(c), rtok), writes=(htok,))


def phase_ffn(k, L):
    nc, S, cfg = k.nc, k.S, k.cfg
    Wg = k.w_gate[L].rearrange("(kc p) f -> p kc f", p=128)
    Wu = k.w_up[L].rearrange("(kc p) f -> p kc f", p=128)
    Wd = k.w_down[L].rearrange("(fc p) d -> p fc d", p=128)
    with ExitStack() as es:
        A = lambda name, shape, dt: es.enter_context(nc.sbuf_tensor(f"{name}_L{L}", shape, dt))
        xt = A("ffn_x", [128, NCH, 520], F32)
        ht = A("ffn_h", [128, NCH, 520], BF16)
        at = A("ffn_a", [128, NFC, 520], BF16)
        wg = [A(f"ffn_wg{i}", [128, NCH, 256], BF16) for i in range(2)]
        wu = [A(f"ffn_wu{i}", [128, NCH, 256], BF16) for i in range(2)]
        wd = [A(f"ffn_wd{i}", [128, NFC, 256], BF16) for i in range(2)]
        sq = [A(f"ffn_sq{i}", [128, 520], F32) for i in range(2)]
        rstd = A("ffn_rstd", [128, 520], F32)
        sg = [A(f"ffn_sg{i}", [128, 520], F32) for i in range(2)]
        gcol = k.g_ffn[:, L * NCH:(L + 1) * NCH]
        SBK = (4, 5)
        nw = 0
        nd = 0
        for ti in range(cfg.ntile):
            c0, ncols = tile_cols(cfg, ti)
            segs = mm_cols(ncols)
            for c in range(NCH):
                S.dma("sp", lambda e, c=c, c0=c0, n=ncols: e.dma_start(out=xt[:, c, 0:n], in_=k.XT[:, c, c0:c0 + n]),
                      reads=(("XT", ti),), writes=(("ffn_x", c),))
            rmsnorm_tile(k, xt, lambda c: ("ffn_x", c), ncols, gcol, ht, "ffn_h", sq, 6, (7, 0), rstd)
            for fc2 in range(NFC // 2):
                slot = nw % 2
                nw += 1
                S.dma("pool", lambda e, slot=slot, fc2=fc2: e.dma_start(
                    out=wg[slot][:, :, :], in_=Wg[:, :, fc2 * 256:(fc2 + 1) * 256]),
                    reads=(), writes=(("wg", slot),))
                S.dma("pool", lambda e, slot=slot, fc2=fc2: e.dma_start(
                    out=wu[slot][:, :, :], in_=Wu[:, :, fc2 * 256:(fc2 + 1) * 256]),
                    reads=(), writes=(("wu", slot),))
                for half in range(2):
                    fc = fc2 * 2 + half
                    par = fc % 2
                    for which, wt_, wtok in ((0, wg, "wg"), (1, wu, "wu")):
                        bank = par * 2 + which
                        for kc in range(NCH):
                            for (a, b, smp) in segs:
                                if smp:
                                    col = which * 8
                                    o = k.ps[SBK[par]][:, col:col + NS]
                                    tok = ("ps", SBK[par])
                                else:
                                    o = k.ps[bank][:, 0:512]
                                    tok = ("ps", bank)
                                S.op("pe", lambda e, o=o, w=wt_[slot], kc=kc, half=half, a=a, b=b: e.matmul(
                                    out=o, lhsT=w[:, kc, half * 128:(half + 1) * 128], rhs=ht[:, kc, a:b],
                                    start=(kc == 0), stop=(kc == NCH - 1)),
                                    reads=((wtok, slot), "ffn_h"), writes=(tok,))
                    for (a, b, smp) in segs:
                        if smp:
                            gin = k.ps[SBK[par]][:, 0:NS]
                            uin = k.ps[SBK[par]][:, 8:8 + NS]
                            gt, ut = ("ps", SBK[par]), ("ps", SBK[par])
                        else:
                            gin = k.ps[par * 2][:, 0:512]
                            uin = k.ps[par * 2 + 1][:, 0:512]
                            gt, ut = ("ps", par * 2), ("ps", par * 2 + 1)
                        sgt = ("ffn_sg", par, smp)
                        S.op("act", lambda e, gin=gin, a=a, b=b, par=par: e.activation(
                            out=sg[par][:, a:b], in_=gin, func=AF.Silu),
                            reads=(gt,), writes=(sgt,))
                        S.op("dve", lambda e, uin=uin, a=a, b=b, par=par, fc=fc: e.tensor_tensor(
                            out=at[:, fc, a:b], in0=sg[par][:, a:b], in1=uin, op=ALU.mult),
                            reads=(sgt, ut), writes=(("ffn_a", fc),))
            for dc2 in range(NCH // 2):
                slot = nd % 2
                nd += 1
                S.dma("pool", lambda e, slot=slot, dc2=dc2: e.dma_start(
                    out=wd[slot][:, :, :], in_=Wd[:, :, dc2 * 256:(dc2 + 1) * 256]),
                    reads=(), writes=(("wd", slot),))
                for half in range(2):
                    dc = dc2 * 2 + half
                    bank = dc % 2
                    for fc in range(NFC):
                        for (a, b, smp) in segs:
                            if smp:
                                o = k.ps[SBK[dc % 2]][:, 32:32 + NS]
                                tok = ("ps", SBK[dc % 2])
                            else:
                                o = k.ps[bank][:, 0:512]
                                tok = ("ps", bank)
                            S.op("pe", lambda e, o=o, slot=slot, fc=fc, half=half, a=a, b=b: e.matmul(
                                out=o, lhsT=wd[slot][:, fc, half * 128:(half + 1) * 128], rhs=at[:, fc, a:b],
                                start=(fc == 0), stop=(fc == NFC - 1)),
                                reads=(("wd", slot), ("ffn_a", fc)), writes=(tok,))
                    for (a, b, smp) in segs:
                        if smp:
                            din = k.ps[SBK[dc % 2]][:, 32:32 + NS]
                            tok = ("ps", SBK[dc % 2])
                        else:
                            din = k.ps[bank][:, 0:512]
                            tok = ("ps", bank)
                        S.op("dve", lambda e, din=din, dc=dc, a=a, b=b: e.tensor_tensor(
                            out=xt[:, dc, a:b], in0=xt[:, dc, a:b], in1=din, op=ALU.add),
                            reads=(tok, ("ffn_x", dc)), writes=(("ffn_x", dc),))
                    S.dma("sp", lambda e, dc=dc, c0=c0, n=ncols: e.dma_start(out=k.XT[:, dc, c0:c0 + n], in_=xt[:, dc, 0:n]),
                          reads=(("ffn_x", dc),), writes=(("XT", ti),))
            S.flush(k.block)
    S.barrier()


def phase_pool(k, L, j):
    nc, S, cfg = k.nc, k.S, k.cfg
    HB = PBUF + 520
    with ExitStack() as es:
        A_ = lambda name, shape, dt: es.enter_context(nc.sbuf_tensor(f"{name}_L{L}", shape, dt))
        xt = A_("pl_x", [128, NCH, 520], F32)
        hb = A_("pl_h", [128, NCH, HB], F32)
        wa = A_("pl_wa", [128, NCH, HB], F32)
        wb = A_("pl_wb", [128, NCH, HB], F32)
        pb = A_("pl_p", [128, NCH, 520], BF16)
        hs = A_("pl_hs", [128, NCH, 24], F32)
        sa = A_("pl_sa", [128, NCH, 24], F32)
        sb_ = A_("pl_sb", [128, NCH, 24], F32)
        wp = A_("pl_wp", [128, NCH, 512], BF16)
        sq = [A_(f"pl_sq{i}", [128, 520], F32) for i in range(2)]
        rstd = A_("pl_rstd", [128, 520], F32)
        tmp = A_("pl_tmp", [128, NCH, PBUF], F32)
        cin = A_("pl_cin", [PBUF, D], F32)
        pout = [A_(f"pl_po{i}", [PBUF, D], F32) for i in range(2)]
        gcol = k.g_mix[:, L * NCH:(L + 1) * NCH]
        SBK = (4, 5)
        S.dma("pool", lambda e: e.dma_start(out=wp[:, :, :], in_=k.w_pool[j].rearrange("g (kc p) o -> p (g kc) o", p=128)),
              writes=("pl_wp",))
        S.op("pool", lambda e: e.memset(hb[:, :, 0:PBUF], 0.0), writes=("pl_h",))
        S.dma("sp", lambda e: e.dma_start(out=cin[:, :], in_=k.cache_pool[j]), writes=("pl_cin",))
        for g in range(4):
            for cc in range(4):
                c = g * 4 + cc
                S.op("pe", lambda e, g=g, cc=cc, c=c: e.transpose(
                    out=k.ps[g][:, cc * 128:cc * 128 + PBUF], in_=cin[0:PBUF, c * 128:(c + 1) * 128],
                    identity=k.ident[0:PBUF, 0:PBUF]), reads=("pl_cin",), writes=(("ps", g),))
            S.op("act", lambda e, g=g: e.copy(
                out=hs[:, g * 4:(g + 1) * 4, 0:PBUF],
                in_=k.ps[g][:, :].rearrange("p (c t) -> p c t", c=4)[:, :, 0:PBUF]),
                reads=(("ps", g),), writes=("pl_hs",))

        def windows(eng, h, a, b, n, toks):
            th, ta, tb = toks
            S.op(eng, lambda e: e.tensor_tensor(out=a[:, :, 1:n], in0=h[:, :, 1:n], in1=h[:, :, 0:n - 1], op=ALU.add),
                 reads=(th,), writes=(ta,))
            S.op(eng, lambda e: e.tensor_tensor(out=b[:, 4:16, 3:n], in0=a[:, 4:16, 3:n], in1=a[:, 4:16, 1:n - 2], op=ALU.add),
                 reads=(ta,), writes=(tb,))
            S.op(eng, lambda e: e.tensor_tensor(out=a[:, 8:16, 7:n], in0=b[:, 8:16, 7:n], in1=b[:, 8:16, 3:n - 4], op=ALU.add),
                 reads=(tb,), writes=(ta,))
            S.op(eng, lambda e: e.tensor_tensor(out=b[:, 12:16, 15:n], in0=a[:, 12:16, 15:n], in1=a[:, 12:16, 7:n - 8], op=ALU.add),
                 reads=(ta,), writes=(tb,))

        for ti in range(cfg.ntile):
            c0, ncols = tile_cols(cfg, ti)
            segs = mm_cols(ncols)
            last = ncols > 512
            for c in range(NCH):
                S.dma("sp", lambda e, c=c, c0=c0, n=ncols: e.dma_start(out=xt[:, c, 0:n], in_=k.XT[:, c, c0:c0 + n]),
                      reads=(("XT", ti),), writes=(("pl_x", c),))
            if ti > 0:
                S.op("pool", lambda e: e.tensor_copy(out=hb[:, :, 0:PBUF], in_=hb[:, :, 512:512 + PBUF]),
                     reads=("pl_h",), writes=("pl_h",))
            rmsnorm_tile(k, xt, lambda c: ("pl_x", c), ncols, gcol, hb[:, :, PBUF:HB], "pl_h", sq, 6, (7, 0), rstd)
            windows("pool", hb, wa, wb, PBUF + 512, ("pl_h", "pl_wa", "pl_wb"))
            for g in range(4):
                w = 2 << g
                src = wa if g % 2 == 0 else wb
                stok = "pl_wa" if g % 2 == 0 else "pl_wb"
                S.op("dve", lambda e, g=g, w=w, src=src: e.scalar_tensor_tensor(
                    out=pb[:, g * 4:(g + 1) * 4, 0:512], in0=src[:, g * 4:(g + 1) * 4, PBUF:PBUF + 512],
                    scalar=1.0 / w, in1=hb[:, g * 4:(g + 1) * 4, PBUF:PBUF + 512], op0=ALU.mult, op1=ALU.subtract),
                    reads=(stok, "pl_h"), writes=("pl_p",))
                if ti == 0:
                    S.op("dve", lambda e, g=g, src=src: e.tensor_tensor(
                        out=tmp[:, g * 4:(g + 1) * 4, :], in0=src[:, g * 4:(g + 1) * 4, PBUF:2 * PBUF],
                        in1=k.invc[:, g * 4:(g + 1) * 4, :], op=ALU.mult),
                        reads=(stok,), writes=("pl_tmp",))
                    S.op("dve", lambda e, g=g: e.tensor_tensor(
                        out=pb[:, g * 4:(g + 1) * 4, 0:PBUF], in0=tmp[:, g * 4:(g + 1) * 4, :],
                        in1=hb[:, g * 4:(g + 1) * 4, PBUF:2 * PBUF], op=ALU.subtract),
                        reads=("pl_tmp", "pl_h"), writes=("pl_p",))
            if last:
                S.op("dve", lambda e: e.tensor_copy(out=hs[:, :, PBUF:PBUF + NS], in_=hb[:, :, PBUF + 512:PBUF + 520]),
                     reads=("pl_h",), writes=("pl_hs",))
                windows("dve", hs, sa, sb_, PBUF + NS, ("pl_hs", "pl_sa", "pl_sb"))
                for g in range(4):
                    w = 2 << g
                    src = sa if g % 2 == 0 else sb_
                    stok = "pl_sa" if g % 2 == 0 else "pl_sb"
                    S.op("dve", lambda e, g=g, w=w, src=src: e.scalar_tensor_tensor(
                        out=pb[:, g * 4:(g + 1) * 4, 512:520], in0=src[:, g * 4:(g + 1) * 4, PBUF:PBUF + NS],
                        scalar=1.0 / w, in1=hs[:, g * 4:(g + 1) * 4, PBUF:PBUF + NS], op0=ALU.mult, op1=ALU.subtract),
                        reads=(stok, "pl_hs"), writes=("pl_p",))
            n_out = 0
            for g in range(4):
                for oc in range(4):
                    c = g * 4 + oc
                    bank = n_out % 4
                    sbank = SBK[n_out % 2]
                    n_out += 1
                    for kc in range(4):
                        for (a, b, smp) in segs:
                            o = k.ps[sbank][:, 0:NS] if smp else k.ps[bank][:, 0:512]
                            tok = ("ps", sbank) if smp else ("ps", bank)
                            S.op("pe", lambda e, o=o, g=g, kc=kc, oc=oc, a=a, b=b: e.matmul(
                                out=o, lhsT=wp[:, g * 4 + kc, oc * 128:(oc + 1) * 128], rhs=pb[:, g * 4 + kc, a:b],
                                start=(kc == 0), stop=(kc == 3)),
                                reads=("pl_wp", "pl_p"), writes=(tok,))
                    for (a, b, smp) in segs:
                        o = k.ps[sbank][:, 0:NS] if smp else k.ps[bank][:, 0:512]
                        tok = ("ps", sbank) if smp else ("ps", bank)
                        S.op("dve", lambda e, o=o, c=c, a=a, b=b: e.scalar_tensor_tensor(
                            out=xt[:, c, a:b], in0=o, scalar=k.pscale[:, j * NCH + c:j * NCH + c + 1], in1=xt[:, c, a:b],
                            op0=ALU.mult, op1=ALU.add),
                            reads=(tok, ("pl_x", c)), writes=(("pl_x", c),))
                    S.dma("sp", lambda e, c=c, c0=c0, n=ncols: e.dma_start(out=k.XT[:, c, c0:c0 + n], in_=xt[:, c, 0:n]),
                          reads=(("pl_x", c),), writes=(("XT", ti),))
            if last:
                for oi, (src, c_lo, dst) in enumerate(((hb, 512, k.new_pool_prompt[j]), (hs, NS, k.new_pool_sample[j]))):
                    stok = "pl_h" if oi == 0 else "pl_hs"
                    for g in range(4):
                        for cc in range(4):
                            c = g * 4 + cc
                            S.op("pe", lambda e, g=g, cc=cc, c=c, src=src, c_lo=c_lo: e.transpose(
                                out=k.ps[g][0:PBUF, cc * 128:(cc + 1) * 128], in_=src[:, c, c_lo:c_lo + PBUF],
                                identity=k.ident[:, :]), reads=(stok,), writes=(("ps", g),))
                        S.op("act", lambda e, g=g, oi=oi: e.copy(
                            out=pout[oi][0:PBUF, g * 512:(g + 1) * 512], in_=k.ps[g][0:PBUF, :]),
                            reads=(("ps", g),), writes=(("pl_po", oi),))
                    S.dma("sp", lambda e, dst=dst, oi=oi: e.dma_start(out=dst, in_=pout[oi][0:PBUF, :]),
                          reads=(("pl_po", oi),), writes=(("pool_out", j, oi),))
            S.flush(k.block)
    S.barrier()


def phase_s5(k, L, j):
    nc, S, cfg = k.nc, k.S, k.cfg
    TC = getattr(cfg, 's5_tc', 32)
    G2 = 64
    TWO_PI_HI = 6.28125
    TWO_PI_LO = 2.0 * math.pi - 6.28125
    with ExitStack() as es:
        A_ = lambda name, shape, dt: es.enter_context(nc.sbuf_tensor(f"{name}_L{L}", shape, dt))
        AR2 = A_("s5_ar2", [128, G2, 2], F32)
        AI2 = A_("s5_ai2", [128, G2, 2], F32)
        LB = [[A_(f"s5_lb{ri}{eo}", [128, NCH, 128], BF16) for eo in range(2)] for ri in range(2)]
        LBa = [[A_(f"s5_lba{ri}{eo}", [128, NCH, 128], BF16) for eo in range(2)] for ri in range(2)]
        AR2q = A_("s5_ar2q", [128, G2, 2], F32)
        AI2q = A_("s5_ai2q", [128, G2, 2], F32)
        LCR = A_("s5_lcr", [128, G2, 64], BF16)
        LCI = A_("s5_lci", [128, G2, 64], BF16)
        Dg = A_("s5_dg", [128, NCH, 128], F32)
        S0 = A_("s5_s0", [128, G2, 2], F32)
        with ExitStack() as pes:
            P_ = lambda name, shape, dt: pes.enter_context(nc.sbuf_tensor(f"{name}_L{L}", shape, dt))
            names = ["ar", "ai", "dt", "th", "mag", "kk", "r", "sh", "sn", "cs", "abr", "abi", "den", "nr", "t1", "t2", "fr", "fi",
                     "far", "fai", "a2r", "a2i"]
            v = {n: P_("s5p_" + n, [128, G2], F32) for n in names}
            fT = [P_(f"s5p_fT{i}", [64, 128], F32) for i in range(2)]
            FW = [P_(f"s5p_fw{i}", [128, NCH, 128], F32) for i in range(2)]
            BW = [P_(f"s5p_bw{i}", [128, NCH, 128], F32) for i in range(2)]
            TM = [P_(f"s5p_tm{i}", [128, NCH, 128], F32) for i in range(2)]
            sel = P_("s5p_sel", [64, NCH, 128], F32)
            msk = P_("s5p_msk", [128, 2], F32)
            LBf = [P_(f"s5p_lbf{i}", [128, NCH, 128], F32) for i in range(2)]
            LCf = [P_(f"s5p_lcf{i}", [128, G2, 64], F32) for i in range(2)]
            dcol = P_("s5p_dcol", [128, NCH], F32)
            st_view = lambda ap2d: ap2d.rearrange("(gp par) n -> (par n) gp", par=2)
            with nc.allow_non_contiguous_dma(reason="small ssm parameter tables"):
                S.dma("sp", lambda e: e.dma_start(out=v["ar"][:, :], in_=st_view(k.ssm_a_re[j])), writes=("p_ar",))
                S.dma("sp", lambda e: e.dma_start(out=v["ai"][:, :], in_=st_view(k.ssm_a_im[j])), writes=("p_ai",))
                for par in range(2):
                    S.dma("sp", lambda e, par=par: e.dma_start(
                        out=v["dt"][par * 64:(par + 1) * 64, :],
                        in_=k.ssm_log_dt[j:j + 1, :].rearrange("o (gp par) -> o gp par", par=2)[:, :, par].broadcast_to([64, G2])),
                        writes=("p_dt",))
                S.dma("sp", lambda e: e.dma_start(out=S0[:, :, 0], in_=st_view(k.state_re[j])), writes=("s5_s0",))
                S.dma("sp", lambda e: e.dma_start(out=S0[:, :, 1], in_=st_view(k.state_im[j])), writes=("s5_s0",))
                S.dma("sp", lambda e: e.dma_start(out=dcol[:, :], in_=k.ssm_d[j].rearrange("(c p) -> p c", p=128)), writes=("p_dcol",))
                S.dma("sp", lambda e: e.dma_start(out=sel[:, :, :], in_=k.c_sel), writes=("p_sel",))
                S.dma("sp", lambda e: e.dma_start(out=BW[0][:, :, :], in_=k.b_re_w[j]), writes=("p_bw0",))
                S.dma("sp", lambda e: e.dma_start(out=BW[1][:, :, :], in_=k.b_im_w[j]), writes=("p_bw1",))
                S.dma("sp", lambda e: e.dma_start(out=LCf[0][:, :, :], in_=k.c_re_w[j]), writes=("p_lcf0",))
                S.dma("sp", lambda e: e.dma_start(out=LCf[1][:, :, :], in_=k.c_im_w[j]), writes=("p_lcf1",))
                S.dma("sp", lambda e: e.dma_start(out=msk[:, :], in_=k.c_msk), writes=("p_msk",))
                S.flush(k.block)
            def tt(o, a, b, op, rd, wr, eng="dve"):
                S.op(eng, lambda e: e.tensor_tensor(out=o, in0=a, in1=b, op=op), reads=rd, writes=wr)
            def ts(o, a, s1, op0, rd, wr, s2=None, op1=None, eng="dve"):
                if op1 is None:
                    S.op(eng, lambda e: e.tensor_scalar(out=o, in0=a, scalar1=s1, scalar2=None, op0=op0), reads=rd, writes=wr)
                else:
                    S.op(eng, lambda e: e.tensor_scalar(out=o, in0=a, scalar1=s1, scalar2=s2, op0=op0, op1=op1), reads=rd, writes=wr)
            def act(o, a, f, rd, wr, scale=1.0):
                S.op("act", lambda e: e.activation(out=o, in_=a, func=f, scale=scale), reads=rd, writes=wr)
            V = lambda n: v[n][:, :]
            act(V("dt"), V("dt"), AF.Exp, ("p_dt",), ("p_dt",))
            tt(V("t1"), V("dt"), V("ar"), ALU.mult, ("p_dt", "p_ar"), ("p_t1",))
            act(V("mag"), V("t1"), AF.Exp, ("p_t1",), ("p_mag",))
            tt(V("th"), V("dt"), V("ai"), ALU.mult, ("p_dt", "p_ai"), ("p_th",))
            ts(V("kk"), V("th"), math.pi, ALU.is_gt, ("p_th",), ("p_kk",))
            for m in (1, 2, 3):
                ts(V("t2"), V("th"), (2 * m + 1) * math.pi, ALU.is_gt, ("p_th",), ("p_t2",))
                tt(V("kk"), V("kk"), V("t2"), ALU.add, ("p_kk", "p_t2"), ("p_kk",))
            ts(V("t2"), V("kk"), -TWO_PI_HI, ALU.mult, ("p_kk",), ("p_t2",))
            tt(V("r"), V("th"), V("t2"), ALU.add, ("p_th", "p_t2"), ("p_r",))
            ts(V("t2"), V("kk"), -TWO_PI_LO, ALU.mult, ("p_kk",), ("p_t2",))
            tt(V("r"), V("r"), V("t2"), ALU.add, ("p_r", "p_t2"), ("p_r",))
            act(V("sn"), V("r"), AF.Sin, ("p_r",), ("p_sn",))
            act(V("sh"), V("r"), AF.Sin, ("p_r",), ("p_sh",), scale=0.5)
            tt(V("cs"), V("sh"), V("sh"), ALU.mult, ("p_sh",), ("p_cs",))
            ts(V("cs"), V("cs"), -2.0, ALU.mult, ("p_cs",), ("p_cs",), s2=1.0, op1=ALU.add)
            tt(V("abr"), V("mag"), V("cs"), ALU.mult, ("p_mag", "p_cs"), ("p_abr",))
            tt(V("abi"), V("mag"), V("sn"), ALU.mult, ("p_mag", "p_sn"), ("p_abi",))
            S.op("dve", lambda e: e.tensor_copy(out=AR2[:, :, 0], in_=V("abr")), reads=("p_abr",), writes=("s5_ar2",))
            S.op("dve", lambda e: e.tensor_copy(out=AR2[:, :, 1], in_=V("abr")), reads=("p_abr",), writes=("s5_ar2",))
            S.op("dve", lambda e: e.tensor_copy(out=AI2[:, :, 1], in_=V("abi")), reads=("p_abi",), writes=("s5_ai2",))
            ts(AI2[:, :, 0], V("abi"), -1.0, ALU.mult, ("p_abi",), ("s5_ai2",))
            tt(V("den"), V("ar"), V("ar"), ALU.mult, ("p_ar",), ("p_den",))
            tt(V("t1"), V("ai"), V("ai"), ALU.mult, ("p_ai",), ("p_t1",))
            tt(V("den"), V("den"), V("t1"), ALU.add, ("p_den", "p_t1"), ("p_den",))
            S.op("dve", lambda e: e.reciprocal(out=V("den"), in_=V("den")), reads=("p_den",), writes=("p_den",))
            ts(V("nr"), V("abr"), -1.0, ALU.add, ("p_abr",), ("p_nr",))
            tt(V("t1"), V("nr"), V("ar"), ALU.mult, ("p_nr", "p_ar"), ("p_t1",))
            tt(V("t2"), V("abi"), V("ai"), ALU.mult, ("p_abi", "p_ai"), ("p_t2",))
            tt(V("t1"), V("t1"), V("t2"), ALU.add, ("p_t1", "p_t2"), ("p_t1",))
            tt(V("fr"), V("t1"), V("den"), ALU.mult, ("p_t1", "p_den"), ("p_fr",))
            tt(V("t1"), V("abi"), V("ar"), ALU.mult, ("p_abi", "p_ar"), ("p_t1",))
            tt(V("t2"), V("nr"), V("ai"), ALU.mult, ("p_nr", "p_ai"), ("p_t2",))
            tt(V("t1"), V("t1"), V("t2"), ALU.subtract, ("p_t1", "p_t2"), ("p_t1",))
            tt(V("fi"), V("t1"), V("den"), ALU.mult, ("p_t1", "p_den"), ("p_fi",))
            tt(V("t1"), V("abr"), V("abr"), ALU.mult, ("p_abr",), ("p_t1",))
            tt(V("t2"), V("abi"), V("abi"), ALU.mult, ("p_abi",), ("p_t2",))
            tt(V("a2r"), V("t1"), V("t2"), ALU.subtract, ("p_t1", "p_t2"), ("p_a2r",))
            tt(V("t1"), V("abr"), V("abi"), ALU.mult, ("p_abr", "p_abi"), ("p_t1",))
            ts(V("a2i"), V("t1"), 2.0, ALU.mult, ("p_t1",), ("p_a2i",))
            S.op("dve", lambda e: e.tensor_copy(out=AR2q[:, :, 0], in_=V("a2r")), reads=("p_a2r",), writes=("s5_ar2q",))
            S.op("dve", lambda e: e.tensor_copy(out=AR2q[:, :, 1], in_=V("a2r")), reads=("p_a2r",), writes=("s5_ar2q",))
            S.op("dve", lambda e: e.tensor_copy(out=AI2q[:, :, 1], in_=V("a2i")), reads=("p_a2i",), writes=("s5_ai2q",))
            ts(AI2q[:, :, 0], V("a2i"), -1.0, ALU.mult, ("p_a2i",), ("s5_ai2q",))
            tt(V("t1"), V("abr"), V("fr"), ALU.mult, ("p_abr", "p_fr"), ("p_t1",))
            tt(V("t2"), V("abi"), V("fi"), ALU.mult, ("p_abi", "p_fi"), ("p_t2",))
            tt(V("far"), V("t1"), V("t2"), ALU.subtract, ("p_t1", "p_t2"), ("p_far",))
            tt(V("t1"), V("abr"), V("fi"), ALU.mult, ("p_abr", "p_fi"), ("p_t1",))
            tt(V("t2"), V("abi"), V("fr"), ALU.mult, ("p_abi", "p_fr"), ("p_t2",))
            tt(V("fai"), V("t1"), V("t2"), ALU.add, ("p_t1", "p_t2"), ("p_fai",))
            FWr, FWi, BWr, BWi = (t[:, :, :] for t in (FW[0], FW[1], BW[0], BW[1]))
            for (nr_, ni_, LBdst) in (("fr", "fi", LB), ("far", "fai", LBa)):
                for i, nm in enumerate((nr_, ni_)):
                    S.op("pe", lambda e, nm=nm, i=i: e.transpose(out=k.ps[i][0:64, 0:128], in_=v[nm][:, :], identity=k.ident[:, :]),
                         reads=("p_" + nm,), writes=(("ps", i),))
                    S.op("act", lambda e, i=i: e.copy(out=fT[i][:, :], in_=k.ps[i][0:64, 0:128]),
                         reads=(("ps", i),), writes=(("p_fT", i),))
                    for q4 in range(4):
                        bank = 2 + (i * 4 + q4) % 4
                        for jj4 in range(4):
                            jj = q4 * 4 + jj4
                            S.op("pe", lambda e, i=i, jj=jj, jj4=jj4, bank=bank: e.matmul(
                                out=k.ps[bank][:, jj4 * 128:(jj4 + 1) * 128], lhsT=sel[:, jj, :], rhs=fT[i][:, :],
                                start=True, stop=True), reads=("p_sel", ("p_fT", i)), writes=(("ps", bank),))
                        S.op("act", lambda e, i=i, q4=q4, bank=bank: e.copy(
                            out=FW[i][:, q4 * 4:(q4 + 1) * 4, :], in_=k.ps[bank][:, :].rearrange("p (a b) -> p a b", a=4)),
                            reads=(("ps", bank),), writes=(("p_fw", i),))
                tt(TM[0][:, :, :], FWr, BWr, ALU.mult, (("p_fw", 0), "p_bw0"), ("p_tm0",))
                tt(TM[1][:, :, :], FWi, BWi, ALU.mult, (("p_fw", 1), "p_bw1"), ("p_tm1",))
                tt(LBf[0][:, :, :], TM[0][:, :, :], TM[1][:, :, :], ALU.subtract, ("p_tm0", "p_tm1"), ("p_lbf0",))
                tt(TM[0][:, :, :], FWr, BWi, ALU.mult, (("p_fw", 0), "p_bw1"), ("p_tm0",))
                tt(TM[1][:, :, :], FWi, BWr, ALU.mult, (("p_fw", 1), "p_bw0"), ("p_tm1",))
                tt(LBf[1][:, :, :], TM[0][:, :, :], TM[1][:, :, :], ALU.add, ("p_tm0", "p_tm1"), ("p_lbf1",))
                for ri in range(2):
                    for eo in range(2):
                        ts(LBdst[ri][eo][:, :, :], LBf[ri][:, :, :], msk[:, eo:eo + 1], ALU.mult, (f"p_lbf{ri}", "p_msk"), ("s5_lb",))
            S.op("dve", lambda e: e.tensor_copy(out=LCR[:, :, :], in_=LCf[0][:, :, :]), reads=("p_lcf0",), writes=("s5_lcr",))
            ts(LCI[:, :, :], LCf[1][:, :, :], -1.0, ALU.mult, ("p_lcf1",), ("s5_lci",))
            for c in range(NCH):
                ts(Dg[:, c, :], k.ident[:, :], dcol[:, c:c + 1], ALU.mult, ("ident", "p_dcol"), ("s5_dg",))
            S.flush(k.block)
        S.barrier()
        if getattr(cfg, "s5_stop", 0) == 1:
            return
        xt = A_("s5_x", [128, NCH, 520], F32)
        hb = A_("s5_hb", [128, NCH, 521], BF16)
        gb = A_("s5_gb", [128, NCH, 520], BF16)
        BU = A_("s5_bu", [128, TC, G2, 2], F32)
        SS = A_("s5_ss", [128, TC + 2, G2, 2], F32)
        SSb = A_("s5_ssb", [128, TC, G2, 2], BF16)
        t1 = A_("s5_t1", [128, 2, G2, 2], F32)
        t2 = A_("s5_t2", [128, 2, G2, 2], F32)
        SSTOK = (("s5_ss", "dve"),)
        sq = [A_(f"s5_sq{i}", [128, 520], F32) for i in range(2)]
        rstd = A_("s5_rstd", [128, 520], F32)
        wa = [A_(f"s5_wa{i}", [128, NCH, 128], BF16) for i in range(2)]
        wb = [A_(f"s5_wb{i}", [128, NCH, 128], BF16) for i in range(2)]
        sg = [A_(f"s5_sg{i}", [128, 520], F32) for i in range(2)]
        xr = [A_(f"s5_xr{i}", [128, 520], F32) for i in range(2)]
        gcol = k.g_mix[:, L * NCH:(L + 1) * NCH]
        Wa = k.w_glu_a[j].rearrange("(kc p) f -> p kc f", p=128)
        Wb = k.w_glu_b[j].rearrange("(kc p) f -> p kc f", p=128)
        SBK = (4, 5)
        S.op("pool", lambda e: e.memset(SS[:, 0:2, :, :], 0.0), writes=SSTOK)
        S.op("pool", lambda e: e.memset(hb[:, :, 0:1], 0.0), writes=("s5_hb",))
        nbank = [0]
        nw = [0]

        stop = getattr(cfg, "s5_stop", 0)

        def scan_chunk(col0, nt, pair):
            if stop == 2:
                return
            hc = col0 + 1
            for g8 in range(G2 // 8):
                banks = (nbank[0] % 4, (nbank[0] + 1) % 4)
                nbank[0] += 2
                for hh in range(2):
                    bank = banks[hh]
                    for a in range(2):
                        for q2 in range(2):
                            gp = g8 * 8 + a * 4 + hh * 2 + q2
                            jj = gp // 4
                            for ri in range(2):
                                col = ((a * 2 + q2) * 2 + ri) * nt
                                S.op("pe", lambda e, bank=bank, col=col, ri=ri, jj=jj, q2=q2, hh=hh: e.matmul(
                                    out=k.ps[bank][:, col:col + nt],
                                    lhsT=LB[ri][q2][64 * hh:64 * hh + 64, jj, :], rhs=hb[64 * hh:64 * hh + 64, jj, hc:hc + nt],
                                    start=True, stop=(not pair)),
                                    reads=("s5_lb", "s5_hb"), writes=(("ps", bank),))
                                if pair:
                                    S.op("pe", lambda e, bank=bank, col=col, ri=ri, jj=jj, q2=q2, hh=hh: e.matmul(
                                        out=k.ps[bank][:, col:col + nt],
                                        lhsT=LBa[ri][q2][64 * hh:64 * hh + 64, jj, :], rhs=hb[64 * hh:64 * hh + 64, jj, hc - 1:hc - 1 + nt],
                                        start=False, stop=True),
                                        reads=("s5_lb", "s5_hb"), writes=(("ps", bank),))
                    for a in range(2):
                        gp0 = g8 * 8 + a * 4 + hh * 2
                        S.op("act", lambda e, bank=bank, a=a, gp0=gp0: e.copy(
                            out=BU[:, 0:nt, gp0:gp0 + 2, :].rearrange("p t g r -> p g r t"),
                            in_=k.ps[bank][:, a * 4 * nt:(a + 1) * 4 * nt].rearrange("p (g r t) -> p g r t", g=2, r=2)),
                            reads=(("ps", bank),), writes=("s5_bu",))
            if stop == 3:
                return
            sst = ("s5_ss", "dve")
            if pair:
                bc = lambda ap3: ap3.unsqueeze(1).broadcast_to([128, 2, G2, 2])
                for t in range(0, nt, 2):
                    S.op("dve", lambda e, t=t: e.tensor_tensor(
                        out=t1[:, :, :, :], in0=bc(AR2q[:, :, :]), in1=SS[:, t:t + 2, :, :], op=ALU.mult),
                        reads=("s5_ar2q", sst), writes=("s5_t1",))
                    S.op("dve", lambda e, t=t: e.tensor_tensor(
                        out=t2[:, :, :, :], in0=bc(AI2q[:, :, :]), in1=SS[:, t:t + 2, :, ::-1], op=ALU.mult),
                        reads=("s5_ai2q", sst), writes=("s5_t2",))
                    S.op("dve", lambda e: e.tensor_tensor(out=t1[:, :, :, :], in0=t1[:, :, :, :], in1=t2[:, :, :, :], op=ALU.add),
                         reads=("s5_t1", "s5_t2"), writes=("s5_t1",))
                    S.op("dve", lambda e, t=t: e.tensor_tensor(
                        out=SS[:, t + 2:t + 4, :, :], in0=t1[:, :, :, :], in1=BU[:, t:t + 2, :, :], op=ALU.add),
                        reads=("s5_t1", "s5_bu"), writes=(sst,))
            else:
                for t in range(nt):
                    S.op("dve", lambda e, t=t: e.tensor_tensor(
                        out=t1[:, 0, :, :], in0=AR2[:, :, :], in1=SS[:, t + 1, :, :], op=ALU.mult),
                        reads=("s5_ar2", sst), writes=("s5_t1",))
                    S.op("dve", lambda e, t=t: e.tensor_tensor(
                        out=t2[:, 0, :, :], in0=AI2[:, :, :], in1=SS[:, t + 1, :, ::-1], op=ALU.mult),
                        reads=("s5_ai2", sst), writes=("s5_t2",))
                    S.op("dve", lambda e: e.tensor_tensor(out=t1[:, 0, :, :], in0=t1[:, 0, :, :], in1=t2[:, 0, :, :], op=ALU.add),
                         reads=("s5_t1", "s5_t2"), writes=("s5_t1",))
                    S.op("dve", lambda e, t=t: e.tensor_tensor(
                        out=SS[:, t + 2, :, :], in0=t1[:, 0, :, :], in1=BU[:, t, :, :], op=ALU.add),
                        reads=("s5_t1", "s5_bu"), writes=(sst,))
            if stop == 4:
                return
            S.op("act", lambda e: e.copy(out=SSb[:, 0:nt, :, :], in_=SS[:, 2:nt + 2, :, :]),
                 reads=SSTOK, writes=("s5_ssb",))
            for jj in range(NCH):
                bank = nbank[0] % 4
                nbank[0] += 1
                for hh in range(2):
                    first = True
                    for q2 in range(2):
                        gp = jj * 4 + hh * 2 + q2
                        for ri, LC in enumerate((LCR, LCI)):
                            S.op("pe", lambda e, bank=bank, hh=hh, gp=gp, ri=ri, LC=LC, first=first: e.matmul(
                                out=k.ps[bank][64 * hh:64 * hh + 64, 0:nt], lhsT=LC[:, gp, :], rhs=SSb[:, 0:nt, gp, ri],
                                start=first, stop=False),
                                reads=("s5_lcr", "s5_lci", "s5_ssb"), writes=(("ps", bank),))
                            first = False
                    S.op("pe", lambda e, bank=bank, hh=hh, jj=jj: e.matmul(
                        out=k.ps[bank][64 * hh:64 * hh + 64, 0:nt], lhsT=Dg[:, jj, 64 * hh:64 * hh + 64], rhs=xt[:, jj, col0:col0 + nt],
                        start=False, stop=True),
                        reads=("s5_dg", "s5_x"), writes=(("ps", bank),))
                S.op("act", lambda e, bank=bank, jj=jj: e.activation(
                    out=gb[:, jj, col0:col0 + nt], in_=k.ps[bank][:, 0:nt], func=AF.Gelu),
                    reads=(("ps", bank),), writes=("s5_gb",))

        st_view = lambda ap2d: ap2d.rearrange("(gp par) n -> (par n) gp", par=2)
        for ti in range(cfg.ntile):
            c0, ncols = tile_cols(cfg, ti)
            segs = mm_cols(ncols)
            last = ncols > 512
            S.dma("sp", lambda e, c0=c0, n=ncols: e.dma_start(out=xt[:, :, 0:n], in_=k.XT[:, :, c0:c0 + n]),
                  reads=(("XT", ti),), writes=("s5_x",))
            rmsnorm_tile(k, xt, "s5_x", ncols, gcol, xt, "s5_x", sq, 6, (7, 0), rstd)
            if ti > 0:
                S.op("act", lambda e: e.copy(out=hb[:, :, 0:1], in_=hb[:, :, 512:513]), reads=("s5_hb",), writes=("s5_hb",))
            for c in range(NCH):
                S.op("act", lambda e, c=c, n=ncols: e.copy(out=hb[:, c, 1:n + 1], in_=xt[:, c, 0:n]),
                     reads=("s5_x",), writes=("s5_hb",))
            for tc in range(512 // TC):
                scan_chunk(tc * TC, TC, True)
                S.op("act", lambda e: e.copy(out=SS[:, 0:2, :, :], in_=SS[:, TC:TC + 2, :, :]),
                     reads=SSTOK, writes=SSTOK)
                S.flush(k.block)
            if last:
                with nc.allow_non_contiguous_dma(reason="ssm state output"):
                    S.dma("sp", lambda e: e.dma_start(out=st_view(k.new_ssm_re_prompt[j]), in_=SS[:, 1, :, 0]),
                          reads=SSTOK, writes=("ssm_out0",))
                    S.dma("sp", lambda e: e.dma_start(out=st_view(k.new_ssm_im_prompt[j]), in_=SS[:, 1, :, 1]),
                          reads=SSTOK, writes=("ssm_out1",))
                    S.flush(k.block)
                S.op("act", lambda e: e.copy(out=SS[:, 1, :, :], in_=S0[:, :, :]),
                     reads=("s5_s0",) + SSTOK, writes=SSTOK)
                scan_chunk(512, NS, False)
                with nc.allow_non_contiguous_dma(reason="ssm state output"):
                    S.dma("sp", lambda e: e.dma_start(out=st_view(k.new_ssm_re_sample[j]), in_=SS[:, NS + 1, :, 0]),
                          reads=SSTOK, writes=("ssm_out2",))
                    S.dma("sp", lambda e: e.dma_start(out=st_view(k.new_ssm_im_sample[j]), in_=SS[:, NS + 1, :, 1]),
                          reads=SSTOK, writes=("ssm_out3",))
                    S.flush(k.block)
            for oc2 in range(NCH if stop not in (2, 3, 4, 5) else 0):
                slot = nw[0] % 2
                nw[0] += 1
                S.dma("pool", lambda e, slot=slot, oc2=oc2: e.dma_start(out=wa[slot][:, :, :], in_=Wa[:, :, oc2 * 128:(oc2 + 1) * 128]),
                      writes=(("s5_wa", slot),))
                S.dma("pool", lambda e, slot=slot, oc2=oc2: e.dma_start(out=wb[slot][:, :, :], in_=Wb[:, :, oc2 * 128:(oc2 + 1) * 128]),
                      writes=(("s5_wb", slot),))
                for half in range(1):
                    oc = oc2
                    par = oc % 2
                    S.dma("sp", lambda e, par=par, oc=oc, c0=c0, n=ncols: e.dma_start(out=xr[par][:, 0:n], in_=k.XT[:, oc, c0:c0 + n]),
                          reads=(), writes=(("s5_xr", par),))
                    for which, wt_, wtok in ((0, wa, "s5_wa"), (1, wb, "s5_wb")):
                        bank = par * 2 + which
                        for kc in range(NCH):
                            for (a, b, smp) in segs:
                                o = k.ps[SBK[par]][:, which * 8:which * 8 + NS] if smp else k.ps[bank][:, 0:512]
                                tok = ("ps", SBK[par]) if smp else ("ps", bank)
                                S.op("pe", lambda e, o=o, w=wt_[slot], kc=kc, half=half, a=a, b=b: e.matmul(
                                    out=o, lhsT=w[:, kc, half * 128:(half + 1) * 128], rhs=gb[:, kc, a:b],
                                    start=(kc == 0), stop=(kc == NCH - 1)),
                                    reads=((wtok, slot), "s5_gb"), writes=(tok,))
                    for (a, b, smp) in segs:
                        if smp:
                            ain, bin_ = k.ps[SBK[par]][:, 0:NS], k.ps[SBK[par]][:, 8:8 + NS]
                            at_, bt_ = ("ps", SBK[par]), ("ps", SBK[par])
                        else:
                            ain, bin_ = k.ps[par * 2][:, 0:512], k.ps[par * 2 + 1][:, 0:512]
                            at_, bt_ = ("ps", par * 2), ("ps", par * 2 + 1)
                        sgt = ("s5_sg", par, smp)
                        S.op("act", lambda e, bin_=bin_, a=a, b=b, par=par: e.activation(out=sg[par][:, a:b], in_=bin_, func=AF.Sigmoid),
                             reads=(bt_,), writes=(sgt,))
                        S.op("dve", lambda e, ain=ain, a=a, b=b, par=par: e.tensor_tensor(
                            out=sg[par][:, a:b], in0=sg[par][:, a:b], in1=ain, op=ALU.mult),
                            reads=(sgt, at_), writes=(sgt,))
                        S.op("pool", lambda e, a=a, b=b, par=par: e.tensor_tensor(
                            out=xr[par][:, a:b], in0=xr[par][:, a:b], in1=sg[par][:, a:b], op=ALU.add),
                            reads=(sgt, ("s5_xr", par)), writes=(("s5_xr", par),))
                    S.dma("sp", lambda e, par=par, oc=oc, c0=c0, n=ncols: e.dma_start(out=k.XT[:, oc, c0:c0 + n], in_=xr[par][:, 0:n]),
                          reads=(("s5_xr", par),), writes=(("XT", ti),))
            S.flush(k.block)
    S.barrier()


def phase_sb(k, L, j):
    nc, S, cfg = k.nc, k.S, k.cfg
    T, TT = cfg.T, cfg.TT
    NH = 16
    SCALE = 1.0 / math.sqrt(128.0)
    Wqkv = k.w_qkv[j].rearrange("(kc p) f -> p kc f", p=128)
    Wo = k.w_o[j].rearrange("(kc p) f -> p kc f", p=128)
    with ExitStack() as es:
        A_ = lambda name, shape, dt: es.enter_context(nc.sbuf_tensor(f"{name}_L{L}", shape, dt))
        gq = A_("sb_gq", [128, 1], F32)
        gk = A_("sb_gk", [128, 1], F32)
        bcol = A_("sb_bcol", [128, NH], F32)
        brow = A_("sb_brow", [128, NH, NS], F32)
        tri = A_("sb_tri", [128, 128], BF16)
        trif = A_("sb_trif", [128, 128], F32)
        mlt = A_("sb_mlt", [128, 128], F32)
        m8 = A_("sb_m8", [128, NH, NS], F32)
        one_col = A_("sb_one", [128, 1], F32)
        qsT = A_("sb_qsT", [128, NH, NS], BF16)
        ksT = A_("sb_ksT", [128, NH, NS], BF16)
        vs = A_("sb_vs", [NS, D], BF16)
        osT = A_("sb_osT", [128, NH, NS], BF16)
        with nc.allow_non_contiguous_dma(reason="tiny sb params"):
            S.dma("sp", lambda e: e.dma_start(out=gq[:, :], in_=k.sb_q_norm[j].rearrange("(p o) -> p o", o=1)), writes=("sb_gq",))
            S.dma("sp", lambda e: e.dma_start(out=gk[:, :], in_=k.sb_k_norm[j].rearrange("(p o) -> p o", o=1)), writes=("sb_gk",))
            S.dma("sp", lambda e: e.dma_start(out=bcol[:, :], in_=k.sb_bias[j:j + 1, :].broadcast_to([128, NH])), writes=("sb_bcol",))
            S.dma("sp", lambda e: e.dma_start(out=trif[:, :], in_=k.c_tri), writes=("sb_trif",))
            S.dma("sp", lambda e: e.dma_start(out=mlt[:, :], in_=k.c_mlt), writes=("sb_mlt",))
            S.flush(k.block)
        S.op("dve", lambda e: e.tensor_copy(out=tri[:, :], in_=trif[:, :]), reads=("sb_trif",), writes=("sb_tri",))
        S.op("dve", lambda e: e.memset(one_col[:, :], 1.0), writes=("sb_one",))
        S.op("dve", lambda e: e.tensor_copy(out=brow[:, :, :], in_=bcol[:, :].unsqueeze(2).broadcast_to([128, NH, NS])),
             reads=("sb_bcol",), writes=("sb_brow",))
        S.op("dve", lambda e: e.tensor_copy(out=m8[:, :, :], in_=mlt[:, 0:NS].unsqueeze(1).broadcast_to([128, NH, NS])),
             reads=("sb_mlt",), writes=("sb_m8",))
        SBK = (4, 5)
        with ExitStack() as aes:
            B_ = lambda name, shape, dt: aes.enter_context(nc.sbuf_tensor(f"{name}_L{L}", shape, dt))
            xt = B_("sa_x", [128, NCH, 520], F32)
            hb = B_("sa_hb", [128, NCH, 520], BF16)
            sq = [B_(f"sa_sq{i}", [128, 520], F32) for i in range(2)]
            rstd = B_("sa_rstd", [128, 520], F32)
            wqk = [B_(f"sa_wqk{i}", [128, NCH, 256], BF16) for i in range(2)]
            wv = [B_(f"sa_wv{i}", [128, NCH, 512], BF16) for i in range(2)]
            hsq = [B_(f"sa_hsq{i}", [128, 520], F32) for i in range(2)]
            hrs = [B_(f"sa_hrs{i}", [128, 520], F32) for i in range(2)]
            knf = [B_(f"sa_knf{i}", [128, 520], F32) for i in range(3)]
            qkb = [B_(f"sa_qkb{i}", [128, 520], BF16) for i in range(2)]
            ktok = B_("sa_ktok", [128, 4, D], F32)
            kstok = B_("sa_kstok", [NS, D], F32)
            vtok = [B_(f"sa_vtok{i}", [128, D], F32) for i in range(2)]
            gcol = k.g_mix[:, L * NCH:(L + 1) * NCH]
            nw = 0
            nv = 0
            nh_ = 0
            for ti in range(cfg.ntile):
                c0, ncols = tile_cols(cfg, ti)
                segs = mm_cols(ncols)
                last = ncols > 512
                S.dma("sp", lambda e, c0=c0, n=ncols: e.dma_start(out=xt[:, :, 0:n], in_=k.XT[:, :, c0:c0 + n]),
                      reads=(("XT", ti),), writes=("sa_x",))
                rmsnorm_tile(k, xt, "sa_x", ncols, gcol, hb, "sa_hb", sq, 6, (7, 0), rstd)
                def a1(fh, par, slot, half, f2):
                    if half == 0:
                        S.dma("pool", lambda e: e.dma_start(out=wqk[slot][:, :, :], in_=Wqkv[:, :, f2 * 256:(f2 + 1) * 256]),
                              writes=(("sa_wqk", slot),))
                    bank = par
                    for kc in range(NCH):
                        for (a, b, smp) in segs:
                            o = k.ps[SBK[par]][:, 0:NS] if smp else k.ps[bank][:, 0:512]
                            tok = ("ps", SBK[par]) if smp else ("ps", bank)
                            S.op("pe", lambda e, o=o, kc=kc, a=a, b=b: e.matmul(
                                out=o, lhsT=wqk[slot][:, kc, half * 128:(half + 1) * 128], rhs=hb[:, kc, a:b],
                                start=(kc == 0), stop=(kc == NCH - 1)),
                                reads=(("sa_wqk", slot), "sa_hb"), writes=(tok,))

                def a2(fh, par, ks):
                    isk = fh >= NH
                    h = fh % NH
                    bank = par
                    for (a, b, smp) in segs:
                        pin = k.ps[SBK[par]][:, 0:NS] if smp else k.ps[bank][:, 0:512]
                        ptok = ("ps", SBK[par]) if smp else ("ps", bank)
                        sso = k.ps[SBK[par]][:, 8:8 + NS] if smp else k.ps[2 + par][:, 0:512]
                        sstok = ("ps", SBK[par]) if smp else ("ps", 2 + par)
                        S.op("act", lambda e, pin=pin, a=a, b=b: e.activation(out=hsq[par][:, a:b], in_=pin, func=AF.Square),
                             reads=(ptok,), writes=(("sa_hsq", par, smp),))
                        S.op("pe", lambda e, sso=sso, a=a, b=b: e.matmul(
                            out=sso, lhsT=k.ones_f[:, :], rhs=hsq[par][:, a:b], start=True, stop=True),
                            reads=(("sa_hsq", par, smp), "ones_f"), writes=(sstok,))
                        S.op("act", lambda e, sso=sso, a=a, b=b: e.activation(
                            out=hrs[par][:, a:b], in_=sso, func=AF.Sqrt, scale=1.0 / 128.0, bias=k.eps_col[:, 0:1]),
                            reads=(sstok,), writes=(("sa_hrs", par, smp),))
                        S.op("dve", lambda e, a=a, b=b: e.reciprocal(out=hrs[par][:, a:b], in_=hrs[par][:, a:b]),
                             reads=(("sa_hrs", par, smp),), writes=(("sa_hrs", par, smp),))
                        if isk:
                            S.op("dve", lambda e, pin=pin, a=a, b=b: e.scalar_tensor_tensor(
                                out=knf[ks][:, a:b], in0=pin, scalar=gk[:, 0:1], in1=hrs[par][:, a:b], op0=ALU.mult, op1=ALU.mult),
                                reads=(ptok, ("sa_hrs", par, smp), "sb_gk"), writes=(("sa_knf", ks, smp),))
                            S.op("act", lambda e, a=a, b=b: e.copy(out=qkb[par][:, a:b], in_=knf[ks][:, a:b]),
                                 reads=(("sa_knf", ks, smp),), writes=(("sa_qkb", par, smp),))
                        else:
                            S.op("dve", lambda e, pin=pin, a=a, b=b: e.scalar_tensor_tensor(
                                out=qkb[par][:, a:b], in0=pin, scalar=gq[:, 0:1], in1=hrs[par][:, a:b], op0=ALU.mult, op1=ALU.mult),
                                reads=(ptok, ("sa_hrs", par, smp), "sb_gq"), writes=(("sa_qkb", par, smp),))
                        if smp:
                            dst = ksT if isk else qsT
                            S.op("act", lambda e, dst=dst: e.copy(out=dst[:, h, :], in_=qkb[par][:, 512:520]),
                                 reads=(("sa_qkb", par, smp),), writes=("sb_ksT" if isk else "sb_qsT",))
                    dsc = k.KTs if isk else k.QTs
                    S.dma("sp", lambda e: e.dma_start(out=dsc[h, :, c0:c0 + 512], in_=qkb[par][:, 0:512]),
                          reads=(("sa_qkb", par, False),), writes=(("qk_scr", isk, h, ti),))

                def a3(fh, par, ks):
                    isk = fh >= NH
                    h = fh % NH
                    if not isk:
                        return
                    for tb in range(4):
                        S.op("pe", lambda e, tb=tb: e.transpose(
                            out=k.ps[6][:, tb * 128:(tb + 1) * 128], in_=knf[ks][:, tb * 128:(tb + 1) * 128], identity=k.ident[:, :]),
                            reads=(("sa_knf", ks, False),), writes=(("ps", 6),))
                    S.op("act", lambda e: e.copy(
                        out=ktok[:, :, h * 128:(h + 1) * 128], in_=k.ps[6][:, :].rearrange("p (b d) -> p b d", b=4)),
                        reads=(("ps", 6),), writes=("sa_ktok",))
                    if last:
                        S.op("pe", lambda e: e.transpose(
                            out=k.ps[7][0:NS, 0:128], in_=knf[ks][:, 512:520], identity=k.ident[:, :]),
                            reads=(("sa_knf", ks, True),), writes=(("ps", 7),))
                        S.op("act", lambda e: e.copy(out=kstok[:, h * 128:(h + 1) * 128], in_=k.ps[7][0:NS, 0:128]),
                             reads=(("ps", 7),), writes=("sa_kstok",))

                atasks = []
                for f2 in range(NH):
                    slot = nw % 2
                    nw += 1
                    for half in range(2):
                        atasks.append(dict(fh=f2 * 2 + half, par=nh_ % 2, ks=nh_ % 3, slot=slot, half=half, f2=f2))
                        nh_ += 1
                ASKEW = getattr(cfg, "sa_skew", 0)
                for n in range(len(atasks) + 2 * ASKEW):
                    if n < len(atasks):
                        t_ = atasks[n]
                        a1(t_["fh"], t_["par"], t_["slot"], t_["half"], t_["f2"])
                    if 0 <= n - ASKEW < len(atasks):
                        t_ = atasks[n - ASKEW]
                        a2(t_["fh"], t_["par"], t_["ks"])
                    if 0 <= n - 2 * ASKEW < len(atasks):
                        t_ = atasks[n - 2 * ASKEW]
                        a3(t_["fh"], t_["par"], t_["ks"])
                S.dma("sp", lambda e, c0=c0: e.dma_start(
                    out=k.new_k_prompt[j, c0:c0 + 512, :].rearrange("(b p) f -> p b f", p=128), in_=ktok[:, :, :]),
                    reads=("sa_ktok",), writes=(("kout", ti),))
                if last:
                    S.dma("sp", lambda e: e.dma_start(out=k.new_k_sample[j], in_=kstok[:, :]), reads=("sa_kstok",), writes=("ksout",))
                blocks = [(128 * b4, 128) for b4 in range(4)] + ([(512, NS)] if last else [])
                for vs4 in range(4):
                    slot = nv % 2
                    nv += 1
                    S.dma("pool", lambda e, slot=slot, vs4=vs4: e.dma_start(
                        out=wv[slot][:, :, :], in_=Wqkv[:, :, 4096 + vs4 * 512:4096 + (vs4 + 1) * 512]),
                        writes=(("sa_wv", slot),))
                    for bi, (b0, nr) in enumerate(blocks):
                        bank = bi if bi < 4 else 7
                        for kc in range(NCH):
                            S.op("pe", lambda e, bank=bank, nr=nr, b0=b0, kc=kc, slot=slot: e.matmul(
                                out=k.ps[bank][0:nr, 0:512], lhsT=hb[:, kc, b0:b0 + nr], rhs=wv[slot][:, kc, :],
                                start=(kc == 0), stop=(kc == NCH - 1)),
                                reads=(("sa_wv", slot), "sa_hb"), writes=(("ps", bank),))
                        vslot = bi % 2
                        eng = "act" if bi % 2 == 0 else "dve"
                        if eng == "act":
                            S.op("act", lambda e, bank=bank, nr=nr, vslot=vslot, vs4=vs4: e.copy(
                                out=vtok[vslot][0:nr, vs4 * 512:(vs4 + 1) * 512], in_=k.ps[bank][0:nr, 0:512]),
                                reads=(("ps", bank),), writes=(("sa_vtok", vslot, vs4),))
                        else:
                            S.op("dve", lambda e, bank=bank, nr=nr, vslot=vslot, vs4=vs4: e.tensor_copy(
                                out=vtok[vslot][0:nr, vs4 * 512:(vs4 + 1) * 512], in_=k.ps[bank][0:nr, 0:512]),
                                reads=(("ps", bank),), writes=(("sa_vtok", vslot, vs4),))
                        if bi == 4:
                            S.op("act", lambda e, vslot=vslot, vs4=vs4: e.copy(
                                out=vs[0:NS, vs4 * 512:(vs4 + 1) * 512], in_=vtok[vslot][0:NS, vs4 * 512:(vs4 + 1) * 512]),
                                reads=(("sa_vtok", vslot, vs4),), writes=("sb_vs",))
                        dst = k.new_v_sample[j][:, vs4 * 512:(vs4 + 1) * 512] if bi == 4 else \
                            k.new_v_prompt[j, c0 + b0:c0 + b0 + 128, vs4 * 512:(vs4 + 1) * 512]
                        S.dma("sp", lambda e, dst=dst, nr=nr, vslot=vslot, vs4=vs4: e.dma_start(
                            out=dst, in_=vtok[vslot][0:nr, vs4 * 512:(vs4 + 1) * 512]),
                            reads=(("sa_vtok", vslot, vs4),), writes=(("vout", ti, bi, vs4),))
                S.flush(k.block)
        S.barrier()
        with ExitStack() as bes:
            B_ = lambda name, shape, dt: bes.enter_context(nc.sbuf_tensor(f"{name}_L{L}", shape, dt))
            NSL = 3
            qh = [B_(f"sbq{i}", [128, T], BF16) for i in range(2)]
            kh = [B_(f"sbk{i}", [128, T], BF16) for i in range(2)]
            vh = [B_(f"sbv{i}", [128, T // 128, 128], BF16) for i in range(2)]
            E = [B_(f"sbE{i}", [128, 512], BF16) for i in range(NSL)]
            X = [B_(f"sbX{i}", [128, 512], BF16) for i in range(NSL)]
            W = [B_(f"sbW{i}", [128, 512], BF16) for i in range(NSL)]
            SPR = [B_(f"sbSP{i}", [128, 512], BF16) for i in range(32)]
            ones_b = B_("sb_onesb", [128, 128], BF16)
            mltb = B_("sb_mltb", [128, 128], BF16)
            oT = [B_(f"sboT{i}", [128, 512], BF16) for i in range(2)]
            S.op("dve", lambda e: e.memset(ones_b[:, :], 1.0), writes=("sb_onesb",))
            S.op("dve", lambda e: e.tensor_copy(out=mltb[:, :], in_=mlt[:, :]), reads=("sb_mlt",), writes=("sb_mltb",))
            def load_head(h):
                hs_ = h % 2
                S.dma("sp", lambda e: e.dma_start(out=qh[hs_][:, :], in_=k.QTs[h, :, 0:T]),
                      reads=tuple(("qk_scr", False, h, ti) for ti in range(cfg.ntile)), writes=(("sbq", hs_),))
                S.dma("sp", lambda e: e.dma_start(out=kh[hs_][:, :], in_=k.KTs[h, :, 0:T]),
                      reads=tuple(("qk_scr", True, h, ti) for ti in range(cfg.ntile)), writes=(("sbk", hs_),))
                S.dma("pool", lambda e: e.dma_start(
                    out=vh[hs_][:, :, :], in_=k.new_v_prompt[j, :, h * 128:(h + 1) * 128].rearrange("(b p) d -> p b d", p=128)),
                    writes=(("sbv", hs_),))

            tasks = []
            npair = 0
            nq = 0
            for h in range(NH if "B" not in getattr(cfg, "sb_skip", "") else 0):
                for qt in range(cfg.ntile):
                    obank = 6 + nq % 2
                    osl = nq % 2
                    nq += 1
                    kb_hi = qt * 4 + 3
                    prev = []
                    for i, kb in enumerate(range(kb_hi, -1, -1)):
                        Q0 = qt * 512
                        lo = max(kb * 128, Q0) - Q0
                        t_ = dict(h=h, hs_=h % 2, qt=qt, Q0=Q0, obank=obank, osl=osl, kb=kb, kb_hi=kb_hi, i=i, lo=lo,
                                  diag=(kb * 128 >= Q0), sl=npair % NSL, prev=list(prev),
                                  first_of_head=(qt == 0 and i == 0), prefetch=(qt == 0 and i == 3), spi=(nq % 2) * 16 + i)
                        prev.append((t_["spi"], lo))
                        npair += 1
                        tasks.append(t_)

            def st1(t_):
                h, hs_, kb, lo, Q0, sl = t_["h"], t_["hs_"], t_["kb"], t_["lo"], t_["Q0"], t_["sl"]
                if t_["first_of_head"] and h == 0:
                    load_head(0)
                if t_["prefetch"] and h + 1 < NH:
                    load_head(h + 1)
                zb = sl
                sp = SPR[t_["spi"]]
                sptok = ("sbSP", t_["spi"])
                S.op("pe", lambda e: e.matmul(
                    out=k.ps[zb][:, lo:512], lhsT=kh[hs_][:, kb * 128:(kb + 1) * 128], rhs=qh[hs_][:, Q0 + lo:Q0 + 512],
                    start=True, stop=True), reads=(("sbk", hs_), ("sbq", hs_)), writes=(("ps", zb),))
                S.op("act", lambda e: e.activation(
                    out=E[sl][:, lo:512], in_=k.ps[zb][:, lo:512], func=AF.Exp, scale=SCALE, bias=bcol[:, h:h + 1]),
                    reads=(("ps", zb), "sb_bcol"), writes=(("sbE", sl),))
                S.op("act", lambda e: e.activation(
                    out=sp[:, lo:512], in_=E[sl][:, lo:512], func=AF.Ln, scale=1.0, bias=one_col[:, 0:1]),
                    reads=(("sbE", sl), "sb_one"), writes=(sptok,))
                if t_["diag"]:
                    S.op("pool", lambda e: e.tensor_tensor(
                        out=sp[:, lo:lo + 128], in0=sp[:, lo:lo + 128], in1=mltb[:, :], op=ALU.mult),
                        reads=(sptok, "sb_mltb"), writes=(sptok,))

            def st2(t_):
                lo, sl, prev = t_["lo"], t_["sl"], t_["prev"]
                sbk_ = 3 + sl
                sp = SPR[t_["spi"]]
                sptok = ("sbSP", t_["spi"])
                S.op("pe", lambda e: e.matmul(
                    out=k.ps[sbk_][:, lo:512], lhsT=tri[:, :], rhs=sp[:, lo:512], start=True, stop=(len(prev) == 0),
                    skip_group_check=True), reads=("sb_tri", sptok), writes=(("ps", sbk_),))
                for pi_, (pidx, plo) in enumerate(prev):
                    S.op("pe", lambda e, pidx=pidx, plo=plo, lastp=(pi_ == len(prev) - 1): e.matmul(
                        out=k.ps[sbk_][:, plo:512], lhsT=ones_b[:, :], rhs=SPR[pidx][:, plo:512], start=False, stop=lastp,
                        skip_group_check=True), reads=("sb_onesb", ("sbSP", pidx)), writes=(("ps", sbk_),))
                S.op("act", lambda e: e.activation(
                    out=X[sl][:, lo:512], in_=k.ps[sbk_][:, lo:512], func=AF.Exp, scale=-1.0),
                    reads=(("ps", sbk_),), writes=(("sbX", sl),))
                if t_["diag"]:
                    S.op("pool", lambda e: e.tensor_tensor(
                        out=X[sl][:, lo:lo + 128], in0=X[sl][:, lo:lo + 128], in1=mltb[:, :], op=ALU.mult),
                        reads=(("sbX", sl), "sb_mltb"), writes=(("sbX", sl),))
                S.op("dve", lambda e: e.tensor_tensor(
                    out=W[sl][:, lo:512], in0=E[sl][:, lo:512], in1=X[sl][:, lo:512], op=ALU.mult),
                    reads=(("sbE", sl), ("sbX", sl)), writes=(("sbW", sl),))

            def st3(t_):
                h, hs_, kb, lo, Q0, sl, obank, osl = (t_[x] for x in ("h", "hs_", "kb", "lo", "Q0", "sl", "obank", "osl"))
                S.op("pe", lambda e: e.matmul(
                    out=k.ps[obank][:, lo:512], lhsT=vh[hs_][:, kb, :], rhs=W[sl][:, lo:512],
                    start=(kb == t_["kb_hi"]), stop=(kb == 0), skip_group_check=True),
                    reads=(("sbv", hs_), ("sbW", sl)), writes=(("ps", obank),))
                if kb == 0:
                    S.op("act", lambda e: e.copy(out=oT[osl][:, :], in_=k.ps[obank][:, :]),
                         reads=(("ps", obank),), writes=(("sboT", osl),))
                    S.dma("sp", lambda e: e.dma_start(out=k.OTs[:, h, Q0:Q0 + 512], in_=oT[osl][:, :]),
                          reads=(("sboT", osl),), writes=(("o_scr", h, t_["qt"]),))

            nt_ = len(tasks)
            for n in range(nt_ + 2):
                if n < nt_:
                    st1(tasks[n])
                if 0 <= n - 1 < nt_:
                    st2(tasks[n - 1])
                if 0 <= n - 2 < nt_:
                    st3(tasks[n - 2])
                if n % 16 == 15:
                    S.flush(k.block)
            S.flush(k.block)
        S.barrier()
        with ExitStack() as ces:
            B_ = lambda name, shape, dt: ces.enter_context(nc.sbuf_tensor(f"{name}_L{L}", shape, dt))
            NP = cfg.npages
            ptb = B_("sc_ptb", [128, NP], I32)
            idx = B_("sc_idx", [128, NP], I32)
            iot = B_("sc_iot", [128, 1], I32)
            kp = [B_(f"sc_kp{i}", [128, D], F32) for i in range(2)]
            vp = [B_(f"sc_vp{i}", [128, D], BF16) for i in range(4)]
            ktp = [B_(f"sc_ktp{i}", [128, NH, 128], BF16) for i in range(2)]
            ZB = [B_(f"sc_zb{i}", [128, 128], F32) for i in range(4)]
            Es = [B_(f"sc_E{i}", [128, 128], F32) for i in range(4)]
            SPs = [B_(f"sc_SP{i}", [128, 128], BF16) for i in range(4)]
            Xs = [B_(f"sc_X{i}", [128, 128], F32) for i in range(4)]
            Ws = [B_(f"sc_W{i}", [128, 128], BF16) for i in range(4)]
            ACCs = B_("sc_ACC", [128, 128], F32)
            S.dma("sp", lambda e: e.dma_start(out=ptb[:, :], in_=k.page_table[0:1, :].broadcast_to([128, NP])), writes=("sc_ptb",))
            S.op("pool", lambda e: e.iota(iot[:, :], pattern=[[0, 1]], base=0, channel_multiplier=1), writes=("sc_iot",))
            S.op("pool", lambda e: e.tensor_scalar(out=idx[:, :], in0=ptb[:, :], scalar1=128, scalar2=None, op0=ALU.mult),
                 reads=("sc_ptb",), writes=("sc_idx",))
            S.op("pool", lambda e: e.tensor_tensor(out=idx[:, :], in0=idx[:, :], in1=iot[:, 0:1].broadcast_to([128, NP]), op=ALU.add),
                 reads=("sc_idx", "sc_iot"), writes=("sc_idx",))
            S.op("pool", lambda e: e.memset(ACCs[:, :], 0.0), writes=("sc_ACC",))
            OB = 7
            NC4 = 4
            first_pv = [True]
            ck = k.cache_k[j]
            cv = k.cache_v[j]

            def blk(bi):
                if bi == 0:
                    return dict(nk=NS, new=True, kt=lambda h: ksT[:, h, :], v=lambda h: vs[0:NS, h * 128:(h + 1) * 128],
                                ktoks=("sb_ksT",), vtoks=("sb_vs",), pg=None)
                pg = NP - bi
                ksl, vsl = bi % 2, bi % NC4
                return dict(nk=128, new=False, kt=lambda h: ktp[ksl][:, h, :], v=lambda h: vp[vsl][:, h * 128:(h + 1) * 128],
                            ktoks=(("sc_ktp", ksl),), vtoks=(("sc_vp", vsl),), pg=pg, ksl=ksl, vsl=vsl)

            def c1(bi):
                d_ = blk(bi)
                if d_["new"]:
                    return
                pg, ksl, vsl = d_["pg"], d_["ksl"], d_["vsl"]
                S.dma("pool", lambda e: e.indirect_dma_start(
                    out=kp[ksl][:, :], out_offset=None, in_=ck, in_offset=bass.IndirectOffsetOnAxis(ap=idx[:, pg:pg + 1], axis=0)),
                    reads=("sc_idx",), writes=(("sc_kp", ksl),))
                S.dma("pool", lambda e: e.indirect_dma_start(
                    out=vp[vsl][:, :], out_offset=None, in_=cv, in_offset=bass.IndirectOffsetOnAxis(ap=idx[:, pg:pg + 1], axis=0)),
                    reads=("sc_idx",), writes=(("sc_vp", vsl),))
                for g in range(4):
                    bank = g % 2
                    for cc in range(4):
                        h = g * 4 + cc
                        S.op("pe", lambda e, bank=bank, cc=cc, h=h: e.transpose(
                            out=k.ps[bank][:, cc * 128:(cc + 1) * 128], in_=kp[ksl][:, h * 128:(h + 1) * 128], identity=k.ident[:, :]),
                            reads=(("sc_kp", ksl),), writes=(("ps", bank),))
                    if g % 2 == 0:
                        S.op("act", lambda e, bank=bank, g=g: e.copy(
                            out=ktp[ksl][:, g * 4:(g + 1) * 4, :], in_=k.ps[bank][:, :].rearrange("p (a b) -> p a b", a=4)),
                            reads=(("ps", bank),), writes=(("sc_ktp", ksl),))
                    else:
                        S.op("dve", lambda e, bank=bank, g=g: e.tensor_copy(
                            out=ktp[ksl][:, g * 4:(g + 1) * 4, :], in_=k.ps[bank][:, :].rearrange("p (a b) -> p a b", a=4)),
                            reads=(("ps", bank),), writes=(("sc_ktp", ksl),))

            def c2(bi):
                d_ = blk(bi)
                nk, sl = d_["nk"], bi % NC4
                zbk = 2 + bi % 2
                for h in range(NH):
                    S.op("pe", lambda e, h=h: e.matmul(
                        out=k.ps[zbk][0:nk, h * NS:(h + 1) * NS], lhsT=d_["kt"](h), rhs=qsT[:, h, :], start=True, stop=True),
                        reads=d_["ktoks"] + ("sb_qsT",), writes=(("ps", zbk),))
                S.op("dve", lambda e: e.scalar_tensor_tensor(
                    out=ZB[sl][0:nk, :], in0=k.ps[zbk][0:nk, 0:128], scalar=SCALE, in1=brow[0:nk, :, :].rearrange("p h t -> p (h t)"),
                    op0=ALU.mult, op1=ALU.add), reads=(("ps", zbk), "sb_brow"), writes=(("sc_zb", sl),))
                S.op("act", lambda e: e.activation(out=Es[sl][0:nk, :], in_=ZB[sl][0:nk, :], func=AF.Exp),
                     reads=(("sc_zb", sl),), writes=(("sc_E", sl),))
                S.op("act", lambda e: e.activation(out=SPs[sl][0:nk, :], in_=Es[sl][0:nk, :], func=AF.Ln, scale=1.0, bias=one_col[0:nk, 0:1]),
                     reads=(("sc_E", sl), "sb_one"), writes=(("sc_SP", sl),))
                if d_["new"]:
                    S.op("dve", lambda e: e.tensor_tensor(
                        out=SPs[sl][0:nk, :], in0=SPs[sl][0:nk, :], in1=m8[0:nk, :, :].rearrange("p h t -> p (h t)"), op=ALU.mult),
                        reads=(("sc_SP", sl), "sb_m8"), writes=(("sc_SP", sl),))

            def c3(bi):
                d_ = blk(bi)
                nk, sl = d_["nk"], bi % NC4
                sbk_ = 4 + bi % 2
                S.op("pe", lambda e: e.matmul(
                    out=k.ps[sbk_][0:nk, 0:128], lhsT=tri[0:nk, 0:nk], rhs=SPs[sl][0:nk, :], start=True, stop=False),
                    reads=("sb_tri", ("sc_SP", sl)), writes=(("ps", sbk_),))
                S.op("pe", lambda e: e.matmul(
                    out=k.ps[sbk_][0:nk, 0:128], lhsT=k.ones_f[:, 0:nk], rhs=ACCs[:, :], start=False, stop=True),
                    reads=("ones_f", "sc_ACC"), writes=(("ps", sbk_),))
                S.op("pool", lambda e: e.tensor_tensor(out=ACCs[0:nk, :], in0=ACCs[0:nk, :], in1=SPs[sl][0:nk, :], op=ALU.add),
                     reads=("sc_ACC", ("sc_SP", sl)), writes=("sc_ACC",))
                S.op("act", lambda e: e.activation(out=Xs[sl][0:nk, :], in_=k.ps[sbk_][0:nk, 0:128], func=AF.Exp, scale=-1.0),
                     reads=(("ps", sbk_),), writes=(("sc_X", sl),))
                if d_["new"]:
                    S.op("dve", lambda e: e.tensor_tensor(
                        out=Xs[sl][0:nk, :], in0=Xs[sl][0:nk, :], in1=m8[0:nk, :, :].rearrange("p h t -> p (h t)"), op=ALU.mult),
                        reads=(("sc_X", sl), "sb_m8"), writes=(("sc_X", sl),))
                S.op("dve", lambda e: e.tensor_tensor(out=Ws[sl][0:nk, :], in0=Es[sl][0:nk, :], in1=Xs[sl][0:nk, :], op=ALU.mult),
                     reads=(("sc_E", sl), ("sc_X", sl)), writes=(("sc_W", sl),))

            def c4(bi, final):
                d_ = blk(bi)
                nk, sl = d_["nk"], bi % NC4
                for h in range(NH):
                    st = first_pv[0] and h == 0
                    S.op("pe", lambda e, h=h, st=st: e.matmul(
                        out=k.ps[OB][:, h * NS:(h + 1) * NS], lhsT=d_["v"](h), rhs=Ws[sl][0:nk, h * NS:(h + 1) * NS],
                        start=st, stop=(final and h == NH - 1), skip_group_check=True),
                        reads=d_["vtoks"] + (("sc_W", sl),), writes=(("ps", OB),))
                first_pv[0] = False

            NB = NP + 1
            for n in range(NB + 3):
                if n < NB:
                    c1(n)
                if 0 <= n - 1 < NB:
                    c2(n - 1)
                if 0 <= n - 2 < NB:
                    c3(n - 2)
                if 0 <= n - 3 < NB:
                    c4(n - 3, n - 3 == NB - 1)
                if n % 8 == 7:
                    S.flush(k.block)
            S.op("act", lambda e: e.copy(out=osT[:, :, :], in_=k.ps[OB][:, 0:128].rearrange("p (h t) -> p h t", h=NH)),
                 reads=(("ps", OB),), writes=("sb_osT",))
            S.flush(k.block)
        S.barrier()
        with ExitStack() as des:
            B_ = lambda name, shape, dt: des.enter_context(nc.sbuf_tensor(f"{name}_L{L}", shape, dt))
            ot = B_("sd_o", [128, NH, 520], BF16)
            wo = [B_(f"sd_wo{i}", [128, NCH, 256], BF16) for i in range(2)]
            xr = [B_(f"sd_xr{i}", [128, 520], F32) for i in range(2)]
            nw = 0
            for ti in range(cfg.ntile if "D" not in getattr(cfg, "sb_skip", "") else 0):
                c0, ncols = tile_cols(cfg, ti)
                segs = mm_cols(ncols)
                last = ncols > 512
                S.dma("sp", lambda e, c0=c0: e.dma_start(out=ot[:, :, 0:512], in_=k.OTs[:, :, c0:c0 + 512]),
                      reads=tuple(("o_scr", h, ti) for h in range(NH)), writes=("sd_o",))
                if last:
                    S.op("dve", lambda e: e.tensor_copy(out=ot[:, :, 512:520], in_=osT[:, :, :]), reads=("sb_osT",), writes=("sd_o",))
                for oc2 in range(NCH // 2):
                    slot = nw % 2
                    nw += 1
                    S.dma("pool", lambda e, slot=slot, oc2=oc2: e.dma_start(out=wo[slot][:, :, :], in_=Wo[:, :, oc2 * 256:(oc2 + 1) * 256]),
                          writes=(("sd_wo", slot),))
                    for half in range(2):
                        oc = oc2 * 2 + half
                        par = oc % 2
                        S.dma("sp", lambda e, par=par, oc=oc, c0=c0, n=ncols: e.dma_start(out=xr[par][:, 0:n], in_=k.XT[:, oc, c0:c0 + n]),
                              reads=(), writes=(("sd_xr", par),))
                        for kc in range(NCH):
                            for (a, b, smp) in segs:
                                o = k.ps[SBK[par]][:, 0:NS] if smp else k.ps[par][:, 0:512]
                                tok = ("ps", SBK[par]) if smp else ("ps", par)
                                S.op("pe", lambda e, o=o, slot=slot, kc=kc, half=half, a=a, b=b: e.matmul(
                                    out=o, lhsT=wo[slot][:, kc, half * 128:(half + 1) * 128], rhs=ot[:, kc, a:b],
                                    start=(kc == 0), stop=(kc == NCH - 1)),
                                    reads=(("sd_wo", slot), "sd_o"), writes=(tok,))
                        for (a, b, smp) in segs:
                            o = k.ps[SBK[par]][:, 0:NS] if smp else k.ps[par][:, 0:512]
                            tok = ("ps", SBK[par]) if smp else ("ps", par)
                            S.op("dve", lambda e, o=o, par=par, a=a, b=b: e.tensor_tensor(
                                out=xr[par][:, a:b], in0=xr[par][:, a:b], in1=o, op=ALU.add),
                                reads=(tok, ("sd_xr", par)), writes=(("sd_xr", par),))
                        S.dma("sp", lambda e, par=par, oc=oc, c0=c0, n=ncols: e.dma_start(out=k.XT[:, oc, c0:c0 + n], in_=xr[par][:, 0:n]),
                              reads=(("sd_xr", par),), writes=(("XT", ti),))
                S.flush(k.block)
    S.barrier()


def phase_transpose_out(k):
    nc, S, cfg = k.nc, k.S, k.cfg
    with ExitStack() as es:
        xi = [es.enter_context(nc.sbuf_tensor(f"to_in{i}", [128, NCH, 520], F32)) for i in range(2)]
        xo = [es.enter_context(nc.sbuf_tensor(f"to_out{i}", [128, D], F32)) for i in range(2)]
        nb = 0
        for ti in range(cfg.ntile):
            c0, ncols = tile_cols(cfg, ti)
            xin = xi[ti % 2]
            itok = ("to_in", ti % 2)
            S.dma("sp", lambda e, xin=xin, c0=c0, n=ncols: e.dma_start(out=xin[:, :, 0:n], in_=k.XT[:, :, c0:c0 + n]),
                  reads=(("XT", ti),), writes=(itok,))
            blocks = [(c0 + 128 * j, 128, 128 * j, False) for j in range(4)]
            if ncols > 512:
                blocks.append((0, NS, 512, True))
            for (r0, nr, ic0, smp) in blocks:
                slot = nb % 2
                nb += 1
                otok = ("to_out", slot)
                for g in range(4):
                    bank = g
                    for cc in range(4):
                        c = g * 4 + cc
                        S.op("pe", lambda e, b=bank, cc=cc, c=c, nr=nr, ic0=ic0, xin=xin: e.transpose(
                            out=k.ps[b][0:nr, cc * 128:(cc + 1) * 128], in_=xin[:, c, ic0:ic0 + nr],
                            identity=k.ident[:, :]),
                            reads=(itok,), writes=(("ps", bank),))
                    eng = "act" if g % 2 == 0 else "dve"
                    src_ap = k.ps[bank][0:nr, :]
                    dst_ap = xo[slot][0:nr, g * 512:(g + 1) * 512]
                    if eng == "act":
                        S.op("act", lambda e, s=src_ap, d=dst_ap: e.copy(out=d, in_=s),
                             reads=(("ps", bank),), writes=(otok,))
                    else:
                        S.op("dve", lambda e, s=src_ap, d=dst_ap: e.tensor_copy(out=d, in_=s),
                             reads=(("ps", bank),), writes=(otok,))
                dst = k.y_sample[0:NS, :] if smp else k.y_prompt[r0:r0 + nr, :]
                S.dma("sp", lambda e, d=dst, slot=slot, nr=nr: e.dma_start(out=d, in_=xo[slot][0:nr, :]),
                      reads=(otok,), writes=(("yout", nb),))
        S.flush(k.block)
    S.barrier()


def build(cfg):
    nc = bass.Bass("TRN2", target_bir_lowering=False)
    k = K()
    k.nc, k.cfg = nc, cfg
    T = cfg.T
    dram = lambda name, shape, dt, kind: nc.dram_tensor(name, list(shape), dt, kind=kind).ap()
    k.x_prompt = dram("x_prompt", [T, D], F32, "ExternalInput")
    k.x_sample = dram("x_sample", [NS, D], F32, "ExternalInput")
    k.norm_mix = dram("norm_mix", [len(cfg.kinds), D], F32, "ExternalInput")
    k.norm_ffn = dram("norm_ffn", [len(cfg.kinds), D], F32, "ExternalInput")
    k.w_gate = dram("w_ffn_gate", [len(cfg.kinds), D, DFF], F32, "ExternalInput")
    k.w_up = dram("w_ffn_up", [len(cfg.kinds), D, DFF], F32, "ExternalInput")
    k.w_down = dram("w_ffn_down", [len(cfg.kinds), DFF, D], F32, "ExternalInput")
    k.c_ident = dram("c_ident", [128, 128], F32, "ExternalInput")
    k.c_invc = dram("c_invc", [128, NCH, PBUF], F32, "ExternalInput")
    npl = max(cfg.n_pool, 1)
    k.cache_pool = dram("cache_pool", [npl, PBUF, D], F32, "ExternalInput")
    k.w_pool = dram("w_pool", [npl, 4, 512, 512], F32, "ExternalInput")
    k.pool_scale = dram("pool_scale", [npl, D], F32, "ExternalInput")
    nss = max(cfg.n_ssm, 1)
    k.state_re = dram("state_ssm_re", [nss, 128, 64], F32, "ExternalInput")
    k.state_im = dram("state_ssm_im", [nss, 128, 64], F32, "ExternalInput")
    k.ssm_a_re = dram("ssm_a_re", [nss, 128, 64], F32, "ExternalInput")
    k.ssm_a_im = dram("ssm_a_im", [nss, 128, 64], F32, "ExternalInput")
    k.ssm_log_dt = dram("ssm_log_dt", [nss, 128], F32, "ExternalInput")
    k.ssm_d = dram("ssm_d", [nss, D], F32, "ExternalInput")
    k.b_re_w = dram("b_re_w", [nss, 128, NCH, 128], F32, "ExternalInput")
    k.b_im_w = dram("b_im_w", [nss, 128, NCH, 128], F32, "ExternalInput")
    k.c_re_w = dram("c_re_w", [nss, 128, 64, 64], F32, "ExternalInput")
    k.c_im_w = dram("c_im_w", [nss, 128, 64, 64], F32, "ExternalInput")
    k.c_msk = dram("c_msk", [128, 2], F32, "ExternalInput")
    k.c_sel = dram("c_sel", [64, NCH, 128], F32, "ExternalInput")
    k.w_glu_a = dram("w_glu_a", [nss, D, D], F32, "ExternalInput")
    k.w_glu_b = dram("w_glu_b", [nss, D, D], F32, "ExternalInput")
    k.new_ssm_re_prompt = dram("new_ssm_re_prompt", [nss, 128, 64], F32, "ExternalOutput")
    k.new_ssm_im_prompt = dram("new_ssm_im_prompt", [nss, 128, 64], F32, "ExternalOutput")
    k.new_ssm_re_sample = dram("new_ssm_re_sample", [nss, 128, 64], F32, "ExternalOutput")
    k.new_ssm_im_sample = dram("new_ssm_im_sample", [nss, 128, 64], F32, "ExternalOutput")
    nsb = max(cfg.n_sb, 1)
    if cfg.n_sb > 0:
        k.cache_k = dram("cache_k", [nsb, cfg.nphys * 128, D], F32, "ExternalInput")
        k.cache_v = dram("cache_v", [nsb, cfg.nphys * 128, D], F32, "ExternalInput")
        k.page_table = dram("page_table", [1, max(cfg.npages, 1)], I32, "ExternalInput")
        k.w_qkv = dram("w_qkv", [nsb, D, 3 * D], F32, "ExternalInput")
        k.w_o = dram("w_o", [nsb, D, D], F32, "ExternalInput")
        k.sb_q_norm = dram("sb_q_norm", [nsb, 128], F32, "ExternalInput")
        k.sb_k_norm = dram("sb_k_norm", [nsb, 128], F32, "ExternalInput")
        k.sb_bias = dram("sb_bias", [nsb, 16], F32, "ExternalInput")
        k.c_tri = dram("c_tri", [128, 128], F32, "ExternalInput")
        k.c_mlt = dram("c_mlt", [128, 128], F32, "ExternalInput")
        k.new_k_prompt = dram("new_k_prompt", [nsb, T, D], F32, "ExternalOutput")
        k.new_v_prompt = dram("new_v_prompt", [nsb, T, D], F32, "ExternalOutput")
        k.new_k_sample = dram("new_k_sample", [nsb, NS, D], F32, "ExternalOutput")
        k.new_v_sample = dram("new_v_sample", [nsb, NS, D], F32, "ExternalOutput")
        k.QTs = dram("scr_qt", [16, 128, cfg.TT], BF16, "Internal")
        k.KTs = dram("scr_kt", [16, 128, cfg.TT], BF16, "Internal")
        k.OTs = dram("scr_ot", [128, 16, cfg.TT], BF16, "Internal")
    k.new_pool_prompt = dram("new_pool_prompt", [npl, PBUF, D], F32, "ExternalOutput")
    k.new_pool_sample = dram("new_pool_sample", [npl, PBUF, D], F32, "ExternalOutput")
    k.y_prompt = dram("y_prompt", [T, D], F32, "ExternalOutput")
    k.y_sample = dram("y_sample", [NS, D], F32, "ExternalOutput")
    k.XT = dram("scr_xt", [128, NCH, cfg.TT], F32, "Internal")

    with ExitStack() as es:
        S = Sched(nc, es)
        k.S = S
        A = lambda name, shape, dt: es.enter_context(nc.sbuf_tensor(name, shape, dt))
        k.ident = A("ident", [128, 128], F32)
        k.ones_f = A("ones_f", [128, 128], F32)
        k.eps_col = A("eps_col", [128, 1], F32)
        nl = len(cfg.kinds)
        k.g_mix = A("g_mix", [128, nl * NCH], F32)
        k.g_ffn = A("g_ffn", [128, nl * NCH], F32)
        k.invc = A("invc", [128, NCH, PBUF], F32)
        k.pscale = A("pscale", [128, npl * NCH], F32)
        k.ps = [es.enter_context(nc.psum_tensor(f"psb{i}", [128, 512], F32)) for i in range(8)]
        with nc.Block() as block:
            k.block = block
            S.dma("sp", lambda e: e.dma_start(out=k.ident[:, :], in_=k.c_ident), writes=("ident",))
            S.op("dve", lambda e: e.memset(k.ones_f[:, :], 1.0), writes=("ones_f",))
            S.op("dve", lambda e: e.memset(k.eps_col[:, :], EPS), writes=("eps",))
            with nc.allow_non_contiguous_dma(reason="tiny gain vectors"):
                S.dma("sp", lambda e: e.dma_start(
                    out=k.g_mix[:, :].rearrange("p (l c) -> p l c", l=nl),
                    in_=k.norm_mix.rearrange("l (c p) -> p l c", p=128)), writes=("g_mix",))
                S.dma("sp", lambda e: e.dma_start(
                    out=k.g_ffn[:, :].rearrange("p (l c) -> p l c", l=nl),
                    in_=k.norm_ffn.rearrange("l (c p) -> p l c", p=128)), writes=("g_ffn",))
                S.dma("sp", lambda e: e.dma_start(
                    out=k.pscale[:, :].rearrange("p (l c) -> p l c", l=npl),
                    in_=k.pool_scale.rearrange("l (c p) -> p l c", p=128)), writes=("pscale",))
                S.dma("sp", lambda e: e.dma_start(out=k.invc[:, :, :], in_=k.c_invc), writes=("invc",))
                S.flush(block)
            S.barrier()
            phase_transpose_in(k)
            cnt = [0, 0, 0]
            for L, kind in enumerate(cfg.kinds):
                if kind == 0:
                    phase_pool(k, L, cnt[0])
                if kind == 1:
                    phase_s5(k, L, cnt[1])
                if kind == 2:
                    phase_sb(k, L, cnt[2])
                if kind in (0, 1, 2):
                    cnt[kind] += 1
                if not cfg.skip_ffn:
                    phase_ffn(k, L)
            phase_transpose_out(k)
            S.barrier()
            S.flush(block)
    k.ninst = dict(S.ninst)
    return nc, k


def _consts():
    invc = np.zeros((128, NCH, PBUF), np.float32)
    for c in range(NCH):
        w = 2 << (c // 4)
        for t in range(PBUF):
            invc[:, c, t] = 1.0 / min(t + 1, w)
    sel = np.zeros((64, NCH, 128), np.float32)
    for jj in range(NCH):
        for p in range(128):
            sel[4 * jj + p // 32, jj, p] = 1.0
    msk = np.zeros((128, 2), np.float32)
    for p in range(128):
        msk[p, (p // 32) % 2] = 1.0
    s_ = np.arange(128)
    tri = (s_[:, None] >= s_[None, :]).astype(np.float32)
    mlt = (s_[:, None] < s_[None, :]).astype(np.float32)
    return {"c_ident": np.eye(128, dtype=np.float32), "c_invc": invc, "c_sel": sel, "c_msk": msk,
            "c_tri": tri, "c_mlt": mlt}


def _s5_layouts(b_re, b_im, c_re, c_im):
    Ls = b_re.shape[0]
    out = {}
    for name, b in (("b_re_w", b_re), ("b_im_w", b_im)):
        w = np.zeros((Ls, 128, NCH, 2, 64), np.float32)
        for p in range(128):
            q, parp, cc = p // 32, (p // 16) % 2, p % 16
            for jj in range(NCH):
                g = 8 * jj + 2 * q + parp
                w[:, p, jj, parp, :] = b[:, g, :, cc]
        out[name] = w.reshape(Ls, 128, NCH, 128)
    for name, c in (("c_re_w", c_re), ("c_im_w", c_im)):
        w = np.zeros((Ls, 2, 64, 64, 2, 2, 16), np.float32)
        for par in range(2):
            for gp in range(64):
                g = 2 * gp + par
                w[:, par, :, gp, gp % 2, par, :] = np.transpose(c[:, g, :, :], (0, 2, 1))
        out[name] = w.reshape(Ls, 128, 64, 64)
    return out


_BUILD_CACHE = {}


def kernel(x_prompt, x_sample, cache_pool, state_ssm_re, state_ssm_im, cache_k, cache_v, page_table,
           norm_mix, norm_ffn, w_ffn_gate, w_ffn_up, w_ffn_down, w_pool, pool_scale,
           ssm_a_re, ssm_a_im, ssm_b_re, ssm_b_im, ssm_c_re, ssm_c_im, ssm_d, ssm_log_dt,
           w_glu_a, w_glu_b, w_qkv, w_o, sb_q_norm, sb_k_norm, sb_bias):
    f32 = lambda a: np.ascontiguousarray(np.asarray(a), dtype=np.float32)
    x_prompt, x_sample = f32(x_prompt), f32(x_sample)
    B, T, _ = x_prompt.shape
    Bs = x_sample.shape[0]
    nphys = np.asarray(cache_k).shape[1]
    npages = np.asarray(page_table).shape[1]
    depth = np.asarray(norm_mix).shape[0]
    kinds = tuple(i % 3 for i in range(depth))
    cfg = Cfg(T=T, npages=npages, nphys=nphys, kinds=kinds)
    key = (T, npages, nphys, kinds)
    if key not in _BUILD_CACHE:
        _BUILD_CACHE[key] = build(cfg)
    nc, kk = _BUILD_CACHE[key]
    ncores = 8
    shared = {
        "norm_mix": f32(norm_mix), "norm_ffn": f32(norm_ffn),
        "w_ffn_gate": f32(w_ffn_gate), "w_ffn_up": f32(w_ffn_up), "w_ffn_down": f32(w_ffn_down),
        "w_pool": f32(w_pool), "pool_scale": f32(pool_scale),
        "ssm_a_re": f32(ssm_a_re), "ssm_a_im": f32(ssm_a_im), "ssm_d": f32(ssm_d), "ssm_log_dt": f32(ssm_log_dt),
        "w_glu_a": f32(w_glu_a), "w_glu_b": f32(w_glu_b), "w_qkv": f32(w_qkv), "w_o": f32(w_o),
        "sb_q_norm": f32(sb_q_norm), "sb_k_norm": f32(sb_k_norm), "sb_bias": f32(sb_bias),
        "cache_k": f32(cache_k).reshape(cfg.n_sb, nphys * 128, D),
        "cache_v": f32(cache_v).reshape(cfg.n_sb, nphys * 128, D),
    }
    shared.update(_consts())
    shared.update(_s5_layouts(f32(ssm_b_re), f32(ssm_b_im), f32(ssm_c_re), f32(ssm_c_im)))
    cache_pool, state_ssm_re, state_ssm_im = f32(cache_pool), f32(state_ssm_re), f32(state_ssm_im)
    pt = np.ascontiguousarray(np.asarray(page_table), dtype=np.int32)
    in_maps = []
    for c in range(ncores):
        b, s = c % B, c % Bs
        m = dict(shared)
        m["x_prompt"] = x_prompt[b]
        m["x_sample"] = x_sample[s]
        m["cache_pool"] = np.ascontiguousarray(cache_pool[:, s])
        m["state_ssm_re"] = np.ascontiguousarray(state_ssm_re[:, s])
        m["state_ssm_im"] = np.ascontiguousarray(state_ssm_im[:, s])
        m["page_table"] = pt[s:s + 1]
        in_maps.append(m)
    res = run_bass_kernel_spmd(nc, in_maps, core_ids=list(range(ncores)))
    R = res.results
    H, Dh = 16, 128
    pc = list(range(B))
    y_prompt = np.stack([R[c]["y_prompt"] for c in pc], 0)
    y_sample = np.stack([R[c]["y_sample"] for c in range(Bs)], 0)
    npp = np.stack([R[c]["new_pool_prompt"] for c in pc], 1)
    nps = np.stack([R[c]["new_pool_sample"] for c in range(Bs)], 1)
    srp = np.stack([R[c]["new_ssm_re_prompt"] for c in pc], 1)
    sip = np.stack([R[c]["new_ssm_im_prompt"] for c in pc], 1)
    srs = np.stack([R[c]["new_ssm_re_sample"] for c in range(Bs)], 1)
    sis = np.stack([R[c]["new_ssm_im_sample"] for c in range(Bs)], 1)
    kp = np.stack([R[c]["new_k_prompt"] for c in pc], 1).reshape(cfg.n_sb, B, T, H, Dh)
    vp = np.stack([R[c]["new_v_prompt"] for c in pc], 1).reshape(cfg.n_sb, B, T, H, Dh)
    ks = np.stack([R[c]["new_k_sample"] for c in range(Bs)], 1).reshape(cfg.n_sb, Bs, NS, H, Dh)
    vs = np.stack([R[c]["new_v_sample"] for c in range(Bs)], 1).reshape(cfg.n_sb, Bs, NS, H, Dh)
    return (y_prompt, y_sample, npp, nps, srp, sip, srs, sis, kp, vp, ks, vs)
```
